# Optimizing a Trainium2 kernel written in Bass

```python
import jax, jax.numpy as jnp
from jax import lax
import numpy as np

D_MODEL = 1024
BATCH = 32
SEQ = 256
DEPTH = 4
DEC_BATCH = 2
DEC_SEQ = 2048
PAST_LEN = 512

GRID_W = 64
N_MIXERS = 2
N_ATTN_LAYERS = (DEPTH + 1) // 2
N_CHUNK_LAYERS = DEPTH // 2
N_HEADS = 8
Q_RANK = 512
KV_RANK = 256
QK_NOPE = 128
QK_ROPE = 64
QK_DIM = QK_NOPE + QK_ROPE
V_DIM = 128
ATTN_WIDTH = N_HEADS * V_DIM
ATTN_IN = Q_RANK + KV_RANK + QK_ROPE + ATTN_WIDTH
AXIS_ROPE = QK_ROPE // 2
ROPE_THETA = 10000.0
Q_BLOCK = 128
CHUNK = 128
MLP_GROUPS = 8
MLP_WIDTH = 2 * D_MODEL
MLP_IN = 3 * MLP_WIDTH
EPS = 1e-6

kernel_name = 'hybrid_mla_chunkmlp_diffusion_step'


def rmsnorm(x, g):
    xf = x.astype(jnp.float32)
    y = xf * lax.rsqrt(jnp.mean(xf * xf, axis=-1, keepdims=True) + EPS)
    return y.astype(x.dtype) * g


def layernorm(x, g, b):
    xf = x.astype(jnp.float32)
    mu = jnp.mean(xf, axis=-1, keepdims=True)
    var = jnp.mean(jnp.square(xf - mu), axis=-1, keepdims=True)
    y = (xf - mu) * lax.rsqrt(var + EPS)
    return y.astype(x.dtype) * g + b


def modulation(cond, w_mod, b_mod):
    m = (jax.nn.silu(cond) @ w_mod + b_mod).reshape(-1, 1, 3 * D_MODEL)
    return m[..., :D_MODEL], m[..., D_MODEL:2 * D_MODEL], m[..., 2 * D_MODEL:]


def axial_rope_tables(n_tokens, dtype):
    rows = n_tokens // GRID_W
    row = jnp.repeat(jnp.arange(rows, dtype=jnp.float32), GRID_W)
    col = jnp.tile(jnp.arange(GRID_W, dtype=jnp.float32), rows)
    inv = 1.0 / (ROPE_THETA ** (jnp.arange(0, AXIS_ROPE, 2, dtype=jnp.float32) / AXIS_ROPE))
    ang = jnp.concatenate([row[:, None] * inv, col[:, None] * inv], axis=-1)
    return jnp.cos(ang).astype(dtype), jnp.sin(ang).astype(dtype)


def _rotate(xa, ca, sa):
    half = xa.shape[-1] // 2
    x1, x2 = xa[..., :half], xa[..., half:]
    return jnp.concatenate([x1 * ca - x2 * sa, x1 * sa + x2 * ca], axis=-1)


def apply_axial_rope(x, cos, sin):
    h = AXIS_ROPE // 2
    xr, xc = x[..., :AXIS_ROPE], x[..., AXIS_ROPE:]
    return jnp.concatenate([_rotate(xr, cos[..., :h], sin[..., :h]),
                            _rotate(xc, cos[..., h:], sin[..., h:])], axis=-1)


def block_attention(q, k, v):
    b, tq, h, dqk = q.shape
    nb = tq // Q_BLOCK
    scale = dqk ** -0.5
    qb = q.reshape(b, nb, Q_BLOCK, h, dqk).swapaxes(0, 1)

    def one_block(qi):
        s = jnp.einsum('bqhd,bkhd->bhqk', qi, k).astype(jnp.float32) * scale
        p = jax.nn.softmax(s, axis=-1).astype(v.dtype)
        return jnp.einsum('bhqk,bkhd->bqhd', p, v)

    out = lax.map(one_block, qb)
    return out.swapaxes(0, 1).reshape(b, tq, h, v.shape[-1])


def mla_project(h, w_in, q_norm_g, kv_norm_g, w_uq):
    b, t, _ = h.shape
    proj = h @ w_in
    c_q = proj[..., :Q_RANK]
    c_kv = proj[..., Q_RANK:Q_RANK + KV_RANK]
    k_pe = proj[..., Q_RANK + KV_RANK:Q_RANK + KV_RANK + QK_ROPE]
    z = proj[..., Q_RANK + KV_RANK + QK_ROPE:]
    q = (rmsnorm(c_q, q_norm_g) @ w_uq).reshape(b, t, N_HEADS, QK_DIM)
    return q, rmsnorm(c_kv, kv_norm_g), k_pe, z


def mla_keys_values(ckv, k_pe, w_ukv):
    b, t, _ = ckv.shape
    kv = (ckv @ w_ukv).reshape(b, t, N_HEADS, QK_NOPE + V_DIM)
    k_nope, v = kv[..., :QK_NOPE], kv[..., QK_NOPE:]
    k_rope = jnp.broadcast_to(k_pe[:, :, None, :], (b, t, N_HEADS, QK_ROPE))
    return jnp.concatenate([k_nope, k_rope], axis=-1), v


def gated_out(o, z, w_o):
    return (o * jax.nn.silu(z)) @ w_o


def chunk_mlp(h, w_in, v_g, v_b, w_s, b_s, w_o):
    b, t, _ = h.shape
    proj = h @ w_in
    uv = jax.nn.gelu(proj[..., :2 * MLP_WIDTH])
    z = proj[..., 2 * MLP_WIDTH:]
    u, v = uv[..., :MLP_WIDTH], uv[..., MLP_WIDTH:]
    v = layernorm(v, v_g, v_b).reshape(b, t // CHUNK, CHUNK, MLP_GROUPS, MLP_WIDTH // MLP_GROUPS)
    s = jnp.einsum('gpq,bcqgd->bcpgd', w_s, v) + b_s.T[:, :, None]
    return gated_out(u * s.reshape(b, t, MLP_WIDTH), z, w_o)


def setup_inputs(seed: int = 0) -> dict:
    key = jax.random.key(seed)
    ks = jax.random.split(key, 24)
    f32 = jnp.float32
    nrm = lambda k, shape, s: jax.random.normal(k, shape, f32) * s
    gain = lambda k, shape: 1.0 + 0.02 * jax.random.normal(k, shape, f32)
    D = D_MODEL
    return {
        'x_prompt': nrm(ks[0], (BATCH, SEQ, D), 1.0),
        'x_sample': nrm(ks[1], (DEC_BATCH, DEC_SEQ, D), 1.0),
        'cache_ckv': nrm(ks[2], (DEC_BATCH, N_ATTN_LAYERS, PAST_LEN, KV_RANK), 1.0),
        'cache_kpe': nrm(ks[3], (DEC_BATCH, N_ATTN_LAYERS, PAST_LEN, QK_ROPE), 1.0),
        'c': nrm(ks[4], (DEC_BATCH, D), 1.0),
        'c_ctx': nrm(ks[5], (D,), 1.0),
        'norm_g': gain(ks[6], (DEPTH, D)),
        'w_mod': nrm(ks[7], (DEPTH, D, 3 * D), 0.5 * D ** -0.5),
        'b_mod': nrm(ks[8], (DEPTH, 3 * D), 0.02),
        'attn_w_in': nrm(ks[9], (N_ATTN_LAYERS, D, ATTN_IN), D ** -0.5),
        'attn_q_norm_g': gain(ks[10], (N_ATTN_LAYERS, Q_RANK)),
        'attn_kv_norm_g': gain(ks[11], (N_ATTN_LAYERS, KV_RANK)),
        'attn_w_uq': nrm(ks[12], (N_ATTN_LAYERS, Q_RANK, N_HEADS * QK_DIM), Q_RANK ** -0.5),
        'attn_w_ukv': nrm(ks[13], (N_ATTN_LAYERS, KV_RANK, N_HEADS * (QK_NOPE + V_DIM)), KV_RANK ** -0.5),
        'attn_w_o': nrm(ks[14], (N_ATTN_LAYERS, ATTN_WIDTH, D), ATTN_WIDTH ** -0.5),
        'mlp_w_in': nrm(ks[15], (N_CHUNK_LAYERS, D, MLP_IN), D ** -0.5),
        'mlp_v_norm_g': gain(ks[16], (N_CHUNK_LAYERS, MLP_WIDTH)),
        'mlp_v_norm_b': nrm(ks[17], (N_CHUNK_LAYERS, MLP_WIDTH), 0.02),
        'mlp_w_s': nrm(ks[18], (N_CHUNK_LAYERS, MLP_GROUPS, CHUNK, CHUNK), CHUNK ** -0.5),
        'mlp_b_s': 1.0 + nrm(ks[19], (N_CHUNK_LAYERS, MLP_GROUPS, CHUNK), 0.02),
        'mlp_w_o': nrm(ks[20], (N_CHUNK_LAYERS, MLP_WIDTH, D), MLP_WIDTH ** -0.5),
        'final_norm_g': gain(ks[21], (D,)),
    }


def reference(x_prompt, x_sample, cache_ckv, cache_kpe, c, c_ctx, norm_g, w_mod, b_mod,
              attn_w_in, attn_q_norm_g, attn_kv_norm_g, attn_w_uq, attn_w_ukv, attn_w_o,
              mlp_w_in, mlp_v_norm_g, mlp_v_norm_b, mlp_w_s, mlp_b_s, mlp_w_o, final_norm_g):
    t_lat = x_sample.shape[1]
    cos, sin = axial_rope_tables(t_lat, x_sample.dtype)
    xc, xl = x_prompt, x_sample
    ckv_out, kpe_out = [], []
    for layer in range(DEPTH):
        sh_c, sc_c, g_c = modulation(c_ctx, w_mod[layer], b_mod[layer])
        sh_l, sc_l, g_l = modulation(c, w_mod[layer], b_mod[layer])
        hc = rmsnorm(xc, norm_g[layer]) * (1.0 + sc_c) + sh_c
        hl = rmsnorm(xl, norm_g[layer]) * (1.0 + sc_l) + sh_l
        if layer % N_MIXERS == 0:
            a = layer // N_MIXERS
            q_c, ckv_c, kpe_c, z_c = mla_project(hc, attn_w_in[a], attn_q_norm_g[a], attn_kv_norm_g[a], attn_w_uq[a])
            k_c, v_c = mla_keys_values(ckv_c, kpe_c, attn_w_ukv[a])
            o_c = block_attention(q_c, k_c, v_c).reshape(hc.shape[0], hc.shape[1], ATTN_WIDTH)
            mix_c = gated_out(o_c, z_c, attn_w_o[a])
            ckv_out.append(ckv_c)
            kpe_out.append(kpe_c)
            q_l, ckv_l, kpe_l, z_l = mla_project(hl, attn_w_in[a], attn_q_norm_g[a], attn_kv_norm_g[a], attn_w_uq[a])
            q_l = jnp.concatenate([q_l[..., :QK_NOPE],
                                   apply_axial_rope(q_l[..., QK_NOPE:], cos[:, None, :], sin[:, None, :])], axis=-1)
            kpe_l = apply_axial_rope(kpe_l, cos, sin)
            k_l, v_l = mla_keys_values(ckv_l, kpe_l, attn_w_ukv[a])
            k_p, v_p = mla_keys_values(cache_ckv[:, a], cache_kpe[:, a], attn_w_ukv[a])
            o_l = block_attention(q_l, jnp.concatenate([k_p, k_l], axis=1),
                                  jnp.concatenate([v_p, v_l], axis=1)).reshape(hl.shape[0], t_lat, ATTN_WIDTH)
            mix_l = gated_out(o_l, z_l, attn_w_o[a])
        else:
            m = layer // N_MIXERS
            mix_c = chunk_mlp(hc, mlp_w_in[m], mlp_v_norm_g[m], mlp_v_norm_b[m], mlp_w_s[m], mlp_b_s[m], mlp_w_o[m])
            mix_l = chunk_mlp(hl, mlp_w_in[m], mlp_v_norm_g[m], mlp_v_norm_b[m], mlp_w_s[m], mlp_b_s[m], mlp_w_o[m])
        xc = xc + g_c * mix_c
        xl = xl + g_l * mix_l
    y_prompt = rmsnorm(xc, final_norm_g)
    y_sample = rmsnorm(xl, final_norm_g)
    new_ckv = jnp.stack(ckv_out, axis=1)
    new_kpe = jnp.stack(kpe_out, axis=1)
    return (y_prompt, y_sample, new_ckv, new_kpe)
```

```python
import numpy as np
import concourse.bass as bass
import concourse.mybir as mybir
from concourse.bass_utils import run_bass_kernel_spmd

F32 = mybir.dt.float32
BF16 = mybir.dt.bfloat16
AF = mybir.ActivationFunctionType
ALU = mybir.AluOpType

NCORES = 8
D = 1024
T = 1536
TP = 1024
TS = 512
NBLK = 3
EPS = 1e-6
NKEY_S = 2560
SM_SCALE = 192.0 ** -0.5
NQ = 16


class Op:
    __slots__ = ("eng", "kind", "fn", "deps", "sig", "cnt", "semkey", "semval", "waits", "hoist", "idx")

    def __init__(self, eng, kind, fn):
        self.hoist = False
        self.idx = 0
        self.eng = eng
        self.kind = kind
        self.fn = fn
        self.deps = set()
        self.sig = False
        self.cnt = 0
        self.semkey = None
        self.semval = 0
        self.waits = []


class Prog:
    def __init__(self, nc):
        self.nc = nc
        self.ops = []
        self.tinfo = {}
        self.recs = {}

    def reg(self, name, kind, row):
        self.tinfo[name] = (kind, row)

    def box(self, ap):
        name = ap.tensor.name
        kind, row = self.tinfo[name]
        aps = ap.ap
        off = ap.offset
        if kind == "dram":
            ext = sum((c - 1) * abs(s) for s, c in aps) + 1
            return name, (0, 1, off, off + ext)
        p0 = off // row
        f0 = off % row
        pc = aps[0][1]
        ext = sum((c - 1) * abs(s) for s, c in aps[1:]) + 1
        return name, (p0, p0 + pc, f0, f0 + ext)

    @staticmethod
    def _ov(a, b):
        return a[0] < b[1] and b[0] < a[1] and a[2] < b[3] and b[2] < a[3]

    @staticmethod
    def _contains(a, b):
        return a[0] <= b[0] and b[1] <= a[1] and a[2] <= b[2] and b[3] <= a[3]

    def _dep(self, i, j):
        if i == j:
            return
        a, b = self.ops[i], self.ops[j]
        if a.eng == "pe" and b.eng == "pe" and a.kind == "c" and b.kind == "c":
            return
        a.deps.add(b)

    def add(self, eng, fn, reads, writes, kind="c"):
        i = len(self.ops)
        op = Op(eng, kind, fn)
        op.idx = i
        self.ops.append(op)
        rkey = eng if kind == "c" else ("x", i)
        for ap in reads:
            name, bx = self.box(ap)
            lst = self.recs.setdefault(name, [])
            found = None
            for r in lst:
                if self._ov(r[0], bx):
                    if r[1] is not None:
                        self._dep(i, r[1])
                    if r[0] == bx:
                        found = r
            if found is None:
                found = [bx, None, {}]
                lst.append(found)
            found[2][rkey] = i
        for ap in writes:
            name, bx = self.box(ap)
            lst = self.recs.setdefault(name, [])
            keep = []
            for r in lst:
                if self._ov(r[0], bx):
                    if r[1] is not None:
                        self._dep(i, r[1])
                    for j in r[2].values():
                        self._dep(i, j)
                    if self._contains(bx, r[0]):
                        continue
                keep.append(r)
            keep.append([bx, i, {}])
            self.recs[name] = keep
        return i

    def mm(self, out, lhsT, rhs, start=True, stop=True):
        rd = [lhsT, rhs] + ([] if start else [out])
        return self.add("pe", lambda e: e.matmul(out, lhsT, rhs, start=start, stop=stop), rd, [out])

    def tr(self, out, in_, ident):
        return self.add("pe", lambda e: e.transpose(out, in_, ident), [in_, ident], [out])

    def act(self, out, in_, func, bias=None, scale=None):
        rd = [in_]
        kw = {}
        if bias is not None:
            kw["bias"] = bias
            if not isinstance(bias, (int, float)):
                rd.append(bias)
        if scale is not None:
            kw["scale"] = scale
            if not isinstance(scale, (int, float)):
                rd.append(scale)
        return self.add("act", lambda e: e.activation(out, in_, func, **kw), rd, [out])

    def tt(self, eng, out, a, b, op):
        return self.add(eng, lambda e: e.tensor_tensor(out, a, b, op), [a, b], [out])

    def ts(self, eng, out, a, s1, s2, op0, op1=None):
        rd = [a] + [s for s in (s1, s2) if s is not None and not isinstance(s, (int, float))]
        if op1 is None:
            return self.add(eng, lambda e: e.tensor_scalar(out, a, s1, None, op0), rd, [out])
        return self.add(eng, lambda e: e.tensor_scalar(out, a, s1, s2, op0, op1), rd, [out])

    def stt(self, out, in0, scalar, in1, op0, op1):
        rd = [in0, in1] + ([] if isinstance(scalar, (int, float)) else [scalar])
        return self.add("dve", lambda e: e.scalar_tensor_tensor(out, in0, scalar, in1, op0, op1), rd, [out])

    def copy(self, eng, out, in_):
        if eng == "act":
            return self.add("act", lambda e: e.copy(out, in_), [in_], [out])
        return self.add(eng, lambda e: e.tensor_copy(out, in_), [in_], [out])

    def recip(self, out, in_):
        return self.add("dve", lambda e: e.reciprocal(out, in_), [in_], [out])

    def memset(self, eng, ap, val):
        return self.add(eng, lambda e: e.memset(ap, val), [], [ap])

    def dma(self, q, out, in_, hoist=False):
        i = self.add(q, lambda e: e.dma_start(out=out, in_=in_), [in_], [out], kind="d")
        self.ops[i].hoist = hoist
        return i

    def hoist_ops(self):
        after = {}
        for op in self.ops:
            if op.hoist:
                t = max((d.idx for d in op.deps), default=-1)
                after.setdefault(t, []).append(op)
        new = list(after.get(-1, []))
        for op in self.ops:
            if not op.hoist:
                new.append(op)
            new.extend(after.get(op.idx, []))
        assert len(new) == len(self.ops)
        self.ops = new

    def generic(self, eng, fn, reads, writes, kind="c"):
        return self.add(eng, fn, reads, writes, kind=kind)

    def finalize(self, sems, qsems, ccsems, block):
        self.hoist_ops()
        ops = self.ops
        qn = {}
        ncc = 0
        for i, op in enumerate(ops):
            if op.kind == "d":
                lst = qn.setdefault(op.eng, [])
                n = len(lst)
                op.semkey = ("q", op.eng, n % NQ)
                op.semval = 16 * (n // NQ + 1)
                if n >= NQ:
                    op.deps.add(lst[n - NQ])
                lst.append(op)
            elif op.kind == "cc":
                op.semkey = ("cc", ncc)
                op.semval = 1
                ncc += 1
        for op in ops:
            for d in op.deps:
                d.sig = True
        cnts = {}
        for op in ops:
            if op.kind == "c" and op.sig:
                cnts[op.eng] = cnts.get(op.eng, 0) + 1
                op.cnt = cnts[op.eng]
                op.semkey = op.eng
                op.semval = op.cnt
        know = {}
        clocks = {}
        for i, op in enumerate(ops):
            kn = know.setdefault(op.eng, {})
            for pj in sorted(op.deps, key=lambda d: d.idx):
                if kn.get(pj.semkey, 0) < pj.semval:
                    op.waits.append((pj.semkey, pj.semval))
                ck = clocks.get(id(pj))
                if ck is not None:
                    for k, v in ck.items():
                        if kn.get(k, 0) < v:
                            kn[k] = v
                if kn.get(pj.semkey, 0) < pj.semval:
                    kn[pj.semkey] = pj.semval
            if op.kind != "c" or op.sig:
                ck = dict(kn)
                ck[op.semkey] = max(ck.get(op.semkey, 0), op.semval)
                clocks[id(op)] = ck
        final = {}
        for op in ops:
            if op.kind in ("d", "cc"):
                final[op.semkey] = max(final.get(op.semkey, 0), op.semval)

        def semof(key):
            if isinstance(key, str):
                return sems[key]
            if key[0] == "q":
                return qsems[key[1]][key[2]]
            return ccsems[key[1]]

        def emit(engname):
            def body(e):
                for op in ops:
                    if op.eng != engname:
                        continue
                    w = {}
                    for k, v in op.waits:
                        w[k] = max(w.get(k, 0), v)
                    for k, v in w.items():
                        e.wait_ge(semof(k), v)
                    inst = op.fn(e)
                    if op.kind == "d":
                        inst.then_inc(semof(op.semkey), 16)
                    elif op.kind == "cc":
                        inst.then_inc(semof(op.semkey))
                    elif op.sig:
                        inst.then_inc(sems[op.eng], 1)
                if engname == "sp":
                    for k, v in final.items():
                        e.wait_ge(semof(k), v)
            return body

        block.tensor(emit("pe"))
        block.scalar(emit("act"))
        block.vector(emit("dve"))
        block.gpsimd(emit("pool"))
        block.sync(emit("sp"))


def build_program():
    nc = bass.Bass("TRN2", target_bir_lowering=False)
    P = Prog(nc)

    def dram(name, shape, dt, kind):
        t = nc.dram_tensor(name, shape, dt, kind=kind) if kind else nc.dram_tensor(name, shape, dt)
        P.reg(name, "dram", 0)
        return t.ap()

    xp = dram("xp", [TP, D], F32, "ExternalInput")
    xs = dram("xs", [TS, D], F32, "ExternalInput")
    cckv = dram("cckv", [2, 512, 256], F32, "ExternalInput")
    ckpe = dram("ckpe", [2, 512, 64], F32, "ExternalInput")
    cT_d = dram("cT", [128, 8, 2], F32, "ExternalInput")
    normg_d = dram("normg", [128, 4, 8], F32, "ExternalInput")
    bmod_d = dram("bmod", [128, 4, 24], F32, "ExternalInput")
    qg_d = dram("qg", [128, 2, 4], F32, "ExternalInput")
    kvg_d = dram("kvg", [128, 2, 2], F32, "ExternalInput")
    vg_d = dram("vg", [128, 2, 16], F32, "ExternalInput")
    fg_d = dram("fg", [128, 8], F32, "ExternalInput")
    vb_d = dram("vb_bc", [128, 2, 2048], F32, "ExternalInput")
    bs_d = dram("bs_bc", [128, 2, 8, 128], F32, "ExternalInput")
    cos_d = dram("cos", [64, 512], F32, "ExternalInput")
    sin_d = dram("sin", [64, 512], F32, "ExternalInput")
    ident_d = dram("ident", [128, 128], F32, "ExternalInput")
    w_mod = dram("w_mod", [4, 1024, 3072], F32, "ExternalInput")
    a_w_in = dram("attn_w_in", [2, 1024, 1856], F32, "ExternalInput")
    a_w_uq = dram("attn_w_uq", [2, 512, 1536], F32, "ExternalInput")
    a_w_ukv = dram("attn_w_ukv", [2, 256, 2048], F32, "ExternalInput")
    a_w_o = dram("attn_w_o", [2, 1024, 1024], F32, "ExternalInput")
    m_w_in = dram("mlp_w_in", [2, 1024, 6144], F32, "ExternalInput")
    m_w_s = dram("mlp_w_s", [2, 8, 128, 128], F32, "ExternalInput")
    m_w_o = dram("mlp_w_o", [2, 2048, 1024], F32, "ExternalInput")
    yp = dram("yp", [TP, D], F32, "ExternalOutput")
    ys = dram("ys", [TS, D], F32, "ExternalOutput")
    nckv = dram("nckv", [4, 2, 256, 256], F32, "ExternalOutput")
    nkpe = dram("nkpe", [4, 2, 256, 64], F32, "ExternalOutput")
    agin = [dram("agin%d" % a, [320, 512], BF16, None) for a in range(2)]
    agout = [dram("agout%d" % a, [4 * 320, 512], BF16, None) for a in range(2)]

    import contextlib
    es = contextlib.ExitStack()

    def sb(name, shape, dt):
        t = es.enter_context(nc.sbuf_tensor(name, shape, dt))
        row = 1
        for s in shape[1:]:
            row *= s
        P.reg(name, "sb", row)
        return t

    ARN = 45056
    X = sb("X", [128, 8, T], F32)
    WR = sb("WR", [128, 4, 4096], BF16)
    AR = sb("AR", [128, ARN], BF16)
    A32 = sb("A32", [128, 3072], F32)
    TMP = sb("TMP", [128, 3, 512], F32)
    RSTD = sb("RSTD", [128, 2, 512], F32)
    SQ = sb("SQ", [128, 2, 512], BF16)
    SQ3 = sb("SQ3", [128, 4, 512], BF16)
    COS = sb("COS", [64, 512], F32)
    SIN = sb("SIN", [64, 512], F32)
    ONES = sb("ONES", [128, 128], BF16)
    IDF = sb("IDF", [128, 128], F32)
    IDB = sb("IDB", [128, 128], BF16)
    CT32 = sb("CT32", [128, 8, 2], F32)
    SC = sb("SC", [128, 8, 2], BF16)
    NORMG = sb("NORMG", [128, 4, 8], F32)
    BMOD = sb("BMOD", [128, 4, 24], F32)
    QG = sb("QG", [128, 2, 4], F32)
    KVG = sb("KVG", [128, 2, 2], F32)
    VG = sb("VG", [128, 2, 16], F32)
    FG = sb("FG", [128, 8], F32)
    MODV = sb("MODV", [128, 4, 24, 2], F32)
    MA = sb("MA", [128, 4, 8, 2], F32)
    BNS = sb("BNS", [128, 12, 24], F32)
    BNA = sb("BNA", [128, 2, 2], F32)
    RS1 = sb("RS1", [128, 2, 1], F32)
    NMR = sb("NMR", [128, 2, 1], F32)
    WSRAW = sb("WSRAW", [128, 2, 128], F32)

    PS = es.enter_context(nc.psum_tensor("PS", [128, 8, 512], F32))
    P.reg("PS", "ps", 4096)

    sems = {k: es.enter_context(nc.semaphore("s_" + k)) for k in ("pe", "act", "dve", "pool")}
    qsems = {q: [es.enter_context(nc.semaphore("q_%s_%d" % (q, i))) for i in range(NQ)] for q in ("sp", "pool")}
    ccsems = [es.enter_context(nc.semaphore("cc%d" % i)) for i in range(2)]

    state = {"slab": 0, "bank": 0, "tmp": 0, "sq": 0, "rstd": 0, "ev": 0, "sq3": 0}

    sl_busy = [False] * 4
    sl_rel = [0, 1, 2, 3]
    sl_tags = {}

    def release_tag(tag):
        if tag in sl_tags:
            i = sl_tags.pop(tag)
            sl_busy[i] = False
            state["slab"] += 1
            sl_rel[i] = 4 + state["slab"]

    def slab(kc, ncols, tag="s"):
        release_tag(tag)
        free = [i for i in range(4) if not sl_busy[i]]
        assert free, "weight ring exhausted"
        i = min(free, key=lambda j: sl_rel[j])
        sl_busy[i] = True
        sl_tags[tag] = i
        return WR[:, i, 0:kc * ncols].rearrange("p (k n) -> p k n", k=kc)

    def wload(dst, src2d, kc):
        P.dma("pool", dst, src2d.rearrange("(k p) n -> p k n", p=128), hoist=True)

    def bank(pool=None):
        pool = pool or (0, 1, 2, 3, 4, 5, 6, 7)
        b = pool[state["bank"] % len(pool)]
        state["bank"] += 1
        return b

    def odpair():
        state["od"] = state.get("od", 0) + 1
        return ((4, 6), (5, 7))[state["od"] % 2]

    def tmp32():
        i = state["tmp"] % 3
        state["tmp"] += 1
        return TMP[:, i, :]

    def sqt():
        i = state["sq"] % 2
        state["sq"] += 1
        return SQ[:, i, :]

    def rstdt():
        i = state["rstd"] % 2
        state["rstd"] += 1
        return RSTD[:, i, :]

    def evac_eng():
        state["ev"] += 1
        return "act" if state["ev"] % 2 else "dve"

    def evac_copy(out, in_, eng=None):
        eng = eng or evac_eng()
        P.copy(eng, out, in_)

    def av(off, n):
        return AR[:, off:off + n]

    def rstd_from_ps(ps_ap, n_feat, npart=128):
        r = rstdt()
        P.act(r[0:npart], ps_ap, AF.Sqrt, bias=EPSB[0:npart, :], scale=1.0 / n_feat)
        P.recip(r[0:npart], r[0:npart])
        return r

    EPSB = sb("EPSB", [128, 1], F32)

    P.memset("dve", EPSB[:], EPS)
    P.memset("dve", ONES[:], 1.0)
    for dst, src in ((IDF, ident_d), (CT32, cT_d), (NORMG, normg_d), (BMOD, bmod_d), (QG, qg_d),
                     (KVG, kvg_d), (VG, vg_d), (FG, fg_d), (COS, cos_d), (SIN, sin_d)):
        P.dma("sp", dst[:], src)
    P.copy("dve", IDB[:], IDF[:])
    P.act(SC[:], CT32[:], AF.Silu)

    def emit_mod_slab(l, sj, pool=None):
        s = slab(8, 512, "mod")
        wload(s, w_mod[l, :, sj * 512:(sj + 1) * 512], 8)
        b = bank(pool)
        for jj in range(4):
            for k in range(8):
                P.mm(PS[:, b, 2 * jj:2 * jj + 2], s[:, k, jj * 128:(jj + 1) * 128], SC[:, k, :],
                     start=(k == 0), stop=(k == 7))
        if pool is not None:
            for jj in (3, 2, 1, 0):
                P.act(MODV[:, l, 4 * sj + jj, :], PS[:, b, 2 * jj:2 * jj + 2], AF.Identity,
                      bias=BMOD[:, l, 4 * sj + jj:4 * sj + jj + 1])
        else:
            P.tt("dve", MODV[:, l, 4 * sj:4 * sj + 4, :], PS[:, b, 0:8].rearrange("p (j v) -> p j v", v=2),
                 BMOD[:, l, 4 * sj:4 * sj + 4].unsqueeze(2).to_broadcast([128, 4, 2]), ALU.add)
        release_tag("mod")
        if sj == 3:
            P.stt(MA[:, l, :, :], MODV[:, l, 8:16, :], 1.0,
                  NORMG[:, l, :].unsqueeze(2).to_broadcast([128, 8, 2]), ALU.add, ALU.mult)

    def emit_mod(l):
        for sj in range(6):
            emit_mod_slab(l, sj)

    def vsel(blk):
        return 0 if blk < 2 else 1

    def ssq_ps(src_fn, nchunks, blk_cols, npart=128):
        b = bank()
        n = blk_cols
        for c in range(nchunks):
            s = sqt()
            src = src_fn(c)
            if c % 2 == 0:
                P.act(s[:, 0:n], src, AF.Square)
            else:
                P.tt("dve", s[:, 0:n], src, src, ALU.mult)
            P.mm(PS[:, b, 0:n], ONES[:], s[:, 0:n], start=(c == 0), stop=(c == nchunks - 1))
        return PS[:, b, 0:n]

    H_OFF = 0
    H = av(H_OFF, 8 * T).rearrange("p (c t) -> p c t", c=8)

    SSQB = {0: 5, 1: 6, 2: 7}

    def x_update(l, dc, blk, b):
        cs = slice(blk * 512, (blk + 1) * 512)
        v = vsel(blk)
        P.stt(X[:, dc, cs], PS[:, b, :], MODV[:, l, 16 + dc, v:v + 1], X[:, dc, cs], ALU.mult, ALU.add)
        s = SQ3[:, state["sq3"] % 4, :]
        state["sq3"] += 1
        P.tt("pool", s, X[:, dc, cs], X[:, dc, cs], ALU.mult)
        pend.append((blk, dc, s))
        while len(pend) > 2:
            x_flush()

    pend = []

    def x_flush_item(item):
        blk, dc, s = item
        P.mm(PS[:, SSQB[blk], :], ONES[:], s, start=(dc == 0), stop=(dc == 7))

    def x_flush():
        x_flush_item(pend.pop(0))

    def flush_blk(blk):
        keep = []
        while pend:
            item = pend.pop(0)
            if item[0] == blk:
                x_flush_item(item)
            else:
                keep.append(item)
        pend.extend(keep)

    def emit_h_blk(l, blk):
        cs = slice(blk * 512, (blk + 1) * 512)
        v = vsel(blk)
        if l == 0:
            ps = ssq_ps(lambda c: X[:, c, cs], 8, 512)
        else:
            flush_blk(blk)
            ps = PS[:, SSQB[blk], :]
        r = rstd_from_ps(ps, 1024.0)

        def one(c):
            t = tmp32()
            P.stt(t, X[:, c, cs], MA[:, l, c, v:v + 1], r, ALU.mult, ALU.mult)
            P.act(H[:, c, cs], t, AF.Identity, bias=MODV[:, l, c, v:v + 1])
        for c in range(8):
            dq.append(lambda c=c: one(c))

    dq = []

    def dq_step():
        if dq:
            dq.pop(0)()

    def dq_flush():
        while dq:
            dq.pop(0)()

    def emit_h(l):
        for blk in (2, 0, 1):
            emit_h_blk(l, blk)

    def after_update_blk(l, blk):
        if l + 1 < 4:
            emit_h_blk(l + 1, blk)
        else:
            emit_final_blk(blk)

    def emit_load_x():
        r0 = {}

        def load_tile(tt_):
            st = A32[:, (tt_ % 3) * 1024:(tt_ % 3 + 1) * 1024]
            if tt_ < 8:
                src = xp[tt_ * 128:(tt_ + 1) * 128, :]
            else:
                src = xs[(tt_ - 8) * 128:(tt_ - 7) * 128, :]
            P.dma("sp", st, src)
            for half in range(2):
                b = bank()
                for cc in range(4):
                    c = half * 4 + cc
                    P.tr(PS[:, b, cc * 128:(cc + 1) * 128], st[:, c * 128:(c + 1) * 128], IDF[:])
                evac_copy(X[:, half * 4:half * 4 + 4, tt_ * 128:(tt_ + 1) * 128],
                          PS[:, b, :].rearrange("p (c t) -> p c t", c=4))

        def stats(blk, rt):
            cs = slice(blk * 512, (blk + 1) * 512)
            ps = ssq_ps(lambda c: X[:, c, cs], 8, 512)
            P.act(rt, ps, AF.Sqrt, bias=EPSB[:], scale=1.0 / 1024.0)
            P.recip(rt, rt)
            r0[blk] = rt

        emit_mod_slab(0, 0)
        load_tile(8)
        load_tile(9)
        emit_mod_slab(0, 1)
        load_tile(10)
        load_tile(11)
        stats(2, rstdt())
        emit_mod_slab(0, 2)
        load_tile(0)
        load_tile(1)
        emit_mod_slab(0, 3)
        load_tile(2)
        load_tile(3)
        for tt_ in (4, 5, 6, 7):
            load_tile(tt_)
        stats(0, A32[:, 0:512])
        stats(1, A32[:, 512:1024])
        emit_cache_part(0)
        for blk in (2, 0, 1):
            cs = slice(blk * 512, (blk + 1) * 512)
            v = vsel(blk)
            for c in range(8):
                t = tmp32()
                P.stt(t, X[:, c, cs], MA[:, 0, c, v:v + 1], r0[blk], ALU.mult, ALU.mult)
                P.act(H[:, c, cs], t, AF.Identity, bias=MODV[:, 0, c, v:v + 1])

    def emit_final_blk(blk):
        flush_blk(blk)
        cs = slice(blk * 512, (blk + 1) * 512)
        r = rstd_from_ps(PS[:, SSQB[blk], :], 1024.0)
        for c in range(8):
            dq.append(lambda c=c: P.stt(X[:, c, cs], X[:, c, cs], FG[:, c:c + 1], r, ALU.mult, ALU.mult))

        def out_tile(tt_):
            st = A32[:, (tt_ % 3) * 1024:(tt_ % 3 + 1) * 1024]
            for half in range(2):
                b = bank((0, 1, 2, 3, 4))
                for cc in range(4):
                    c = half * 4 + cc
                    P.tr(PS[:, b, cc * 128:(cc + 1) * 128], X[:, c, tt_ * 128:(tt_ + 1) * 128], IDF[:])
                evac_copy(st[:, half * 512:(half + 1) * 512], PS[:, b, :])
            if tt_ < 8:
                dst = yp[tt_ * 128:(tt_ + 1) * 128, :]
            else:
                dst = ys[(tt_ - 8) * 128:(tt_ - 7) * 128, :]
            P.dma("sp", dst, st)
        for t4 in range(4):
            dq.append(lambda t4=t4: out_tile(blk * 4 + t4))

    GZ = av(12288, 8 * T).rearrange("p (c t) -> p c t", c=8)
    CQN = av(24576, 4 * T).rearrange("p (c t) -> p c t", c=4)
    CKVN = av(30720, 2 * T).rearrange("p (c t) -> p c t", c=2)
    KPEB = av(33792, T)
    KVALL = av(35328, 2 * NKEY_S).rearrange("p (c t) -> p c t", c=2)
    KPEALL = av(40448, NKEY_S)
    WUQSW = av(43008, 2048).rearrange("p (k n) -> p k n", k=4)
    QN = av(0, T)
    QR = av(1536, T)
    KTP = av(3072, 1024)
    KTS = av(4096, NKEY_S)
    VV = av(6656, 28 * 128).rearrange("p (t d) -> p t d", t=28)
    PT = av(10240, 2048).rearrange("p (i n) -> p i n", i=4)
    ACCD = A32[:, 0:512]
    ACCP = A32[:, 512:1024]
    OUTKV = A32[:, 0:1280].rearrange("p (t f) -> p t f", t=4)
    CST = A32[:, 1280:2560].rearrange("p (t f) -> p t f", t=4)

    def emit_cache_part(a):
        P.dma("sp", CST[:, :, 0:256], cckv[a].rearrange("(t p) f -> p t f", p=128), hoist=True)
        P.dma("sp", CST[:, :, 256:320], ckpe[a].rearrange("(t p) f -> p t f", p=128), hoist=True)
        for j in range(2):
            b = bank((0, 1, 2, 3))
            for t4 in range(4):
                P.tr(PS[:, b, t4 * 128:(t4 + 1) * 128], CST[:, t4, j * 128:(j + 1) * 128], IDF[:])
            evac_copy(KVALL[:, j, 0:512], PS[:, b, :])
        b = bank((0, 1, 2, 3))
        for t4 in range(4):
            P.tr(PS[0:64, b, t4 * 128:(t4 + 1) * 128], CST[:, t4, 256:320], IDF[:])
        evac_copy(KPEALL[0:64, 0:512], PS[0:64, b, :])

    def emit_attn(a, l):
        blks = (2, 0, 1)
        P.memset("pool", KPEB[64:128, :], 0.0)
        P.memset("pool", KPEALL[64:128, :], 0.0)
        if l != 0:
            emit_cache_part(a)
        s0 = slab(8, 512, "q")
        wload(s0, a_w_in[a, :, 0:512], 8)
        s1 = slab(8, 384, "kv")
        wload(s1[:, :, 0:320], a_w_in[a, :, 512:832], 8)
        zsl = []
        for zs in range(2):
            s2 = slab(8, 512, "z%d" % zs)
            wload(s2, a_w_in[a, :, 832 + zs * 512:832 + (zs + 1) * 512], 8)
            zsl.append(s2)
        zq = [(zs, j, blk) for zs in range(2) for j in range(4) for blk in blks]

        def emit_z(n):
            for _ in range(n):
                if not zq:
                    return
                zs, j, blk = zq.pop(0)
                if zs == 1 and "z0" in sl_tags:
                    release_tag("z0")
                cs_ = slice(blk * 512, (blk + 1) * 512)
                b_ = bank((4, 5, 6, 7))
                for k in range(8):
                    P.mm(PS[:, b_, :], zsl[zs][:, k, j * 128:(j + 1) * 128], H[:, k, cs_], start=(k == 0), stop=(k == 7))
                P.act(GZ[:, zs * 4 + j, cs_], PS[:, b_, :], AF.Silu)

        for blk in blks:
            cs = slice(blk * 512, (blk + 1) * 512)
            bq = [bank((0, 1, 2, 3)) for _ in range(4)]
            for j in range(4):
                for k in range(8):
                    P.mm(PS[:, bq[j], :], s0[:, k, j * 128:(j + 1) * 128], H[:, k, cs], start=(k == 0), stop=(k == 7))
            bs_ = bank((4, 5, 6, 7))
            for j in range(4):
                s = sqt()
                P.act(s, PS[:, bq[j], :], AF.Square)
                P.mm(PS[:, bs_, :], ONES[:], s, start=(j == 0), stop=(j == 3))
            r = rstd_from_ps(PS[:, bs_, :], 512.0)
            for j in range(4):
                P.stt(CQN[:, j, cs], PS[:, bq[j], :], QG[:, a, j:j + 1], r, ALU.mult, ALU.mult)
            emit_z(4)
        release_tag("q")
        kv4 = s1[:, :, 256:320].rearrange("p k (x h i) -> p k x h i", x=2, h=2)
        sw4 = s1[:, :, 320:384].rearrange("p k (x h i) -> p k x h i", x=2, h=2)
        for x_ in range(2):
            P.ts("dve", sw4[:, :, x_, 0, :], kv4[:, :, x_, 1, :], -1.0, None, ALU.mult)
            P.copy("dve", sw4[:, :, x_, 1, :], kv4[:, :, x_, 0, :])
        for blk in blks:
            cs = slice(blk * 512, (blk + 1) * 512)
            bk = [bank((0, 1, 2, 3)) for _ in range(2)]
            for j in range(2):
                for k in range(8):
                    P.mm(PS[:, bk[j], :], s1[:, k, j * 128:(j + 1) * 128], H[:, k, cs], start=(k == 0), stop=(k == 7))
            bs_ = bank((4, 5, 6, 7))
            for j in range(2):
                s = sqt()
                P.act(s, PS[:, bk[j], :], AF.Square)
                P.mm(PS[:, bs_, :], ONES[:], s, start=(j == 0), stop=(j == 1))
            r = rstd_from_ps(PS[:, bs_, :], 256.0)
            bp = bank((4, 5, 6, 7))
            for k in range(8):
                P.mm(PS[0:64, bp, :], s1[:, k, 256:320], H[:, k, cs], start=(k == 0), stop=(k == 7))
            if blk == 2:
                bp2 = bank((4, 5, 6, 7))
                for k in range(8):
                    P.mm(PS[0:64, bp2, :], s1[:, k, 320:384], H[:, k, cs], start=(k == 0), stop=(k == 7))
                t1 = tmp32()
                t2 = tmp32()
                P.tt("dve", t1[0:64], PS[0:64, bp, :], COS[:], ALU.mult)
                P.tt("dve", t2[0:64], PS[0:64, bp2, :], SIN[:], ALU.mult)
                P.tt("dve", KPEB[0:64, cs], t1[0:64], t2[0:64], ALU.add)
                for j in range(2):
                    P.stt(CKVN[:, j, cs], PS[:, bk[j], :], KVG[:, a, j:j + 1], r, ALU.mult, ALU.mult)
                for j in range(2):
                    P.dma("sp", agin[a][j * 128:(j + 1) * 128, :], CKVN[:, j, cs])
                P.dma("sp", agin[a][256:320, :], KPEB[0:64, cs])
                P.generic("pool", lambda e, a=a: e.collective_compute(
                    "AllGather", ALU.bypass, replica_groups=[[0, 1, 2, 3], [4, 5, 6, 7]],
                    ins=[agin[a].opt()], outs=[agout[a].opt()]), [agin[a]], [agout[a]], kind="cc")
            else:
                evac_copy(KPEB[0:64, cs], PS[0:64, bp, :], "act")
                tk = tmp32()
                evac_copy(tk[0:64], PS[0:64, bp, :], "dve")
                tcs = []
                for j in range(2):
                    tc = tmp32()
                    P.stt(tc, PS[:, bk[j], :], KVG[:, a, j:j + 1], r, ALU.mult, ALU.mult)
                    P.copy("act", CKVN[:, j, cs], tc)
                    tcs.append(tc)
                emit_z(3)
                for j in range(2):
                    tc = tcs[j]
                    b = bank((0, 1, 2, 3))
                    for t4 in range(4):
                        P.tr(PS[:, b, t4 * 128:(t4 + 1) * 128], tc[:, t4 * 128:(t4 + 1) * 128], IDF[:])
                    evac_copy(OUTKV[:, :, j * 128:(j + 1) * 128], PS[:, b, :].rearrange("p (t f) -> p t f", t=4))
                b = bank((0, 1, 2, 3))
                for t4 in range(4):
                    P.tr(PS[:, b, t4 * 64:(t4 + 1) * 64], tk[0:64, t4 * 128:(t4 + 1) * 128], IDF[0:64, 0:64])
                evac_copy(OUTKV[:, :, 256:320], PS[:, b, 0:256].rearrange("p (t f) -> p t f", t=4))
                for sl in range(2):
                    seq = blk * 2 + sl
                    P.dma("sp", nckv[seq, a].rearrange("(t p) f -> p t f", p=128), OUTKV[:, 2 * sl:2 * sl + 2, 0:256])
                    P.dma("sp", nkpe[seq, a].rearrange("(t p) f -> p t f", p=128), OUTKV[:, 2 * sl:2 * sl + 2, 256:320])
            if blk == 2:
                emit_z(3)
        release_tag("kv")
        emit_z(len(zq))
        release_tag("z0")
        release_tag("z1")
        agv = agout[a].rearrange("(r f) t -> f r t", f=320)
        for j in range(2):
            P.dma("sp", KVALL[:, j, 512:NKEY_S].rearrange("p (r t) -> p r t", r=4),
                  agv[j * 128:(j + 1) * 128, :, :])
        P.dma("sp", KPEALL[0:64, 512:NKEY_S].rearrange("p (r t) -> p r t", r=4), agv[256:320, :, :])
        P.memset("pool", QR[64:128, :], 0.0)
        wkv = slab(2, 2048, "wkv")
        wload(wkv[:, :, 0:1024], a_w_ukv[a, :, 0:1024], 2)
        wload(wkv[:, :, 1024:2048], a_w_ukv[a, :, 1024:2048], 2)
        wq = None
        for h in range(8):
            if l + 1 < 4 and h < 6:
                emit_mod_slab(l + 1, h, (0, 1, 2, 3))
            if l == 0 and h >= 6:
                emit_mod_slab(0, h - 2, (0, 1, 2, 3))
            if h == 0 or h == 3:
                g4 = 0 if h == 0 else 1
                wq_new = slab(4, 768, "wq%d" % g4)
                wload(wq_new, a_w_uq[a, :, g4 * 768:(g4 + 1) * 768], 4)

                def build_sw(wq_new=wq_new, g4=g4):
                    for hh in range(4):
                        src = wq_new[:, :, hh * 192 + 128:hh * 192 + 192].rearrange("p k (x h i) -> p k x h i", x=2, h=2)
                        dst = WUQSW[:, :, (g4 * 4 + hh) * 64:(g4 * 4 + hh + 1) * 64].rearrange(
                            "p k (x h i) -> p k x h i", x=2, h=2)
                        for x_ in range(2):
                            P.ts("dve", dst[:, :, x_, 0, :], src[:, :, x_, 1, :], -1.0, None, ALU.mult)
                            P.copy("dve", dst[:, :, x_, 1, :], src[:, :, x_, 0, :])
                if h == 0:
                    build_sw()
                    wq = wq_new
            if h == 4:
                release_tag("wq0")
                wq = wq_new
            hq = h % 4
            for blk in (0, 1, 2):
                cs = slice(blk * 512, (blk + 1) * 512)
                b = bank((0, 1, 2, 3))
                for k in range(4):
                    P.mm(PS[:, b, :], wq[:, k, hq * 192:hq * 192 + 128], CQN[:, k, cs], start=(k == 0), stop=(k == 3))
                evac_copy(QN[:, cs], PS[:, b, :], "act")
                b = bank((0, 1, 2, 3))
                for k in range(4):
                    P.mm(PS[0:64, b, :], wq[:, k, hq * 192 + 128:hq * 192 + 192], CQN[:, k, cs],
                         start=(k == 0), stop=(k == 3))
                if blk == 2:
                    b2 = bank((0, 1, 2, 3))
                    for k in range(4):
                        P.mm(PS[0:64, b2, :], WUQSW[:, k, h * 64:(h + 1) * 64], CQN[:, k, cs],
                             start=(k == 0), stop=(k == 3))
                    t1 = tmp32()
                    t2 = tmp32()
                    P.tt("dve", t1[0:64], PS[0:64, b, :], COS[:], ALU.mult)
                    P.tt("dve", t2[0:64], PS[0:64, b2, :], SIN[:], ALU.mult)
                    P.tt("pool", QR[0:64, cs], t1[0:64], t2[0:64], ALU.add)
                else:
                    evac_copy(QR[0:64, cs], PS[0:64, b, :], "act")
            for cb in range(2):
                b = bank((0, 1, 2, 3))
                for k in range(2):
                    P.mm(PS[:, b, :], wkv[:, k, h * 256:h * 256 + 128], CKVN[:, k, cb * 512:(cb + 1) * 512],
                         start=(k == 0), stop=(k == 1))
                evac_copy(KTP[:, cb * 512:(cb + 1) * 512], PS[:, b, :], "act")
            for cb in range(5):
                b = bank((0, 1, 2, 3))
                for k in range(2):
                    P.mm(PS[:, b, :], wkv[:, k, h * 256:h * 256 + 128], KVALL[:, k, cb * 512:(cb + 1) * 512],
                         start=(k == 0), stop=(k == 1))
                evac_copy(KTS[:, cb * 512:(cb + 1) * 512], PS[:, b, :], "act")
            for g in range(7):
                b = bank((0, 1, 2, 3))
                for t4 in range(4):
                    ti = g * 4 + t4
                    for k in range(2):
                        if ti < 8:
                            lt = CKVN[:, k, ti * 128:(ti + 1) * 128]
                        else:
                            lt = KVALL[:, k, (ti - 8) * 128:(ti - 7) * 128]
                        P.mm(PS[:, b, t4 * 128:(t4 + 1) * 128], lt, wkv[:, k, h * 256 + 128:h * 256 + 256],
                             start=(k == 0), stop=(k == 1))
                evac_copy(VV[:, g * 4:(g + 1) * 4, :], PS[:, b, :].rearrange("p (t d) -> p t d", t=4), "dve")
            cs2 = slice(1024, 1536)
            units = []
            bo_p, bd_p = odpair()
            for sl in range(2):
                units.append(("p", sl, bo_p, bd_p))
            bo_s, bd_s = odpair()
            for kt in range(20):
                units.append(("s", kt, bo_s, bd_s))
            bo_p, bd_p = odpair()
            for sl in range(2):
                units.append(("p", 2 + sl, bo_p, bd_p))
            LA = 3
            nun = len(units)
            pts = {}
            for i in range(nun + LA):
                if i < nun:
                    kind, idx, bo, bd = units[i]
                    b = bank((0, 1, 2, 3))
                    if kind == "s":
                        kt = idx
                        P.mm(PS[:, b, :], KTS[:, kt * 128:(kt + 1) * 128], QN[:, cs2], start=True, stop=False)
                        P.mm(PS[:, b, :], KPEALL[:, kt * 128:(kt + 1) * 128], QR[:, cs2], start=False, stop=True)
                    else:
                        seq = idx
                        qs = slice(seq * 256, (seq + 1) * 256)
                        for kt in range(2):
                            ks = slice(seq * 256 + kt * 128, seq * 256 + (kt + 1) * 128)
                            P.mm(PS[:, b, kt * 256:(kt + 1) * 256], KTP[:, ks], QN[:, qs], start=True, stop=False)
                            P.mm(PS[:, b, kt * 256:(kt + 1) * 256], KPEB[:, ks], QR[:, qs], start=False, stop=True)
                    pt = PT[:, i % 4, :]
                    P.act(pt, PS[:, b, :], AF.Exp, scale=SM_SCALE)
                    pts[i] = pt
                j = i - LA
                if j >= 0:
                    kind, idx, bo, bd = units[j]
                    pt = pts.pop(j)
                    if kind == "s":
                        kt = idx
                        P.mm(PS[:, bo, :], VV[:, 8 + kt, :], pt, start=(kt == 0), stop=(kt == 19))
                        P.mm(PS[:, bd, :], ONES[:], pt, start=(kt == 0), stop=(kt == 19))
                        if kt == 19:
                            emit_og(h, 2, bo, bd)
                    else:
                        seq = idx
                        sl = seq % 2
                        for kt in range(2):
                            P.mm(PS[:, bo, sl * 256:(sl + 1) * 256], VV[:, seq * 2 + kt, :], pt[:, kt * 256:(kt + 1) * 256],
                                 start=(kt == 0), stop=(kt == 1))
                        for kt in range(2):
                            P.mm(PS[:, bd, sl * 256:(sl + 1) * 256], ONES[:], pt[:, kt * 256:(kt + 1) * 256],
                                 start=(kt == 0), stop=(kt == 1))
                        if sl == 1:
                            emit_og(h, seq // 2, bo, bd)
            if h == 3:
                build_sw()
        release_tag("wkv")
        release_tag("wq1")
        sos = []
        for os_ in range(2):
            so = slab(8, 512, "wo%d" % os_)
            wload(so, a_w_o[a, :, os_ * 512:(os_ + 1) * 512], 8)
            sos.append(so)
        prev = None
        for blk in blks:
            cs = slice(blk * 512, (blk + 1) * 512)
            for dc in range(8):
                so = sos[dc // 4]
                j = dc % 4
                b = bank((0, 1, 2, 3, 4))
                for k in range(8):
                    P.mm(PS[:, b, :], so[:, k, j * 128:(j + 1) * 128], GZ[:, k, cs], start=(k == 0), stop=(k == 7))
                x_update(l, dc, blk, b)
                if dc == 1 and prev is not None:
                    after_update_blk(l, prev)
                dq_step()
            prev = blk
        emit_mlp_setup(l // 2)
        after_update_blk(l, prev)
        dq_flush()
        release_tag("wo0")
        release_tag("wo1")

    def emit_og(h, blk, bo, bd):
        cs = slice(blk * 512, (blk + 1) * 512)
        rd = tmp32()
        P.act(rd, PS[:, bd, :], AF.Ln)
        P.act(rd, rd, AF.Exp, scale=-1.0)
        t = tmp32()
        P.tt("dve", t, PS[:, bo, :], rd, ALU.mult)
        P.tt("pool", GZ[:, h, cs], t, GZ[:, h, cs], ALU.mult)

    VH = av(12288, 12 * 2048).rearrange("p (t f) -> p t f", t=12)
    WST = av(36864, 1024).rearrange("p (g q) -> p g q", g=8)
    VBB = av(37888, 2048)
    UT = av(39936, 1536).rearrange("p (i n) -> p i n", i=3)
    GT = av(41472, 1536).rearrange("p (i n) -> p i n", i=3)
    CC = A32[:, 0:2048].rearrange("p (f q) -> p f q", f=16)
    BSB = A32[:, 2048:3072].rearrange("p (g q) -> p g q", g=8)

    def ln_tile(ti):
        i2 = ti % 2
        P.generic("dve", lambda e: e.bn_aggr(BNA[:, i2, :], BNS[:, ti, :]), [BNS[:, ti, :]], [BNA[:, i2, :]])
        P.act(RS1[:, i2, :], BNA[:, i2, 1:2], AF.Sqrt, bias=EPSB[:], scale=1.0)
        P.recip(RS1[:, i2, :], RS1[:, i2, :])
        P.ts("dve", VH[:, ti, :], VH[:, ti, :], BNA[:, i2, 0:1], RS1[:, i2, :], ALU.subtract, ALU.mult)

    def emit_mlp_setup(m):
        bp5 = (0, 1, 2, 3, 4)
        P.dma("pool", VBB, vb_d[:, m, :], hoist=True)
        P.dma("sp", BSB, bs_d[:, m, :, :], hoist=True)
        WSRAW8 = A32[:, 0:1024].rearrange("p (g q) -> p g q", g=8)
        P.dma("sp", WSRAW8, m_w_s[m].rearrange("g p q -> p g q"), hoist=True)
        for g2 in range(2):
            b = bank(bp5)
            for gg in range(4):
                P.tr(PS[:, b, gg * 128:(gg + 1) * 128], WSRAW8[:, g2 * 4 + gg, :], IDF[:])
            evac_copy(WST[:, g2 * 4:(g2 + 1) * 4, :], PS[:, b, :].rearrange("p (g q) -> p g q", g=4))
        for f4 in range(4):
            b = bank(bp5)
            for ff in range(4):
                fc = f4 * 4 + ff
                P.mm(PS[:, b, ff * 128:(ff + 1) * 128], VBB[:, fc * 128:(fc + 1) * 128], WST[:, fc // 2, :],
                     start=True, stop=True)
            for g1 in (1, 0):
                g = f4 * 2 + g1
                P.tt("dve", CC[:, 2 * g:2 * g + 2, :],
                     PS[:, b, g1 * 256:(g1 + 1) * 256].rearrange("p (f q) -> p f q", f=2),
                     BSB[:, g, :].unsqueeze(1).to_broadcast([128, 2, 128]), ALU.add)

    def emit_mlp(m, l):
        blks = (2, 0, 1)
        for vs in range(4):
            s = slab(8, 512)
            wload(s, m_w_in[m, :, 2048 + vs * 512:2048 + (vs + 1) * 512], 8)
            for ti in range(12):
                b = bank()
                for k in range(8):
                    P.mm(PS[:, b, :], H[:, k, ti * 128:(ti + 1) * 128], s[:, k, :], start=(k == 0), stop=(k == 7))
                P.act(VH[:, ti, vs * 512:(vs + 1) * 512], PS[:, b, :], AF.Gelu_apprx_tanh)
                P.generic("dve", lambda e, ti=ti, q4=vs: e.bn_stats(
                    BNS[:, ti, q4 * 6:(q4 + 1) * 6], VH[:, ti, q4 * 512:(q4 + 1) * 512]),
                    [VH[:, ti, vs * 512:(vs + 1) * 512]], [BNS[:, ti, vs * 6:(vs + 1) * 6]])
                if vs == 3:
                    ln_tile(ti)
        release_tag("s")
        su = sz = None
        for fc in range(16):
            if l + 1 < 4 and fc % 2 == 1 and fc < 12:
                emit_mod_slab(l + 1, fc // 2)
            if fc % 4 == 0:
                su = slab(8, 512, "su")
                wload(su, m_w_in[m, :, fc * 128:fc * 128 + 512], 8)
                sz = slab(8, 512, "sz")
                wload(sz, m_w_in[m, :, 4096 + fc * 128:4096 + fc * 128 + 512], 8)
            j = fc % 4
            g = fc // 2
            for blk in blks:
                cs = slice(blk * 512, (blk + 1) * 512)
                b = bank()
                for k in range(8):
                    P.mm(PS[:, b, :], su[:, k, j * 128:(j + 1) * 128], H[:, k, cs], start=(k == 0), stop=(k == 7))
                P.act(UT[:, blk, :], PS[:, b, :], AF.Gelu_apprx_tanh)
            for blk in blks:
                cs = slice(blk * 512, (blk + 1) * 512)
                b = bank()
                for k in range(8):
                    P.mm(PS[:, b, :], sz[:, k, j * 128:(j + 1) * 128], H[:, k, cs], start=(k == 0), stop=(k == 7))
                P.act(GT[:, blk, :], PS[:, b, :], AF.Silu)
                P.tt("pool", UT[:, blk, :], UT[:, blk, :], GT[:, blk, :], ALU.mult)
            for blk in (0, 1, 2):
                b = bank()
                for t4 in range(4):
                    ti = blk * 4 + t4
                    P.mm(PS[:, b, t4 * 128:(t4 + 1) * 128], VH[:, ti, fc * 128:(fc + 1) * 128], WST[:, g, :],
                         start=True, stop=True)
                t = tmp32()
                P.stt(t.rearrange("p (t q) -> p t q", t=4), PS[:, b, :].rearrange("p (t q) -> p t q", t=4),
                      VG[:, m, fc:fc + 1], CC[:, fc, :].unsqueeze(1).to_broadcast([128, 4, 128]), ALU.mult, ALU.add)
                P.tt("dve", VH[:, blk * 4:(blk + 1) * 4, fc * 128:(fc + 1) * 128],
                     t.rearrange("p (t q) -> p t q", t=4), UT[:, blk, :].rearrange("p (t q) -> p t q", t=4), ALU.mult)
        release_tag("su")
        release_tag("sz")
        sos = []
        for os_ in range(4):
            so = slab(16, 256, "wo%d" % os_)
            wload(so, m_w_o[m, :, os_ * 256:(os_ + 1) * 256], 16)
            sos.append(so)
        prev = None
        for blk in blks:
            cs = slice(blk * 512, (blk + 1) * 512)
            for dc in range(8):
                so = sos[dc // 2]
                j = dc % 2
                b = bank((0, 1, 2, 3, 4))
                for k in range(16):
                    P.mm(PS[:, b, :].rearrange("p (t q) -> p t q", t=4), so[:, k, j * 128:(j + 1) * 128],
                         VH[:, blk * 4:(blk + 1) * 4, k * 128:(k + 1) * 128], start=(k == 0), stop=(k == 15))
                x_update(l, dc, blk, b)
                if dc == 1 and prev is not None:
                    after_update_blk(l, prev)
                dq_step()
            prev = blk
        after_update_blk(l, prev)
        dq_flush()
        for os_ in range(4):
            release_tag("wo%d" % os_)

    emit_load_x()
    for l in range(4):
        if l % 2 == 0:
            emit_attn(l // 2, l)
        else:
            emit_mlp(l // 2, l)

    with nc.Block() as block:
        P.finalize(sems, qsems, ccsems, block)
    es.close()
    return nc


def _rope_tables(core):
    r = core % 4
    t = np.arange(r * 512, (r + 1) * 512)
    row = (t // 64).astype(np.float32)
    col = (t % 64).astype(np.float32)
    inv = (1.0 / (np.float32(10000.0) ** (np.arange(0, 32, 2, dtype=np.float32) / np.float32(32)))).astype(np.float32)
    ang = np.concatenate([row[:, None] * inv, col[:, None] * inv], axis=-1).astype(np.float32)
    cos = np.cos(ang).astype(np.float32)
    sin = np.sin(ang).astype(np.float32)
    idx = np.array([(d // 32) * 16 + (d % 16) for d in range(64)])
    return np.ascontiguousarray(cos[:, idx].T), np.ascontiguousarray(sin[:, idx].T)


def _fm(v, nchunk):
    v = np.asarray(v, np.float32)
    lead = v.shape[:-1]
    v = v.reshape(lead + (nchunk, 128))
    v = np.moveaxis(v, -1, 0)
    return np.ascontiguousarray(v)


_NC_CACHE = {}


def kernel(x_prompt, x_sample, cache_ckv, cache_kpe, c, c_ctx, norm_g, w_mod, b_mod,
           attn_w_in, attn_q_norm_g, attn_kv_norm_g, attn_w_uq, attn_w_ukv, attn_w_o,
           mlp_w_in, mlp_v_norm_g, mlp_v_norm_b, mlp_w_s, mlp_b_s, mlp_w_o, final_norm_g):
    f = lambda a: np.ascontiguousarray(np.asarray(a, np.float32))
    x_prompt, x_sample, cache_ckv, cache_kpe = f(x_prompt), f(x_sample), f(cache_ckv), f(cache_kpe)
    c, c_ctx = f(c), f(c_ctx)
    if "nc" not in _NC_CACHE:
        _NC_CACHE["nc"] = build_program()
    nc = _NC_CACHE["nc"]
    shared = {
        "normg": _fm(norm_g, 8), "bmod": _fm(b_mod, 24), "qg": _fm(attn_q_norm_g, 4),
        "kvg": _fm(attn_kv_norm_g, 2), "vg": _fm(mlp_v_norm_g, 16), "fg": _fm(final_norm_g, 8),
        "vb_bc": np.ascontiguousarray(np.broadcast_to(f(mlp_v_norm_b)[None], (128, 2, 2048))),
        "bs_bc": np.ascontiguousarray(np.broadcast_to(f(mlp_b_s)[None], (128, 2, 8, 128))),
        "ident": np.eye(128, dtype=np.float32),
        "w_mod": f(w_mod), "attn_w_in": f(attn_w_in), "attn_w_uq": f(attn_w_uq), "attn_w_ukv": f(attn_w_ukv),
        "attn_w_o": f(attn_w_o), "mlp_w_in": f(mlp_w_in), "mlp_w_s": f(mlp_w_s), "mlp_w_o": f(mlp_w_o),
    }
    in_maps = []
    for i in range(NCORES):
        b = i // 4
        r = i % 4
        cos, sin = _rope_tables(i)
        cT = np.stack([c_ctx, c[b]], axis=-1).reshape(8, 128, 2).transpose(1, 0, 2)
        m = dict(shared)
        m.update({
            "xp": np.ascontiguousarray(x_prompt[4 * i:4 * i + 4].reshape(TP, D)),
            "xs": np.ascontiguousarray(x_sample[b, r * 512:(r + 1) * 512]),
            "cckv": np.ascontiguousarray(cache_ckv[b]),
            "ckpe": np.ascontiguousarray(cache_kpe[b]),
            "cT": np.ascontiguousarray(cT),
            "cos": cos, "sin": sin,
        })
        in_maps.append(m)
    res = run_bass_kernel_spmd(nc, in_maps, core_ids=list(range(NCORES)))
    rs = res.results
    y_prompt = np.concatenate([np.asarray(rs[i]["yp"]).reshape(4, 256, D) for i in range(NCORES)], axis=0)
    y_sample = np.stack([np.concatenate([np.asarray(rs[b * 4 + r]["ys"]) for r in range(4)], axis=0) for b in range(2)], axis=0)
    new_ckv = np.concatenate([np.asarray(rs[i]["nckv"]) for i in range(NCORES)], axis=0)
    new_kpe = np.concatenate([np.asarray(rs[i]["nkpe"]) for i in range(NCORES)], axis=0)
    return (y_prompt.astype(np.float32), y_sample.astype(np.float32),
            new_ckv.astype(np.float32), new_kpe.astype(np.float32))
```

```python
import numpy as np
import concourse.bass as bass
import concourse.mybir as mybir
from concourse.bass_utils import run_bass_kernel_spmd

F32 = mybir.dt.float32
BF16 = mybir.dt.bfloat16
AF = mybir.ActivationFunctionType
ALU = mybir.AluOpType

NCORES = 8
D = 1024
T = 1536
TP = 1024
TS = 512
NBLK = 3
EPS = 1e-6
NKEY_S = 2560
SM_SCALE = 192.0 ** -0.5
NQ = 16


class Op:
    __slots__ = ("eng", "kind", "fn", "deps", "sig", "cnt", "semkey", "semval", "waits", "hoist", "idx")

    def __init__(self, eng, kind, fn):
        self.hoist = False
        self.idx = 0
        self.eng = eng
        self.kind = kind
        self.fn = fn
        self.deps = set()
        self.sig = False
        self.cnt = 0
        self.semkey = None
        self.semval = 0
        self.waits = []


class Prog:
    def __init__(self, nc):
        self.nc = nc
        self.ops = []
        self.tinfo = {}
        self.recs = {}

    def reg(self, name, kind, row):
        self.tinfo[name] = (kind, row)

    def box(self, ap):
        name = ap.tensor.name
        kind, row = self.tinfo[name]
        aps = ap.ap
        off = ap.offset
        if kind == "dram":
            ext = sum((c - 1) * abs(s) for s, c in aps) + 1
            return name, (0, 1, off, off + ext)
        p0 = off // row
        f0 = off % row
        pc = aps[0][1]
        ext = sum((c - 1) * abs(s) for s, c in aps[1:]) + 1
        return name, (p0, p0 + pc, f0, f0 + ext)

    @staticmethod
    def _ov(a, b):
        return a[0] < b[1] and b[0] < a[1] and a[2] < b[3] and b[2] < a[3]

    @staticmethod
    def _contains(a, b):
        return a[0] <= b[0] and b[1] <= a[1] and a[2] <= b[2] and b[3] <= a[3]

    def _dep(self, i, j):
        if i == j:
            return
        a, b = self.ops[i], self.ops[j]
        if a.eng == "pe" and b.eng == "pe" and a.kind == "c" and b.kind == "c":
            return
        a.deps.add(b)

    def add(self, eng, fn, reads, writes, kind="c"):
        i = len(self.ops)
        op = Op(eng, kind, fn)
        op.idx = i
        self.ops.append(op)
        rkey = eng if kind == "c" else ("x", i)
        for ap in reads:
            name, bx = self.box(ap)
            lst = self.recs.setdefault(name, [])
            found = None
            for r in lst:
                if self._ov(r[0], bx):
                    if r[1] is not None:
                        self._dep(i, r[1])
                    if r[0] == bx:
                        found = r
            if found is None:
                found = [bx, None, {}]
                lst.append(found)
            found[2][rkey] = i
        for ap in writes:
            name, bx = self.box(ap)
            lst = self.recs.setdefault(name, [])
            keep = []
            for r in lst:
                if self._ov(r[0], bx):
                    if r[1] is not None:
                        self._dep(i, r[1])
                    for j in r[2].values():
                        self._dep(i, j)
                    if self._contains(bx, r[0]):
                        continue
                keep.append(r)
            keep.append([bx, i, {}])
            self.recs[name] = keep
        return i

    def mm(self, out, lhsT, rhs, start=True, stop=True):
        rd = [lhsT, rhs] + ([] if start else [out])
        return self.add("pe", lambda e: e.matmul(out, lhsT, rhs, start=start, stop=stop), rd, [out])

    def tr(self, out, in_, ident):
        return self.add("pe", lambda e: e.transpose(out, in_, ident), [in_, ident], [out])

    def act(self, out, in_, func, bias=None, scale=None):
        rd = [in_]
        kw = {}
        if bias is not None:
            kw["bias"] = bias
            if not isinstance(bias, (int, float)):
                rd.append(bias)
        if scale is not None:
            kw["scale"] = scale
            if not isinstance(scale, (int, float)):
                rd.append(scale)
        return self.add("act", lambda e: e.activation(out, in_, func, **kw), rd, [out])

    def tt(self, eng, out, a, b, op):
        return self.add(eng, lambda e: e.tensor_tensor(out, a, b, op), [a, b], [out])

    def ts(self, eng, out, a, s1, s2, op0, op1=None):
        rd = [a] + [s for s in (s1, s2) if s is not None and not isinstance(s, (int, float))]
        if op1 is None:
            return self.add(eng, lambda e: e.tensor_scalar(out, a, s1, None, op0), rd, [out])
        return self.add(eng, lambda e: e.tensor_scalar(out, a, s1, s2, op0, op1), rd, [out])

    def stt(self, out, in0, scalar, in1, op0, op1):
        rd = [in0, in1] + ([] if isinstance(scalar, (int, float)) else [scalar])
        return self.add("dve", lambda e: e.scalar_tensor_tensor(out, in0, scalar, in1, op0, op1), rd, [out])

    def copy(self, eng, out, in_):
        if eng == "act":
            return self.add("act", lambda e: e.copy(out, in_), [in_], [out])
        return self.add(eng, lambda e: e.tensor_copy(out, in_), [in_], [out])

    def recip(self, out, in_):
        return self.add("dve", lambda e: e.reciprocal(out, in_), [in_], [out])

    def memset(self, eng, ap, val):
        return self.add(eng, lambda e: e.memset(ap, val), [], [ap])

    def dma(self, q, out, in_, hoist=False):
        i = self.add(q, lambda e: e.dma_start(out=out, in_=in_), [in_], [out], kind="d")
        self.ops[i].hoist = hoist
        return i

    def hoist_ops(self):
        after = {}
        for op in self.ops:
            if op.hoist:
                t = max((d.idx for d in op.deps), default=-1)
                after.setdefault(t, []).append(op)
        new = list(after.get(-1, []))
        for op in self.ops:
            if not op.hoist:
                new.append(op)
            new.extend(after.get(op.idx, []))
        assert len(new) == len(self.ops)
        self.ops = new

    def generic(self, eng, fn, reads, writes, kind="c"):
        return self.add(eng, fn, reads, writes, kind=kind)

    def finalize(self, sems, qsems, ccsems, block):
        self.hoist_ops()
        ops = self.ops
        qn = {}
        ncc = 0
        for i, op in enumerate(ops):
            if op.kind == "d":
                lst = qn.setdefault(op.eng, [])
                n = len(lst)
                op.semkey = ("q", op.eng, n % NQ)
                op.semval = 16 * (n // NQ + 1)
                if n >= NQ:
                    op.deps.add(lst[n - NQ])
                lst.append(op)
            elif op.kind == "cc":
                op.semkey = ("cc", ncc)
                op.semval = 1
                ncc += 1
        for op in ops:
            for d in op.deps:
                d.sig = True
        cnts = {}
        for op in ops:
            if op.kind == "c" and op.sig:
                cnts[op.eng] = cnts.get(op.eng, 0) + 1
                op.cnt = cnts[op.eng]
                op.semkey = op.eng
                op.semval = op.cnt
        know = {}
        clocks = {}
        for i, op in enumerate(ops):
            kn = know.setdefault(op.eng, {})
            for pj in sorted(op.deps, key=lambda d: d.idx):
                if kn.get(pj.semkey, 0) < pj.semval:
                    op.waits.append((pj.semkey, pj.semval))
                ck = clocks.get(id(pj))
                if ck is not None:
                    for k, v in ck.items():
                        if kn.get(k, 0) < v:
                            kn[k] = v
                if kn.get(pj.semkey, 0) < pj.semval:
                    kn[pj.semkey] = pj.semval
            if op.kind != "c" or op.sig:
                ck = dict(kn)
                ck[op.semkey] = max(ck.get(op.semkey, 0), op.semval)
                clocks[id(op)] = ck
        final = {}
        for op in ops:
            if op.kind in ("d", "cc"):
                final[op.semkey] = max(final.get(op.semkey, 0), op.semval)

        def semof(key):
            if isinstance(key, str):
                return sems[key]
            if key[0] == "q":
                return qsems[key[1]][key[2]]
            return ccsems[key[1]]

        def emit(engname):
            def body(e):
                for op in ops:
                    if op.eng != engname:
                        continue
                    w = {}
                    for k, v in op.waits:
                        w[k] = max(w.get(k, 0), v)
                    for k, v in w.items():
                        e.wait_ge(semof(k), v)
                    inst = op.fn(e)
                    if op.kind == "d":
                        inst.then_inc(semof(op.semkey), 16)
                    elif op.kind == "cc":
                        inst.then_inc(semof(op.semkey))
                    elif op.sig:
                        inst.then_inc(sems[op.eng], 1)
                if engname == "sp":
                    for k, v in final.items():
                        e.wait_ge(semof(k), v)
            return body

        block.tensor(emit("pe"))
        block.scalar(emit("act"))
        block.vector(emit("dve"))
        block.gpsimd(emit("pool"))
        block.sync(emit("sp"))


def build_program():
    nc = bass.Bass("TRN2", target_bir_lowering=False)
    P = Prog(nc)

    def dram(name, shape, dt, kind):
        t = nc.dram_tensor(name, shape, dt, kind=kind) if kind else nc.dram_tensor(name, shape, dt)
        P.reg(name, "dram", 0)
        return t.ap()

    xp = dram("xp", [TP, D], F32, "ExternalInput")
    xs = dram("xs", [TS, D], F32, "ExternalInput")
    cckv = dram("cckv", [2, 512, 256], F32, "ExternalInput")
    ckpe = dram("ckpe", [2, 512, 64], F32, "ExternalInput")
    cT_d = dram("cT", [128, 8, 2], F32, "ExternalInput")
    normg_d = dram("normg", [128, 4, 8], F32, "ExternalInput")
    bmod_d = dram("bmod", [128, 4, 24], F32, "ExternalInput")
    qg_d = dram("qg", [128, 2, 4], F32, "ExternalInput")
    kvg_d = dram("kvg", [128, 2, 2], F32, "ExternalInput")
    vg_d = dram("vg", [128, 2, 16], F32, "ExternalInput")
    fg_d = dram("fg", [128, 8], F32, "ExternalInput")
    vb_d = dram("vb_bc", [128, 2, 2048], F32, "ExternalInput")
    bs_d = dram("bs_bc", [128, 2, 8, 128], F32, "ExternalInput")
    cos_d = dram("cos", [64, 512], F32, "ExternalInput")
    sin_d = dram("sin", [64, 512], F32, "ExternalInput")
    ident_d = dram("ident", [128, 128], F32, "ExternalInput")
    w_mod = dram("w_mod", [4, 1024, 3072], F32, "ExternalInput")
    a_w_in = dram("attn_w_in", [2, 1024, 1856], F32, "ExternalInput")
    a_w_uq = dram("attn_w_uq", [2, 512, 1536], F32, "ExternalInput")
    a_w_ukv = dram("attn_w_ukv", [2, 256, 2048], F32, "ExternalInput")
    a_w_o = dram("attn_w_o", [2, 1024, 1024], F32, "ExternalInput")
    m_w_in = dram("mlp_w_in", [2, 1024, 6144], F32, "ExternalInput")
    m_w_s = dram("mlp_w_s", [2, 8, 128, 128], F32, "ExternalInput")
    m_w_o = dram("mlp_w_o", [2, 2048, 1024], F32, "ExternalInput")
    yp = dram("yp", [TP, D], F32, "ExternalOutput")
    ys = dram("ys", [TS, D], F32, "ExternalOutput")
    nckv = dram("nckv", [4, 2, 256, 256], F32, "ExternalOutput")
    nkpe = dram("nkpe", [4, 2, 256, 64], F32, "ExternalOutput")
    agin = [dram("agin%d" % a, [320, 512], BF16, None) for a in range(2)]
    agout = [dram("agout%d" % a, [4 * 320, 512], BF16, None) for a in range(2)]

    import contextlib
    es = contextlib.ExitStack()

    def sb(name, shape, dt):
        t = es.enter_context(nc.sbuf_tensor(name, shape, dt))
        row = 1
        for s in shape[1:]:
            row *= s
        P.reg(name, "sb", row)
        return t

    ARN = 45056
    X = sb("X", [128, 8, T], F32)
    WR = sb("WR", [128, 4, 4096], BF16)
    AR = sb("AR", [128, ARN], BF16)
    A32 = sb("A32", [128, 3072], F32)
    TMP = sb("TMP", [128, 3, 512], F32)
    RSTD = sb("RSTD", [128, 2, 512], F32)
    SQ = sb("SQ", [128, 2, 512], BF16)
    SQ3 = sb("SQ3", [128, 4, 512], BF16)
    COS = sb("COS", [64, 512], F32)
    SIN = sb("SIN", [64, 512], F32)
    ONES = sb("ONES", [128, 128], BF16)
    IDF = sb("IDF", [128, 128], F32)
    IDB = sb("IDB", [128, 128], BF16)
    CT32 = sb("CT32", [128, 8, 2], F32)
    SC = sb("SC", [128, 8, 2], BF16)
    NORMG = sb("NORMG", [128, 4, 8], F32)
    BMOD = sb("BMOD", [128, 4, 24], F32)
    QG = sb("QG", [128, 2, 4], F32)
    KVG = sb("KVG", [128, 2, 2], F32)
    VG = sb("VG", [128, 2, 16], F32)
    FG = sb("FG", [128, 8], F32)
    MODV = sb("MODV", [128, 4, 24, 2], F32)
    MA = sb("MA", [128, 4, 8, 2], F32)
    BNS = sb("BNS", [128, 12, 24], F32)
    BNA = sb("BNA", [128, 12, 2], F32)
    RS1 = sb("RS1", [128, 12, 1], F32)
    NMR = sb("NMR", [128, 2, 1], F32)
    WSRAW = sb("WSRAW", [128, 2, 128], F32)

    PS = es.enter_context(nc.psum_tensor("PS", [128, 8, 512], F32))
    P.reg("PS", "ps", 4096)

    sems = {k: es.enter_context(nc.semaphore("s_" + k)) for k in ("pe", "act", "dve", "pool")}
    qsems = {q: [es.enter_context(nc.semaphore("q_%s_%d" % (q, i))) for i in range(NQ)] for q in ("sp", "pool")}
    ccsems = [es.enter_context(nc.semaphore("cc%d" % i)) for i in range(2)]

    state = {"slab": 0, "bank": 0, "tmp": 0, "sq": 0, "rstd": 0, "ev": 0, "sq3": 0}

    sl_busy = [False] * 4
    sl_rel = [0, 1, 2, 3]
    sl_tags = {}

    def release_tag(tag):
        if tag in sl_tags:
            i = sl_tags.pop(tag)
            sl_busy[i] = False
            state["slab"] += 1
            sl_rel[i] = 4 + state["slab"]

    def slab(kc, ncols, tag="s"):
        release_tag(tag)
        free = [i for i in range(4) if not sl_busy[i]]
        assert free, "weight ring exhausted"
        i = min(free, key=lambda j: sl_rel[j])
        sl_busy[i] = True
        sl_tags[tag] = i
        return WR[:, i, 0:kc * ncols].rearrange("p (k n) -> p k n", k=kc)

    def wload(dst, src2d, kc):
        P.dma("pool", dst, src2d.rearrange("(k p) n -> p k n", p=128), hoist=True)

    def bank(pool=None):
        pool = pool or (0, 1, 2, 3, 4, 5, 6, 7)
        b = pool[state["bank"] % len(pool)]
        state["bank"] += 1
        return b

    def odpair():
        state["od"] = state.get("od", 0) + 1
        return ((4, 6), (5, 7))[state["od"] % 2]

    def tmp32():
        i = state["tmp"] % 3
        state["tmp"] += 1
        return TMP[:, i, :]

    def sqt():
        i = state["sq"] % 2
        state["sq"] += 1
        return SQ[:, i, :]

    def rstdt():
        i = state["rstd"] % 2
        state["rstd"] += 1
        return RSTD[:, i, :]

    def evac_eng():
        state["ev"] += 1
        return "act" if state["ev"] % 2 else "dve"

    def evac_copy(out, in_, eng=None):
        eng = eng or evac_eng()
        P.copy(eng, out, in_)

    def av(off, n):
        return AR[:, off:off + n]

    def rstd_from_ps(ps_ap, n_feat, npart=128):
        r = rstdt()
        P.act(r[0:npart], ps_ap, AF.Sqrt, bias=EPSB[0:npart, :], scale=1.0 / n_feat)
        P.recip(r[0:npart], r[0:npart])
        return r

    EPSB = sb("EPSB", [128, 1], F32)

    P.memset("dve", EPSB[:], EPS)
    P.memset("dve", ONES[:], 1.0)
    for dst, src in ((IDF, ident_d), (CT32, cT_d), (NORMG, normg_d), (BMOD, bmod_d), (QG, qg_d),
                     (KVG, kvg_d), (VG, vg_d), (FG, fg_d), (COS, cos_d), (SIN, sin_d)):
        P.dma("sp", dst[:], src)
    P.copy("dve", IDB[:], IDF[:])
    P.act(SC[:], CT32[:], AF.Silu)

    def emit_mod_slab(l, sj, pool=None):
        s = slab(8, 512, "mod")
        wload(s, w_mod[l, :, sj * 512:(sj + 1) * 512], 8)
        b = bank(pool)
        for jj in range(4):
            for k in range(8):
                P.mm(PS[:, b, 2 * jj:2 * jj + 2], s[:, k, jj * 128:(jj + 1) * 128], SC[:, k, :],
                     start=(k == 0), stop=(k == 7))
        if pool is not None:
            for jj in (3, 2, 1, 0):
                P.act(MODV[:, l, 4 * sj + jj, :], PS[:, b, 2 * jj:2 * jj + 2], AF.Identity,
                      bias=BMOD[:, l, 4 * sj + jj:4 * sj + jj + 1])
        else:
            P.tt("dve", MODV[:, l, 4 * sj:4 * sj + 4, :], PS[:, b, 0:8].rearrange("p (j v) -> p j v", v=2),
                 BMOD[:, l, 4 * sj:4 * sj + 4].unsqueeze(2).to_broadcast([128, 4, 2]), ALU.add)
        release_tag("mod")
        if sj == 3:
            P.stt(MA[:, l, :, :], MODV[:, l, 8:16, :], 1.0,
                  NORMG[:, l, :].unsqueeze(2).to_broadcast([128, 8, 2]), ALU.add, ALU.mult)

    def emit_mod(l):
        for sj in range(6):
            emit_mod_slab(l, sj)

    def vsel(blk):
        return 0 if blk < 2 else 1

    def ssq_ps(src_fn, nchunks, blk_cols, npart=128):
        b = bank()
        n = blk_cols
        for c in range(nchunks):
            s = sqt()
            src = src_fn(c)
            if c % 2 == 0:
                P.act(s[:, 0:n], src, AF.Square)
            else:
                P.tt("dve", s[:, 0:n], src, src, ALU.mult)
            P.mm(PS[:, b, 0:n], ONES[:], s[:, 0:n], start=(c == 0), stop=(c == nchunks - 1))
        return PS[:, b, 0:n]

    H_OFF = 0
    H = av(H_OFF, 8 * T).rearrange("p (c t) -> p c t", c=8)

    SSQB = {0: 5, 1: 6, 2: 7}

    def x_update(l, dc, blk, b):
        cs = slice(blk * 512, (blk + 1) * 512)
        v = vsel(blk)
        P.stt(X[:, dc, cs], PS[:, b, :], MODV[:, l, 16 + dc, v:v + 1], X[:, dc, cs], ALU.mult, ALU.add)
        s = SQ3[:, state["sq3"] % 4, :]
        state["sq3"] += 1
        P.tt("pool", s, X[:, dc, cs], X[:, dc, cs], ALU.mult)
        pend.append((blk, dc, s))
        while len(pend) > 2:
            x_flush()

    pend = []

    def x_flush_item(item):
        blk, dc, s = item
        P.mm(PS[:, SSQB[blk], :], ONES[:], s, start=(dc == 0), stop=(dc == 7))

    def x_flush():
        x_flush_item(pend.pop(0))

    def flush_blk(blk):
        keep = []
        while pend:
            item = pend.pop(0)
            if item[0] == blk:
                x_flush_item(item)
            else:
                keep.append(item)
        pend.extend(keep)

    def emit_h_blk(l, blk):
        cs = slice(blk * 512, (blk + 1) * 512)
        v = vsel(blk)
        if l == 0:
            ps = ssq_ps(lambda c: X[:, c, cs], 8, 512)
        else:
            flush_blk(blk)
            ps = PS[:, SSQB[blk], :]
        r = rstd_from_ps(ps, 1024.0)

        def one(c):
            t = tmp32()
            P.stt(t, X[:, c, cs], MA[:, l, c, v:v + 1], r, ALU.mult, ALU.mult)
            P.act(H[:, c, cs], t, AF.Identity, bias=MODV[:, l, c, v:v + 1])
        for c in range(8):
            dq.append(lambda c=c: one(c))

    dq = []

    def dq_step():
        if dq:
            dq.pop(0)()

    def dq_flush():
        while dq:
            dq.pop(0)()

    def emit_h(l):
        for blk in (2, 0, 1):
            emit_h_blk(l, blk)

    def after_update_blk(l, blk):
        if l + 1 < 4:
            emit_h_blk(l + 1, blk)
        else:
            emit_final_blk(blk)

    def emit_load_x():
        r0 = {}

        def load_tile(tt_):
            st = A32[:, (tt_ % 3) * 1024:(tt_ % 3 + 1) * 1024]
            if tt_ < 8:
                src = xp[tt_ * 128:(tt_ + 1) * 128, :]
            else:
                src = xs[(tt_ - 8) * 128:(tt_ - 7) * 128, :]
            P.dma("sp", st, src)
            for half in range(2):
                b = bank()
                for cc in range(4):
                    c = half * 4 + cc
                    P.tr(PS[:, b, cc * 128:(cc + 1) * 128], st[:, c * 128:(c + 1) * 128], IDF[:])
                evac_copy(X[:, half * 4:half * 4 + 4, tt_ * 128:(tt_ + 1) * 128],
                          PS[:, b, :].rearrange("p (c t) -> p c t", c=4))

        def stats(blk, rt):
            cs = slice(blk * 512, (blk + 1) * 512)
            ps = ssq_ps(lambda c: X[:, c, cs], 8, 512)
            P.act(rt, ps, AF.Sqrt, bias=EPSB[:], scale=1.0 / 1024.0)
            P.recip(rt, rt)
            r0[blk] = rt

        emit_mod_slab(0, 0)
        load_tile(8)
        load_tile(9)
        emit_mod_slab(0, 1)
        load_tile(10)
        load_tile(11)
        stats(2, rstdt())
        emit_mod_slab(0, 2)
        load_tile(0)
        load_tile(1)
        emit_mod_slab(0, 3)
        load_tile(2)
        load_tile(3)
        for tt_ in (4, 5, 6, 7):
            load_tile(tt_)
        stats(0, A32[:, 0:512])
        stats(1, A32[:, 512:1024])
        emit_cache_part(0)
        for blk in (2, 0, 1):
            cs = slice(blk * 512, (blk + 1) * 512)
            v = vsel(blk)
            for c in range(8):
                t = tmp32()
                P.stt(t, X[:, c, cs], MA[:, 0, c, v:v + 1], r0[blk], ALU.mult, ALU.mult)
                P.act(H[:, c, cs], t, AF.Identity, bias=MODV[:, 0, c, v:v + 1])

    def emit_final_blk(blk):
        flush_blk(blk)
        cs = slice(blk * 512, (blk + 1) * 512)
        r = rstd_from_ps(PS[:, SSQB[blk], :], 1024.0)
        for c in range(8):
            dq.append(lambda c=c: P.stt(X[:, c, cs], X[:, c, cs], FG[:, c:c + 1], r, ALU.mult, ALU.mult))

        def out_tile(tt_):
            st = A32[:, (tt_ % 3) * 1024:(tt_ % 3 + 1) * 1024]
            for half in range(2):
                b = bank((0, 1, 2, 3, 4))
                for cc in range(4):
                    c = half * 4 + cc
                    P.tr(PS[:, b, cc * 128:(cc + 1) * 128], X[:, c, tt_ * 128:(tt_ + 1) * 128], IDF[:])
                evac_copy(st[:, half * 512:(half + 1) * 512], PS[:, b, :])
            if tt_ < 8:
                dst = yp[tt_ * 128:(tt_ + 1) * 128, :]
            else:
                dst = ys[(tt_ - 8) * 128:(tt_ - 7) * 128, :]
            P.dma("sp", dst, st)
        for t4 in range(4):
            dq.append(lambda t4=t4: out_tile(blk * 4 + t4))

    GZ = av(12288, 8 * T).rearrange("p (c t) -> p c t", c=8)
    CQN = av(24576, 4 * T).rearrange("p (c t) -> p c t", c=4)
    CKVN = av(30720, 2 * T).rearrange("p (c t) -> p c t", c=2)
    KPEB = av(33792, T)
    KVALL = av(35328, 2 * NKEY_S).rearrange("p (c t) -> p c t", c=2)
    KPEALL = av(40448, NKEY_S)
    WUQSW = av(43008, 2048).rearrange("p (k n) -> p k n", k=4)
    QN = av(0, T)
    QR = av(1536, T)
    KTP = av(3072, 1024)
    KTS = av(4096, NKEY_S)
    VV = av(6656, 28 * 128).rearrange("p (t d) -> p t d", t=28)
    PT = av(10240, 2048).rearrange("p (i n) -> p i n", i=4)
    ACCD = A32[:, 0:512]
    ACCP = A32[:, 512:1024]
    OUTKV = A32[:, 0:1280].rearrange("p (t f) -> p t f", t=4)
    CST = A32[:, 1280:2560].rearrange("p (t f) -> p t f", t=4)

    def emit_cache_part(a):
        P.dma("sp", CST[:, :, 0:256], cckv[a].rearrange("(t p) f -> p t f", p=128), hoist=True)
        P.dma("sp", CST[:, :, 256:320], ckpe[a].rearrange("(t p) f -> p t f", p=128), hoist=True)
        for j in range(2):
            b = bank((0, 1, 2, 3))
            for t4 in range(4):
                P.tr(PS[:, b, t4 * 128:(t4 + 1) * 128], CST[:, t4, j * 128:(j + 1) * 128], IDF[:])
            evac_copy(KVALL[:, j, 0:512], PS[:, b, :])
        b = bank((0, 1, 2, 3))
        for t4 in range(4):
            P.tr(PS[0:64, b, t4 * 128:(t4 + 1) * 128], CST[:, t4, 256:320], IDF[:])
        evac_copy(KPEALL[0:64, 0:512], PS[0:64, b, :])

    def emit_attn(a, l):
        blks = (2, 0, 1)
        P.memset("pool", KPEB[64:128, :], 0.0)
        P.memset("pool", KPEALL[64:128, :], 0.0)
        if l != 0:
            emit_cache_part(a)
        s0 = slab(8, 512, "q")
        wload(s0, a_w_in[a, :, 0:512], 8)
        s1 = slab(8, 384, "kv")
        wload(s1[:, :, 0:320], a_w_in[a, :, 512:832], 8)
        zsl = []
        for zs in range(2):
            s2 = slab(8, 512, "z%d" % zs)
            wload(s2, a_w_in[a, :, 832 + zs * 512:832 + (zs + 1) * 512], 8)
            zsl.append(s2)
        zq = [(zs, j, blk) for zs in range(2) for j in range(4) for blk in blks]

        def emit_z(n):
            for _ in range(n):
                if not zq:
                    return
                zs, j, blk = zq.pop(0)
                if zs == 1 and "z0" in sl_tags:
                    release_tag("z0")
                cs_ = slice(blk * 512, (blk + 1) * 512)
                b_ = bank((4, 5, 6, 7))
                for k in range(8):
                    P.mm(PS[:, b_, :], zsl[zs][:, k, j * 128:(j + 1) * 128], H[:, k, cs_], start=(k == 0), stop=(k == 7))
                P.act(GZ[:, zs * 4 + j, cs_], PS[:, b_, :], AF.Silu)

        for blk in blks:
            cs = slice(blk * 512, (blk + 1) * 512)
            bq = [bank((0, 1, 2, 3)) for _ in range(4)]
            for j in range(4):
                for k in range(8):
                    P.mm(PS[:, bq[j], :], s0[:, k, j * 128:(j + 1) * 128], H[:, k, cs], start=(k == 0), stop=(k == 7))
            bs_ = bank((4, 5, 6, 7))
            for j in range(4):
                s = sqt()
                P.act(s, PS[:, bq[j], :], AF.Square)
                P.mm(PS[:, bs_, :], ONES[:], s, start=(j == 0), stop=(j == 3))
            r = rstd_from_ps(PS[:, bs_, :], 512.0)
            for j in range(4):
                P.stt(CQN[:, j, cs], PS[:, bq[j], :], QG[:, a, j:j + 1], r, ALU.mult, ALU.mult)
            emit_z(4)
        release_tag("q")
        kv4 = s1[:, :, 256:320].rearrange("p k (x h i) -> p k x h i", x=2, h=2)
        sw4 = s1[:, :, 320:384].rearrange("p k (x h i) -> p k x h i", x=2, h=2)
        for x_ in range(2):
            P.ts("dve", sw4[:, :, x_, 0, :], kv4[:, :, x_, 1, :], -1.0, None, ALU.mult)
            P.copy("dve", sw4[:, :, x_, 1, :], kv4[:, :, x_, 0, :])
        for blk in blks:
            cs = slice(blk * 512, (blk + 1) * 512)
            bk = [bank((0, 1, 2, 3)) for _ in range(2)]
            for j in range(2):
                for k in range(8):
                    P.mm(PS[:, bk[j], :], s1[:, k, j * 128:(j + 1) * 128], H[:, k, cs], start=(k == 0), stop=(k == 7))
            bs_ = bank((4, 5, 6, 7))
            for j in range(2):
                s = sqt()
                P.act(s, PS[:, bk[j], :], AF.Square)
                P.mm(PS[:, bs_, :], ONES[:], s, start=(j == 0), stop=(j == 1))
            r = rstd_from_ps(PS[:, bs_, :], 256.0)
            bp = bank((4, 5, 6, 7))
            for k in range(8):
                P.mm(PS[0:64, bp, :], s1[:, k, 256:320], H[:, k, cs], start=(k == 0), stop=(k == 7))
            if blk == 2:
                bp2 = bank((4, 5, 6, 7))
                for k in range(8):
                    P.mm(PS[0:64, bp2, :], s1[:, k, 320:384], H[:, k, cs], start=(k == 0), stop=(k == 7))
                t1 = tmp32()
                t2 = tmp32()
                P.tt("dve", t1[0:64], PS[0:64, bp, :], COS[:], ALU.mult)
                P.tt("dve", t2[0:64], PS[0:64, bp2, :], SIN[:], ALU.mult)
                P.tt("dve", KPEB[0:64, cs], t1[0:64], t2[0:64], ALU.add)
                for j in range(2):
                    P.stt(CKVN[:, j, cs], PS[:, bk[j], :], KVG[:, a, j:j + 1], r, ALU.mult, ALU.mult)
                for j in range(2):
                    P.dma("sp", agin[a][j * 128:(j + 1) * 128, :], CKVN[:, j, cs])
                P.dma("sp", agin[a][256:320, :], KPEB[0:64, cs])
                P.generic("pool", lambda e, a=a: e.collective_compute(
                    "AllGather", ALU.bypass, replica_groups=[[0, 1, 2, 3], [4, 5, 6, 7]],
                    ins=[agin[a].opt()], outs=[agout[a].opt()]), [agin[a]], [agout[a]], kind="cc")
            else:
                evac_copy(KPEB[0:64, cs], PS[0:64, bp, :], "act")
                tk = tmp32()
                evac_copy(tk[0:64], PS[0:64, bp, :], "dve")
                tcs = []
                for j in range(2):
                    tc = tmp32()
                    P.stt(tc, PS[:, bk[j], :], KVG[:, a, j:j + 1], r, ALU.mult, ALU.mult)
                    P.copy("act", CKVN[:, j, cs], tc)
                    tcs.append(tc)
                emit_z(3)
                for j in range(2):
                    tc = tcs[j]
                    b = bank((0, 1, 2, 3))
                    for t4 in range(4):
                        P.tr(PS[:, b, t4 * 128:(t4 + 1) * 128], tc[:, t4 * 128:(t4 + 1) * 128], IDF[:])
                    evac_copy(OUTKV[:, :, j * 128:(j + 1) * 128], PS[:, b, :].rearrange("p (t f) -> p t f", t=4))
                b = bank((0, 1, 2, 3))
                for t4 in range(4):
                    P.tr(PS[:, b, t4 * 64:(t4 + 1) * 64], tk[0:64, t4 * 128:(t4 + 1) * 128], IDF[0:64, 0:64])
                evac_copy(OUTKV[:, :, 256:320], PS[:, b, 0:256].rearrange("p (t f) -> p t f", t=4))
                for sl in range(2):
                    seq = blk * 2 + sl
                    P.dma("sp", nckv[seq, a].rearrange("(t p) f -> p t f", p=128), OUTKV[:, 2 * sl:2 * sl + 2, 0:256])
                    P.dma("sp", nkpe[seq, a].rearrange("(t p) f -> p t f", p=128), OUTKV[:, 2 * sl:2 * sl + 2, 256:320])
            if blk == 2:
                emit_z(3)
        release_tag("kv")
        emit_z(len(zq))
        release_tag("z0")
        release_tag("z1")
        agv = agout[a].rearrange("(r f) t -> f r t", f=320)
        for j in range(2):
            P.dma("sp", KVALL[:, j, 512:NKEY_S].rearrange("p (r t) -> p r t", r=4),
                  agv[j * 128:(j + 1) * 128, :, :])
        P.dma("sp", KPEALL[0:64, 512:NKEY_S].rearrange("p (r t) -> p r t", r=4), agv[256:320, :, :])
        P.memset("pool", QR[64:128, :], 0.0)
        wkv = slab(2, 2048, "wkv")
        wload(wkv[:, :, 0:1024], a_w_ukv[a, :, 0:1024], 2)
        wload(wkv[:, :, 1024:2048], a_w_ukv[a, :, 1024:2048], 2)
        wq = None
        for h in range(8):
            if l + 1 < 4 and h < 6:
                emit_mod_slab(l + 1, h, (0, 1, 2, 3))
            if l == 0 and h >= 6:
                emit_mod_slab(0, h - 2, (0, 1, 2, 3))
            if h == 0 or h == 3:
                g4 = 0 if h == 0 else 1
                wq_new = slab(4, 768, "wq%d" % g4)
                wload(wq_new, a_w_uq[a, :, g4 * 768:(g4 + 1) * 768], 4)

                def build_sw(wq_new=wq_new, g4=g4):
                    for hh in range(4):
                        src = wq_new[:, :, hh * 192 + 128:hh * 192 + 192].rearrange("p k (x h i) -> p k x h i", x=2, h=2)
                        dst = WUQSW[:, :, (g4 * 4 + hh) * 64:(g4 * 4 + hh + 1) * 64].rearrange(
                            "p k (x h i) -> p k x h i", x=2, h=2)
                        for x_ in range(2):
                            P.ts("dve", dst[:, :, x_, 0, :], src[:, :, x_, 1, :], -1.0, None, ALU.mult)
                            P.copy("dve", dst[:, :, x_, 1, :], src[:, :, x_, 0, :])
                if h == 0:
                    build_sw()
                    wq = wq_new
            if h == 4:
                release_tag("wq0")
                wq = wq_new
            hq = h % 4
            for blk in (0, 1, 2):
                cs = slice(blk * 512, (blk + 1) * 512)
                b = bank((0, 1, 2, 3))
                for k in range(4):
                    P.mm(PS[:, b, :], wq[:, k, hq * 192:hq * 192 + 128], CQN[:, k, cs], start=(k == 0), stop=(k == 3))
                evac_copy(QN[:, cs], PS[:, b, :], "act")
                b = bank((0, 1, 2, 3))
                for k in range(4):
                    P.mm(PS[0:64, b, :], wq[:, k, hq * 192 + 128:hq * 192 + 192], CQN[:, k, cs],
                         start=(k == 0), stop=(k == 3))
                if blk == 2:
                    b2 = bank((0, 1, 2, 3))
                    for k in range(4):
                        P.mm(PS[0:64, b2, :], WUQSW[:, k, h * 64:(h + 1) * 64], CQN[:, k, cs],
                             start=(k == 0), stop=(k == 3))
                    t1 = tmp32()
                    t2 = tmp32()
                    P.tt("dve", t1[0:64], PS[0:64, b, :], COS[:], ALU.mult)
                    P.tt("dve", t2[0:64], PS[0:64, b2, :], SIN[:], ALU.mult)
                    P.tt("pool", QR[0:64, cs], t1[0:64], t2[0:64], ALU.add)
                else:
                    evac_copy(QR[0:64, cs], PS[0:64, b, :], "act")
            for cb in range(2):
                b = bank((0, 1, 2, 3))
                for k in range(2):
                    P.mm(PS[:, b, :], wkv[:, k, h * 256:h * 256 + 128], CKVN[:, k, cb * 512:(cb + 1) * 512],
                         start=(k == 0), stop=(k == 1))
                evac_copy(KTP[:, cb * 512:(cb + 1) * 512], PS[:, b, :], "act")
            for cb in range(5):
                b = bank((0, 1, 2, 3))
                for k in range(2):
                    P.mm(PS[:, b, :], wkv[:, k, h * 256:h * 256 + 128], KVALL[:, k, cb * 512:(cb + 1) * 512],
                         start=(k == 0), stop=(k == 1))
                evac_copy(KTS[:, cb * 512:(cb + 1) * 512], PS[:, b, :], "act")
            for g in range(7):
                b = bank((0, 1, 2, 3))
                for t4 in range(4):
                    ti = g * 4 + t4
                    for k in range(2):
                        if ti < 8:
                            lt = CKVN[:, k, ti * 128:(ti + 1) * 128]
                        else:
                            lt = KVALL[:, k, (ti - 8) * 128:(ti - 7) * 128]
                        P.mm(PS[:, b, t4 * 128:(t4 + 1) * 128], lt, wkv[:, k, h * 256 + 128:h * 256 + 256],
                             start=(k == 0), stop=(k == 1))
                evac_copy(VV[:, g * 4:(g + 1) * 4, :], PS[:, b, :].rearrange("p (t d) -> p t d", t=4), "dve")
            cs2 = slice(1024, 1536)
            units = []
            bo_p, bd_p = odpair()
            for sl in range(2):
                units.append(("p", sl, bo_p, bd_p))
            bo_s, bd_s = odpair()
            for kt in range(20):
                units.append(("s", kt, bo_s, bd_s))
            bo_p, bd_p = odpair()
            for sl in range(2):
                units.append(("p", 2 + sl, bo_p, bd_p))
            LA = 3
            nun = len(units)
            pts = {}
            for i in range(nun + LA):
                if i < nun:
                    kind, idx, bo, bd = units[i]
                    b = bank((0, 1, 2, 3))
                    if kind == "s":
                        kt = idx
                        P.mm(PS[:, b, :], KTS[:, kt * 128:(kt + 1) * 128], QN[:, cs2], start=True, stop=False)
                        P.mm(PS[:, b, :], KPEALL[:, kt * 128:(kt + 1) * 128], QR[:, cs2], start=False, stop=True)
                    else:
                        seq = idx
                        qs = slice(seq * 256, (seq + 1) * 256)
                        for kt in range(2):
                            ks = slice(seq * 256 + kt * 128, seq * 256 + (kt + 1) * 128)
                            P.mm(PS[:, b, kt * 256:(kt + 1) * 256], KTP[:, ks], QN[:, qs], start=True, stop=False)
                            P.mm(PS[:, b, kt * 256:(kt + 1) * 256], KPEB[:, ks], QR[:, qs], start=False, stop=True)
                    pt = PT[:, i % 4, :]
                    P.act(pt, PS[:, b, :], AF.Exp, scale=SM_SCALE)
                    pts[i] = pt
                j = i - LA
                if j >= 0:
                    kind, idx, bo, bd = units[j]
                    pt = pts.pop(j)
                    if kind == "s":
                        kt = idx
                        P.mm(PS[:, bo, :], VV[:, 8 + kt, :], pt, start=(kt == 0), stop=(kt == 19))
                        P.mm(PS[:, bd, :], ONES[:], pt, start=(kt == 0), stop=(kt == 19))
                        if kt == 19:
                            emit_og(h, 2, bo, bd)
                    else:
                        seq = idx
                        sl = seq % 2
                        for kt in range(2):
                            P.mm(PS[:, bo, sl * 256:(sl + 1) * 256], VV[:, seq * 2 + kt, :], pt[:, kt * 256:(kt + 1) * 256],
                                 start=(kt == 0), stop=(kt == 1))
                        for kt in range(2):
                            P.mm(PS[:, bd, sl * 256:(sl + 1) * 256], ONES[:], pt[:, kt * 256:(kt + 1) * 256],
                                 start=(kt == 0), stop=(kt == 1))
                        if sl == 1:
                            emit_og(h, seq // 2, bo, bd)
            if h == 3:
                build_sw()
        release_tag("wkv")
        release_tag("wq1")
        sos = []
        for os_ in range(2):
            so = slab(8, 512, "wo%d" % os_)
            wload(so, a_w_o[a, :, os_ * 512:(os_ + 1) * 512], 8)
            sos.append(so)
        prev = None
        for blk in blks:
            cs = slice(blk * 512, (blk + 1) * 512)
            for dc in range(8):
                so = sos[dc // 4]
                j = dc % 4
                b = bank((0, 1, 2, 3, 4))
                for k in range(8):
                    P.mm(PS[:, b, :], so[:, k, j * 128:(j + 1) * 128], GZ[:, k, cs], start=(k == 0), stop=(k == 7))
                x_update(l, dc, blk, b)
                if dc == 1 and prev is not None:
                    after_update_blk(l, prev)
                dq_step()
            prev = blk
        emit_mlp_setup(l // 2)
        after_update_blk(l, prev)
        dq_flush()
        release_tag("wo0")
        release_tag("wo1")

    def emit_og(h, blk, bo, bd):
        cs = slice(blk * 512, (blk + 1) * 512)
        rd = tmp32()
        P.act(rd, PS[:, bd, :], AF.Ln)
        P.act(rd, rd, AF.Exp, scale=-1.0)
        t = tmp32()
        P.tt("dve", t, PS[:, bo, :], rd, ALU.mult)
        P.tt("pool", GZ[:, h, cs], t, GZ[:, h, cs], ALU.mult)

    VH = av(12288, 12 * 2048).rearrange("p (t f) -> p t f", t=12)
    WST = av(36864, 1024).rearrange("p (g q) -> p g q", g=8)
    VBB = av(37888, 2048)
    UT = av(39936, 1536).rearrange("p (i n) -> p i n", i=3)
    GT = av(41472, 1536).rearrange("p (i n) -> p i n", i=3)
    CC = A32[:, 0:2048].rearrange("p (f q) -> p f q", f=16)
    BSB = A32[:, 2048:3072].rearrange("p (g q) -> p g q", g=8)

    def ln_group(t0):
        for ti in range(t0, t0 + 4):
            P.generic("dve", lambda e, ti=ti: e.bn_aggr(BNA[:, ti, :], BNS[:, ti, :]), [BNS[:, ti, :]], [BNA[:, ti, :]])
        P.act(RS1[:, t0:t0 + 4, :], BNA[:, t0:t0 + 4, 1:2], AF.Sqrt, bias=EPSB[:], scale=1.0)
        P.recip(RS1[:, t0:t0 + 4, :], RS1[:, t0:t0 + 4, :])
        for ti in range(t0, t0 + 4):
            P.ts("dve", VH[:, ti, :], VH[:, ti, :], BNA[:, ti, 0:1], RS1[:, ti, :], ALU.subtract, ALU.mult)

    def emit_mlp_setup(m):
        bp5 = (0, 1, 2, 3, 4)
        P.dma("pool", VBB, vb_d[:, m, :], hoist=True)
        P.dma("sp", BSB, bs_d[:, m, :, :], hoist=True)
        WSRAW8 = A32[:, 0:1024].rearrange("p (g q) -> p g q", g=8)
        P.dma("sp", WSRAW8, m_w_s[m].rearrange("g p q -> p g q"), hoist=True)
        for g2 in range(2):
            b = bank(bp5)
            for gg in range(4):
                P.tr(PS[:, b, gg * 128:(gg + 1) * 128], WSRAW8[:, g2 * 4 + gg, :], IDF[:])
            evac_copy(WST[:, g2 * 4:(g2 + 1) * 4, :], PS[:, b, :].rearrange("p (g q) -> p g q", g=4))
        for f4 in range(4):
            b = bank(bp5)
            for ff in range(4):
                fc = f4 * 4 + ff
                P.mm(PS[:, b, ff * 128:(ff + 1) * 128], VBB[:, fc * 128:(fc + 1) * 128], WST[:, fc // 2, :],
                     start=True, stop=True)
            for g1 in (1, 0):
                g = f4 * 2 + g1
                P.tt("dve", CC[:, 2 * g:2 * g + 2, :],
                     PS[:, b, g1 * 256:(g1 + 1) * 256].rearrange("p (f q) -> p f q", f=2),
                     BSB[:, g, :].unsqueeze(1).to_broadcast([128, 2, 128]), ALU.add)

    def emit_mlp(m, l):
        blks = (2, 0, 1)
        for vs in range(4):
            s = slab(8, 512)
            wload(s, m_w_in[m, :, 2048 + vs * 512:2048 + (vs + 1) * 512], 8)
            for ti in range(12):
                b = bank()
                for k in range(8):
                    P.mm(PS[:, b, :], H[:, k, ti * 128:(ti + 1) * 128], s[:, k, :], start=(k == 0), stop=(k == 7))
                P.act(VH[:, ti, vs * 512:(vs + 1) * 512], PS[:, b, :], AF.Gelu_apprx_tanh)
                P.generic("dve", lambda e, ti=ti, q4=vs: e.bn_stats(
                    BNS[:, ti, q4 * 6:(q4 + 1) * 6], VH[:, ti, q4 * 512:(q4 + 1) * 512]),
                    [VH[:, ti, vs * 512:(vs + 1) * 512]], [BNS[:, ti, vs * 6:(vs + 1) * 6]])
                if vs == 3 and ti % 4 == 3:
                    ln_group(ti - 3)
        release_tag("s")
        su = sz = None
        for fc in range(16):
            if l + 1 < 4 and fc % 2 == 1 and fc < 12:
                emit_mod_slab(l + 1, fc // 2)
            if fc % 4 == 0:
                su = slab(8, 512, "su")
                wload(su, m_w_in[m, :, fc * 128:fc * 128 + 512], 8)
                sz = slab(8, 512, "sz")
                wload(sz, m_w_in[m, :, 4096 + fc * 128:4096 + fc * 128 + 512], 8)
            j = fc % 4
            g = fc // 2
            for blk in blks:
                cs = slice(blk * 512, (blk + 1) * 512)
                b = bank()
                for k in range(8):
                    P.mm(PS[:, b, :], su[:, k, j * 128:(j + 1) * 128], H[:, k, cs], start=(k == 0), stop=(k == 7))
                P.act(UT[:, blk, :], PS[:, b, :], AF.Gelu_apprx_tanh)
            for blk in blks:
                cs = slice(blk * 512, (blk + 1) * 512)
                b = bank()
                for k in range(8):
                    P.mm(PS[:, b, :], sz[:, k, j * 128:(j + 1) * 128], H[:, k, cs], start=(k == 0), stop=(k == 7))
                P.act(GT[:, blk, :], PS[:, b, :], AF.Silu)
                P.tt("pool", UT[:, blk, :], UT[:, blk, :], GT[:, blk, :], ALU.mult)
            for blk in blks:
                b = bank()
                for t4 in range(4):
                    ti = blk * 4 + t4
                    P.mm(PS[:, b, t4 * 128:(t4 + 1) * 128], VH[:, ti, fc * 128:(fc + 1) * 128], WST[:, g, :],
                         start=True, stop=True)
                t = tmp32()
                P.stt(t.rearrange("p (t q) -> p t q", t=4), PS[:, b, :].rearrange("p (t q) -> p t q", t=4),
                      VG[:, m, fc:fc + 1], CC[:, fc, :].unsqueeze(1).to_broadcast([128, 4, 128]), ALU.mult, ALU.add)
                P.tt("dve", VH[:, blk * 4:(blk + 1) * 4, fc * 128:(fc + 1) * 128],
                     t.rearrange("p (t q) -> p t q", t=4), UT[:, blk, :].rearrange("p (t q) -> p t q", t=4), ALU.mult)
        release_tag("su")
        release_tag("sz")
        sos = []
        for os_ in range(4):
            so = slab(16, 256, "wo%d" % os_)
            wload(so, m_w_o[m, :, os_ * 256:(os_ + 1) * 256], 16)
            sos.append(so)
        prev = None
        for blk in blks:
            cs = slice(blk * 512, (blk + 1) * 512)
            for dc in range(8):
                so = sos[dc // 2]
                j = dc % 2
                b = bank((0, 1, 2, 3, 4))
                for k in range(16):
                    P.mm(PS[:, b, :].rearrange("p (t q) -> p t q", t=4), so[:, k, j * 128:(j + 1) * 128],
                         VH[:, blk * 4:(blk + 1) * 4, k * 128:(k + 1) * 128], start=(k == 0), stop=(k == 15))
                x_update(l, dc, blk, b)
                if dc == 1 and prev is not None:
                    after_update_blk(l, prev)
                dq_step()
            prev = blk
        after_update_blk(l, prev)
        dq_flush()
        for os_ in range(4):
            release_tag("wo%d" % os_)

    emit_load_x()
    for l in range(4):
        if l % 2 == 0:
            emit_attn(l // 2, l)
        else:
            emit_mlp(l // 2, l)

    with nc.Block() as block:
        P.finalize(sems, qsems, ccsems, block)
    es.close()
    return nc


def _rope_tables(core):
    r = core % 4
    t = np.arange(r * 512, (r + 1) * 512)
    row = (t // 64).astype(np.float32)
    col = (t % 64).astype(np.float32)
    inv = (1.0 / (np.float32(10000.0) ** (np.arange(0, 32, 2, dtype=np.float32) / np.float32(32)))).astype(np.float32)
    ang = np.concatenate([row[:, None] * inv, col[:, None] * inv], axis=-1).astype(np.float32)
    cos = np.cos(ang).astype(np.float32)
    sin = np.sin(ang).astype(np.float32)
    idx = np.array([(d // 32) * 16 + (d % 16) for d in range(64)])
    return np.ascontiguousarray(cos[:, idx].T), np.ascontiguousarray(sin[:, idx].T)


def _fm(v, nchunk):
    v = np.asarray(v, np.float32)
    lead = v.shape[:-1]
    v = v.reshape(lead + (nchunk, 128))
    v = np.moveaxis(v, -1, 0)
    return np.ascontiguousarray(v)


_NC_CACHE = {}


def kernel(x_prompt, x_sample, cache_ckv, cache_kpe, c, c_ctx, norm_g, w_mod, b_mod,
           attn_w_in, attn_q_norm_g, attn_kv_norm_g, attn_w_uq, attn_w_ukv, attn_w_o,
           mlp_w_in, mlp_v_norm_g, mlp_v_norm_b, mlp_w_s, mlp_b_s, mlp_w_o, final_norm_g):
    f = lambda a: np.ascontiguousarray(np.asarray(a, np.float32))
    x_prompt, x_sample, cache_ckv, cache_kpe = f(x_prompt), f(x_sample), f(cache_ckv), f(cache_kpe)
    c, c_ctx = f(c), f(c_ctx)
    if "nc" not in _NC_CACHE:
        _NC_CACHE["nc"] = build_program()
    nc = _NC_CACHE["nc"]
    shared = {
        "normg": _fm(norm_g, 8), "bmod": _fm(b_mod, 24), "qg": _fm(attn_q_norm_g, 4),
        "kvg": _fm(attn_kv_norm_g, 2), "vg": _fm(mlp_v_norm_g, 16), "fg": _fm(final_norm_g, 8),
        "vb_bc": np.ascontiguousarray(np.broadcast_to(f(mlp_v_norm_b)[None], (128, 2, 2048))),
        "bs_bc": np.ascontiguousarray(np.broadcast_to(f(mlp_b_s)[None], (128, 2, 8, 128))),
        "ident": np.eye(128, dtype=np.float32),
        "w_mod": f(w_mod), "attn_w_in": f(attn_w_in), "attn_w_uq": f(attn_w_uq), "attn_w_ukv": f(attn_w_ukv),
        "attn_w_o": f(attn_w_o), "mlp_w_in": f(mlp_w_in), "mlp_w_s": f(mlp_w_s), "mlp_w_o": f(mlp_w_o),
    }
    in_maps = []
    for i in range(NCORES):
        b = i // 4
        r = i % 4
        cos, sin = _rope_tables(i)
        cT = np.stack([c_ctx, c[b]], axis=-1).reshape(8, 128, 2).transpose(1, 0, 2)
        m = dict(shared)
        m.update({
            "xp": np.ascontiguousarray(x_prompt[4 * i:4 * i + 4].reshape(TP, D)),
            "xs": np.ascontiguousarray(x_sample[b, r * 512:(r + 1) * 512]),
            "cckv": np.ascontiguousarray(cache_ckv[b]),
            "ckpe": np.ascontiguousarray(cache_kpe[b]),
            "cT": np.ascontiguousarray(cT),
            "cos": cos, "sin": sin,
        })
        in_maps.append(m)
    res = run_bass_kernel_spmd(nc, in_maps, core_ids=list(range(NCORES)))
    rs = res.results
    y_prompt = np.concatenate([np.asarray(rs[i]["yp"]).reshape(4, 256, D) for i in range(NCORES)], axis=0)
    y_sample = np.stack([np.concatenate([np.asarray(rs[b * 4 + r]["ys"]) for r in range(4)], axis=0) for b in range(2)], axis=0)
    new_ckv = np.concatenate([np.asarray(rs[i]["nckv"]) for i in range(NCORES)], axis=0)
    new_kpe = np.concatenate([np.asarray(rs[i]["nkpe"]) for i in range(NCORES)], axis=0)
    return (y_prompt.astype(np.float32), y_sample.astype(np.float32),
            new_ckv.astype(np.float32), new_kpe.astype(np.float32))
```

```python
import numpy as np
import concourse.bass as bass
import concourse.mybir as mybir
from concourse.bass_utils import run_bass_kernel_spmd

F32 = mybir.dt.float32
BF16 = mybir.dt.bfloat16
AF = mybir.ActivationFunctionType
ALU = mybir.AluOpType

NCORES = 8
D = 1024
T = 1536
TP = 1024
TS = 512
NBLK = 3
EPS = 1e-6
NKEY_S = 2560
SM_SCALE = 192.0 ** -0.5
NQ = 16


class Op:
    __slots__ = ("eng", "kind", "fn", "deps", "sig", "cnt", "semkey", "semval", "waits", "hoist", "idx")

    def __init__(self, eng, kind, fn):
        self.hoist = False
        self.idx = 0
        self.eng = eng
        self.kind = kind
        self.fn = fn
        self.deps = set()
        self.sig = False
        self.cnt = 0
        self.semkey = None
        self.semval = 0
        self.waits = []


class Prog:
    def __init__(self, nc):
        self.nc = nc
        self.ops = []
        self.tinfo = {}
        self.recs = {}

    def reg(self, name, kind, row):
        self.tinfo[name] = (kind, row)

    def box(self, ap):
        name = ap.tensor.name
        kind, row = self.tinfo[name]
        aps = ap.ap
        off = ap.offset
        if kind == "dram":
            ext = sum((c - 1) * abs(s) for s, c in aps) + 1
            return name, (0, 1, off, off + ext)
        p0 = off // row
        f0 = off % row
        pc = aps[0][1]
        ext = sum((c - 1) * abs(s) for s, c in aps[1:]) + 1
        return name, (p0, p0 + pc, f0, f0 + ext)

    @staticmethod
    def _ov(a, b):
        return a[0] < b[1] and b[0] < a[1] and a[2] < b[3] and b[2] < a[3]

    @staticmethod
    def _contains(a, b):
        return a[0] <= b[0] and b[1] <= a[1] and a[2] <= b[2] and b[3] <= a[3]

    def _dep(self, i, j):
        if i == j:
            return
        a, b = self.ops[i], self.ops[j]
        if a.eng == "pe" and b.eng == "pe" and a.kind == "c" and b.kind == "c":
            return
        a.deps.add(b)

    def add(self, eng, fn, reads, writes, kind="c"):
        i = len(self.ops)
        op = Op(eng, kind, fn)
        op.idx = i
        self.ops.append(op)
        rkey = eng if kind == "c" else ("x", i)
        for ap in reads:
            name, bx = self.box(ap)
            lst = self.recs.setdefault(name, [])
            found = None
            for r in lst:
                if self._ov(r[0], bx):
                    if r[1] is not None:
                        self._dep(i, r[1])
                    if r[0] == bx:
                        found = r
            if found is None:
                found = [bx, None, {}]
                lst.append(found)
            found[2][rkey] = i
        for ap in writes:
            name, bx = self.box(ap)
            lst = self.recs.setdefault(name, [])
            keep = []
            for r in lst:
                if self._ov(r[0], bx):
                    if r[1] is not None:
                        self._dep(i, r[1])
                    for j in r[2].values():
                        self._dep(i, j)
                    if self._contains(bx, r[0]):
                        continue
                keep.append(r)
            keep.append([bx, i, {}])
            self.recs[name] = keep
        return i

    def mm(self, out, lhsT, rhs, start=True, stop=True):
        rd = [lhsT, rhs] + ([] if start else [out])
        return self.add("pe", lambda e: e.matmul(out, lhsT, rhs, start=start, stop=stop), rd, [out])

    def tr(self, out, in_, ident):
        return self.add("pe", lambda e: e.transpose(out, in_, ident), [in_, ident], [out])

    def act(self, out, in_, func, bias=None, scale=None):
        rd = [in_]
        kw = {}
        if bias is not None:
            kw["bias"] = bias
            if not isinstance(bias, (int, float)):
                rd.append(bias)
        if scale is not None:
            kw["scale"] = scale
            if not isinstance(scale, (int, float)):
                rd.append(scale)
        return self.add("act", lambda e: e.activation(out, in_, func, **kw), rd, [out])

    def tt(self, eng, out, a, b, op):
        return self.add(eng, lambda e: e.tensor_tensor(out, a, b, op), [a, b], [out])

    def ts(self, eng, out, a, s1, s2, op0, op1=None):
        rd = [a] + [s for s in (s1, s2) if s is not None and not isinstance(s, (int, float))]
        if op1 is None:
            return self.add(eng, lambda e: e.tensor_scalar(out, a, s1, None, op0), rd, [out])
        return self.add(eng, lambda e: e.tensor_scalar(out, a, s1, s2, op0, op1), rd, [out])

    def stt(self, out, in0, scalar, in1, op0, op1):
        rd = [in0, in1] + ([] if isinstance(scalar, (int, float)) else [scalar])
        return self.add("dve", lambda e: e.scalar_tensor_tensor(out, in0, scalar, in1, op0, op1), rd, [out])

    def copy(self, eng, out, in_):
        if eng == "act":
            return self.add("act", lambda e: e.copy(out, in_), [in_], [out])
        return self.add(eng, lambda e: e.tensor_copy(out, in_), [in_], [out])

    def recip(self, out, in_):
        return self.add("dve", lambda e: e.reciprocal(out, in_), [in_], [out])

    def memset(self, eng, ap, val):
        return self.add(eng, lambda e: e.memset(ap, val), [], [ap])

    def dma(self, q, out, in_, hoist=False):
        i = self.add(q, lambda e: e.dma_start(out=out, in_=in_), [in_], [out], kind="d")
        self.ops[i].hoist = hoist
        return i

    def hoist_ops(self):
        after = {}
        for op in self.ops:
            if op.hoist:
                t = max((d.idx for d in op.deps), default=-1)
                after.setdefault(t, []).append(op)
        new = list(after.get(-1, []))
        for op in self.ops:
            if not op.hoist:
                new.append(op)
            new.extend(after.get(op.idx, []))
        assert len(new) == len(self.ops)
        self.ops = new

    def generic(self, eng, fn, reads, writes, kind="c"):
        return self.add(eng, fn, reads, writes, kind=kind)

    def finalize(self, sems, qsems, ccsems, block):
        self.hoist_ops()
        ops = self.ops
        qn = {}
        ncc = 0
        for i, op in enumerate(ops):
            if op.kind == "d":
                lst = qn.setdefault(op.eng, [])
                n = len(lst)
                op.semkey = ("q", op.eng, n % NQ)
                op.semval = 16 * (n // NQ + 1)
                if n >= NQ:
                    op.deps.add(lst[n - NQ])
                lst.append(op)
            elif op.kind == "cc":
                op.semkey = ("cc", ncc)
                op.semval = 1
                ncc += 1
        for op in ops:
            for d in op.deps:
                d.sig = True
        cnts = {}
        for op in ops:
            if op.kind == "c" and op.sig:
                cnts[op.eng] = cnts.get(op.eng, 0) + 1
                op.cnt = cnts[op.eng]
                op.semkey = op.eng
                op.semval = op.cnt
        know = {}
        clocks = {}
        for i, op in enumerate(ops):
            kn = know.setdefault(op.eng, {})
            for pj in sorted(op.deps, key=lambda d: d.idx):
                if kn.get(pj.semkey, 0) < pj.semval:
                    op.waits.append((pj.semkey, pj.semval))
                ck = clocks.get(id(pj))
                if ck is not None:
                    for k, v in ck.items():
                        if kn.get(k, 0) < v:
                            kn[k] = v
                if kn.get(pj.semkey, 0) < pj.semval:
                    kn[pj.semkey] = pj.semval
            if op.kind != "c" or op.sig:
                ck = dict(kn)
                ck[op.semkey] = max(ck.get(op.semkey, 0), op.semval)
                clocks[id(op)] = ck
        final = {}
        for op in ops:
            if op.kind in ("d", "cc"):
                final[op.semkey] = max(final.get(op.semkey, 0), op.semval)

        def semof(key):
            if isinstance(key, str):
                return sems[key]
            if key[0] == "q":
                return qsems[key[1]][key[2]]
            return ccsems[key[1]]

        def emit(engname):
            def body(e):
                for op in ops:
                    if op.eng != engname:
                        continue
                    w = {}
                    for k, v in op.waits:
                        w[k] = max(w.get(k, 0), v)
                    for k, v in w.items():
                        e.wait_ge(semof(k), v)
                    inst = op.fn(e)
                    if op.kind == "d":
                        inst.then_inc(semof(op.semkey), 16)
                    elif op.kind == "cc":
                        inst.then_inc(semof(op.semkey))
                    elif op.sig:
                        inst.then_inc(sems[op.eng], 1)
                if engname == "sp":
                    for k, v in final.items():
                        e.wait_ge(semof(k), v)
            return body

        block.tensor(emit("pe"))
        block.scalar(emit("act"))
        block.vector(emit("dve"))
        block.gpsimd(emit("pool"))
        block.sync(emit("sp"))


def build_program():
    nc = bass.Bass("TRN2", target_bir_lowering=False)
    P = Prog(nc)

    def dram(name, shape, dt, kind):
        t = nc.dram_tensor(name, shape, dt, kind=kind) if kind else nc.dram_tensor(name, shape, dt)
        P.reg(name, "dram", 0)
        return t.ap()

    xp = dram("xp", [TP, D], F32, "ExternalInput")
    xs = dram("xs", [TS, D], F32, "ExternalInput")
    cckv = dram("cckv", [2, 512, 256], F32, "ExternalInput")
    ckpe = dram("ckpe", [2, 512, 64], F32, "ExternalInput")
    cT_d = dram("cT", [128, 8, 2], F32, "ExternalInput")
    normg_d = dram("normg", [128, 4, 8], F32, "ExternalInput")
    bmod_d = dram("bmod", [128, 4, 24], F32, "ExternalInput")
    qg_d = dram("qg", [128, 2, 4], F32, "ExternalInput")
    kvg_d = dram("kvg", [128, 2, 2], F32, "ExternalInput")
    vg_d = dram("vg", [128, 2, 16], F32, "ExternalInput")
    fg_d = dram("fg", [128, 8], F32, "ExternalInput")
    vb_d = dram("vb_bc", [128, 2, 2048], F32, "ExternalInput")
    bs_d = dram("bs_bc", [128, 2, 8, 128], F32, "ExternalInput")
    cos_d = dram("cos", [64, 512], F32, "ExternalInput")
    sin_d = dram("sin", [64, 512], F32, "ExternalInput")
    ident_d = dram("ident", [128, 128], F32, "ExternalInput")
    w_mod = dram("w_mod", [4, 1024, 3072], F32, "ExternalInput")
    a_w_in = dram("attn_w_in", [2, 1024, 1856], F32, "ExternalInput")
    a_w_uq = dram("attn_w_uq", [2, 512, 1536], F32, "ExternalInput")
    a_w_ukv = dram("attn_w_ukv", [2, 256, 2048], F32, "ExternalInput")
    a_w_o = dram("attn_w_o", [2, 1024, 1024], F32, "ExternalInput")
    m_w_in = dram("mlp_w_in", [2, 1024, 6144], F32, "ExternalInput")
    m_w_s = dram("mlp_w_s", [2, 8, 128, 128], F32, "ExternalInput")
    m_w_o = dram("mlp_w_o", [2, 2048, 1024], F32, "ExternalInput")
    yp = dram("yp", [TP, D], F32, "ExternalOutput")
    ys = dram("ys", [TS, D], F32, "ExternalOutput")
    nckv = dram("nckv", [4, 2, 256, 256], F32, "ExternalOutput")
    nkpe = dram("nkpe", [4, 2, 256, 64], F32, "ExternalOutput")
    agin = [dram("agin%d" % a, [320, 512], BF16, None) for a in range(2)]
    agout = [dram("agout%d" % a, [4 * 320, 512], BF16, None) for a in range(2)]

    import contextlib
    es = contextlib.ExitStack()

    def sb(name, shape, dt):
        t = es.enter_context(nc.sbuf_tensor(name, shape, dt))
        row = 1
        for s in shape[1:]:
            row *= s
        P.reg(name, "sb", row)
        return t

    ARN = 45056
    X = sb("X", [128, 8, T], F32)
    WR = sb("WR", [128, 4, 4096], BF16)
    AR = sb("AR", [128, ARN], BF16)
    A32 = sb("A32", [128, 3072], F32)
    TMP = sb("TMP", [128, 3, 512], F32)
    RSTD = sb("RSTD", [128, 2, 512], F32)
    SQ = sb("SQ", [128, 2, 512], BF16)
    SQ3 = sb("SQ3", [128, 4, 512], BF16)
    COS = sb("COS", [64, 512], F32)
    SIN = sb("SIN", [64, 512], F32)
    ONES = sb("ONES", [128, 128], BF16)
    IDF = sb("IDF", [128, 128], F32)
    IDB = sb("IDB", [128, 128], BF16)
    CT32 = sb("CT32", [128, 8, 2], F32)
    SC = sb("SC", [128, 8, 2], BF16)
    NORMG = sb("NORMG", [128, 4, 8], F32)
    BMOD = sb("BMOD", [128, 4, 24], F32)
    QG = sb("QG", [128, 2, 4], F32)
    KVG = sb("KVG", [128, 2, 2], F32)
    VG = sb("VG", [128, 2, 16], F32)
    FG = sb("FG", [128, 8], F32)
    MODV = sb("MODV", [128, 4, 24, 2], F32)
    MA = sb("MA", [128, 4, 8, 2], F32)
    BNS = sb("BNS", [128, 12, 24], F32)
    BNA = sb("BNA", [128, 12, 2], F32)
    RS1 = sb("RS1", [128, 12, 1], F32)
    NMR = sb("NMR", [128, 2, 1], F32)
    WSRAW = sb("WSRAW", [128, 2, 128], F32)

    PS = es.enter_context(nc.psum_tensor("PS", [128, 8, 512], F32))
    P.reg("PS", "ps", 4096)

    sems = {k: es.enter_context(nc.semaphore("s_" + k)) for k in ("pe", "act", "dve", "pool")}
    qsems = {q: [es.enter_context(nc.semaphore("q_%s_%d" % (q, i))) for i in range(NQ)] for q in ("sp", "pool")}
    ccsems = [es.enter_context(nc.semaphore("cc%d" % i)) for i in range(2)]

    state = {"slab": 0, "bank": 0, "tmp": 0, "sq": 0, "rstd": 0, "ev": 0, "sq3": 0}

    sl_busy = [False] * 4
    sl_rel = [0, 1, 2, 3]
    sl_tags = {}

    def release_tag(tag):
        if tag in sl_tags:
            i = sl_tags.pop(tag)
            sl_busy[i] = False
            state["slab"] += 1
            sl_rel[i] = 4 + state["slab"]

    def slab(kc, ncols, tag="s"):
        release_tag(tag)
        free = [i for i in range(4) if not sl_busy[i]]
        assert free, "weight ring exhausted"
        i = min(free, key=lambda j: sl_rel[j])
        sl_busy[i] = True
        sl_tags[tag] = i
        return WR[:, i, 0:kc * ncols].rearrange("p (k n) -> p k n", k=kc)

    def wload(dst, src2d, kc):
        P.dma("pool", dst, src2d.rearrange("(k p) n -> p k n", p=128), hoist=True)

    def bank(pool=None):
        pool = pool or (0, 1, 2, 3, 4, 5, 6, 7)
        b = pool[state["bank"] % len(pool)]
        state["bank"] += 1
        return b

    def odpair():
        state["od"] = state.get("od", 0) + 1
        return ((4, 6), (5, 7))[state["od"] % 2]

    def tmp32():
        i = state["tmp"] % 3
        state["tmp"] += 1
        return TMP[:, i, :]

    def sqt():
        i = state["sq"] % 2
        state["sq"] += 1
        return SQ[:, i, :]

    def rstdt():
        i = state["rstd"] % 2
        state["rstd"] += 1
        return RSTD[:, i, :]

    def evac_eng():
        state["ev"] += 1
        return "act" if state["ev"] % 2 else "dve"

    def evac_copy(out, in_, eng=None):
        eng = eng or evac_eng()
        P.copy(eng, out, in_)

    def av(off, n):
        return AR[:, off:off + n]

    def rstd_from_ps(ps_ap, n_feat, npart=128):
        r = rstdt()
        P.act(r[0:npart], ps_ap, AF.Sqrt, bias=EPSB[0:npart, :], scale=1.0 / n_feat)
        P.recip(r[0:npart], r[0:npart])
        return r

    EPSB = sb("EPSB", [128, 1], F32)

    P.memset("dve", EPSB[:], EPS)
    P.memset("dve", ONES[:], 1.0)
    for dst, src in ((IDF, ident_d), (CT32, cT_d), (NORMG, normg_d), (BMOD, bmod_d), (QG, qg_d),
                     (KVG, kvg_d), (VG, vg_d), (FG, fg_d), (COS, cos_d), (SIN, sin_d)):
        P.dma("sp", dst[:], src)
    P.copy("dve", IDB[:], IDF[:])
    P.act(SC[:], CT32[:], AF.Silu)

    def emit_mod_slab(l, sj, pool=None):
        s = slab(8, 512, "mod")
        wload(s, w_mod[l, :, sj * 512:(sj + 1) * 512], 8)
        b = bank(pool)
        for jj in range(4):
            for k in range(8):
                P.mm(PS[:, b, 2 * jj:2 * jj + 2], s[:, k, jj * 128:(jj + 1) * 128], SC[:, k, :],
                     start=(k == 0), stop=(k == 7))
        if pool is not None:
            for jj in (3, 2, 1, 0):
                P.act(MODV[:, l, 4 * sj + jj, :], PS[:, b, 2 * jj:2 * jj + 2], AF.Identity,
                      bias=BMOD[:, l, 4 * sj + jj:4 * sj + jj + 1])
        else:
            P.tt("dve", MODV[:, l, 4 * sj:4 * sj + 4, :], PS[:, b, 0:8].rearrange("p (j v) -> p j v", v=2),
                 BMOD[:, l, 4 * sj:4 * sj + 4].unsqueeze(2).to_broadcast([128, 4, 2]), ALU.add)
        release_tag("mod")
        if sj == 3:
            P.stt(MA[:, l, :, :], MODV[:, l, 8:16, :], 1.0,
                  NORMG[:, l, :].unsqueeze(2).to_broadcast([128, 8, 2]), ALU.add, ALU.mult)

    def emit_mod(l):
        for sj in range(6):
            emit_mod_slab(l, sj)

    def vsel(blk):
        return 0 if blk < 2 else 1

    def ssq_ps(src_fn, nchunks, blk_cols, npart=128):
        b = bank()
        n = blk_cols
        for c in range(nchunks):
            s = sqt()
            src = src_fn(c)
            if c % 2 == 0:
                P.act(s[:, 0:n], src, AF.Square)
            else:
                P.tt("dve", s[:, 0:n], src, src, ALU.mult)
            P.mm(PS[:, b, 0:n], ONES[:], s[:, 0:n], start=(c == 0), stop=(c == nchunks - 1))
        return PS[:, b, 0:n]

    H_OFF = 0
    H = av(H_OFF, 8 * T).rearrange("p (c t) -> p c t", c=8)

    SSQB = {0: 5, 1: 6, 2: 7}

    def x_update(l, dc, blk, b):
        cs = slice(blk * 512, (blk + 1) * 512)
        v = vsel(blk)
        P.stt(X[:, dc, cs], PS[:, b, :], MODV[:, l, 16 + dc, v:v + 1], X[:, dc, cs], ALU.mult, ALU.add)
        s = SQ3[:, state["sq3"] % 4, :]
        state["sq3"] += 1
        P.tt("pool", s, X[:, dc, cs], X[:, dc, cs], ALU.mult)
        pend.append((blk, dc, s))
        while len(pend) > 2:
            x_flush()

    pend = []

    def x_flush_item(item):
        blk, dc, s = item
        P.mm(PS[:, SSQB[blk], :], ONES[:], s, start=(dc == 0), stop=(dc == 7))

    def x_flush():
        x_flush_item(pend.pop(0))

    def flush_blk(blk):
        keep = []
        while pend:
            item = pend.pop(0)
            if item[0] == blk:
                x_flush_item(item)
            else:
                keep.append(item)
        pend.extend(keep)

    def emit_h_blk(l, blk):
        cs = slice(blk * 512, (blk + 1) * 512)
        v = vsel(blk)
        if l == 0:
            ps = ssq_ps(lambda c: X[:, c, cs], 8, 512)
        else:
            flush_blk(blk)
            ps = PS[:, SSQB[blk], :]
        r = rstd_from_ps(ps, 1024.0)

        def one(c):
            t = tmp32()
            P.stt(t, X[:, c, cs], MA[:, l, c, v:v + 1], r, ALU.mult, ALU.mult)
            P.act(H[:, c, cs], t, AF.Identity, bias=MODV[:, l, c, v:v + 1])
        for c in range(8):
            dq.append(lambda c=c: one(c))

    dq = []

    def dq_step():
        if dq:
            dq.pop(0)()

    def dq_flush():
        while dq:
            dq.pop(0)()

    def emit_h(l):
        for blk in (2, 0, 1):
            emit_h_blk(l, blk)

    def after_update_blk(l, blk):
        if l + 1 < 4:
            emit_h_blk(l + 1, blk)
        else:
            emit_final_blk(blk)

    def emit_load_x():
        r0 = {}

        def load_tile(tt_):
            st = A32[:, (tt_ % 3) * 1024:(tt_ % 3 + 1) * 1024]
            if tt_ < 8:
                src = xp[tt_ * 128:(tt_ + 1) * 128, :]
            else:
                src = xs[(tt_ - 8) * 128:(tt_ - 7) * 128, :]
            P.dma("sp", st, src)
            for half in range(2):
                b = bank()
                for cc in range(4):
                    c = half * 4 + cc
                    P.tr(PS[:, b, cc * 128:(cc + 1) * 128], st[:, c * 128:(c + 1) * 128], IDF[:])
                evac_copy(X[:, half * 4:half * 4 + 4, tt_ * 128:(tt_ + 1) * 128],
                          PS[:, b, :].rearrange("p (c t) -> p c t", c=4))

        def stats(blk, rt):
            cs = slice(blk * 512, (blk + 1) * 512)
            ps = ssq_ps(lambda c: X[:, c, cs], 8, 512)
            P.act(rt, ps, AF.Sqrt, bias=EPSB[:], scale=1.0 / 1024.0)
            P.recip(rt, rt)
            r0[blk] = rt

        emit_mod_slab(0, 0)
        load_tile(8)
        load_tile(9)
        emit_mod_slab(0, 1)
        load_tile(10)
        load_tile(11)
        stats(2, rstdt())
        emit_mod_slab(0, 2)
        load_tile(0)
        load_tile(1)
        emit_mod_slab(0, 3)
        load_tile(2)
        load_tile(3)
        for tt_ in (4, 5, 6, 7):
            load_tile(tt_)
        stats(0, A32[:, 0:512])
        stats(1, A32[:, 512:1024])
        emit_cache_part(0)
        for blk in (2, 0, 1):
            cs = slice(blk * 512, (blk + 1) * 512)
            v = vsel(blk)
            for c in range(8):
                t = tmp32()
                P.stt(t, X[:, c, cs], MA[:, 0, c, v:v + 1], r0[blk], ALU.mult, ALU.mult)
                P.act(H[:, c, cs], t, AF.Identity, bias=MODV[:, 0, c, v:v + 1])

    def emit_final_blk(blk):
        flush_blk(blk)
        cs = slice(blk * 512, (blk + 1) * 512)
        r = rstd_from_ps(PS[:, SSQB[blk], :], 1024.0)
        for c in range(8):
            dq.append(lambda c=c: P.stt(X[:, c, cs], X[:, c, cs], FG[:, c:c + 1], r, ALU.mult, ALU.mult))

        def out_tile(tt_):
            st = A32[:, (tt_ % 3) * 1024:(tt_ % 3 + 1) * 1024]
            for half in range(2):
                b = bank((0, 1, 2, 3, 4))
                for cc in range(4):
                    c = half * 4 + cc
                    P.tr(PS[:, b, cc * 128:(cc + 1) * 128], X[:, c, tt_ * 128:(tt_ + 1) * 128], IDF[:])
                evac_copy(st[:, half * 512:(half + 1) * 512], PS[:, b, :])
            if tt_ < 8:
                dst = yp[tt_ * 128:(tt_ + 1) * 128, :]
            else:
                dst = ys[(tt_ - 8) * 128:(tt_ - 7) * 128, :]
            P.dma("sp", dst, st)
        for t4 in range(4):
            dq.append(lambda t4=t4: out_tile(blk * 4 + t4))

    GZ = av(12288, 8 * T).rearrange("p (c t) -> p c t", c=8)
    CQN = av(24576, 4 * T).rearrange("p (c t) -> p c t", c=4)
    CKVN = av(30720, 2 * T).rearrange("p (c t) -> p c t", c=2)
    KPEB = av(33792, T)
    KVALL = av(35328, 2 * NKEY_S).rearrange("p (c t) -> p c t", c=2)
    KPEALL = av(40448, NKEY_S)
    WUQSW = av(43008, 2048).rearrange("p (k n) -> p k n", k=4)
    QN = av(0, T)
    QR = av(1536, T)
    KTP = av(3072, 1024)
    KTS = av(4096, NKEY_S)
    VV = av(6656, 28 * 128).rearrange("p (t d) -> p t d", t=28)
    PT = av(10240, 2048).rearrange("p (i n) -> p i n", i=4)
    ACCD = A32[:, 0:512]
    ACCP = A32[:, 512:1024]
    OUTKV = A32[:, 0:1280].rearrange("p (t f) -> p t f", t=4)
    CST = A32[:, 1280:2560].rearrange("p (t f) -> p t f", t=4)

    def emit_cache_part(a):
        P.dma("sp", CST[:, :, 0:256], cckv[a].rearrange("(t p) f -> p t f", p=128), hoist=True)
        P.dma("sp", CST[:, :, 256:320], ckpe[a].rearrange("(t p) f -> p t f", p=128), hoist=True)
        for j in range(2):
            b = bank((0, 1, 2, 3))
            for t4 in range(4):
                P.tr(PS[:, b, t4 * 128:(t4 + 1) * 128], CST[:, t4, j * 128:(j + 1) * 128], IDF[:])
            evac_copy(KVALL[:, j, 0:512], PS[:, b, :])
        b = bank((0, 1, 2, 3))
        for t4 in range(4):
            P.tr(PS[0:64, b, t4 * 128:(t4 + 1) * 128], CST[:, t4, 256:320], IDF[:])
        evac_copy(KPEALL[0:64, 0:512], PS[0:64, b, :])

    def emit_attn(a, l):
        blks = (2, 0, 1)
        P.memset("pool", KPEB[64:128, :], 0.0)
        P.memset("pool", KPEALL[64:128, :], 0.0)
        if l != 0:
            emit_cache_part(a)
        s0 = slab(8, 512, "q")
        wload(s0, a_w_in[a, :, 0:512], 8)
        s1 = slab(8, 384, "kv")
        wload(s1[:, :, 0:320], a_w_in[a, :, 512:832], 8)
        zsl = []
        for zs in range(2):
            s2 = slab(8, 512, "z%d" % zs)
            wload(s2, a_w_in[a, :, 832 + zs * 512:832 + (zs + 1) * 512], 8)
            zsl.append(s2)
        zq = [(zs, j, blk) for zs in range(2) for j in range(4) for blk in blks]

        def emit_z(n):
            for _ in range(n):
                if not zq:
                    return
                zs, j, blk = zq.pop(0)
                if zs == 1 and "z0" in sl_tags:
                    release_tag("z0")
                cs_ = slice(blk * 512, (blk + 1) * 512)
                b_ = bank((4, 5, 6, 7))
                for k in range(8):
                    P.mm(PS[:, b_, :], zsl[zs][:, k, j * 128:(j + 1) * 128], H[:, k, cs_], start=(k == 0), stop=(k == 7))
                P.act(GZ[:, zs * 4 + j, cs_], PS[:, b_, :], AF.Silu)

        for blk in blks:
            cs = slice(blk * 512, (blk + 1) * 512)
            bq = [bank((0, 1, 2, 3)) for _ in range(4)]
            for j in range(4):
                for k in range(8):
                    P.mm(PS[:, bq[j], :], s0[:, k, j * 128:(j + 1) * 128], H[:, k, cs], start=(k == 0), stop=(k == 7))
            bs_ = bank((4, 5, 6, 7))
            for j in range(4):
                s = sqt()
                P.act(s, PS[:, bq[j], :], AF.Square)
                P.mm(PS[:, bs_, :], ONES[:], s, start=(j == 0), stop=(j == 3))
            r = rstd_from_ps(PS[:, bs_, :], 512.0)
            for j in range(4):
                P.stt(CQN[:, j, cs], PS[:, bq[j], :], QG[:, a, j:j + 1], r, ALU.mult, ALU.mult)
            emit_z(4)
        release_tag("q")
        kv4 = s1[:, :, 256:320].rearrange("p k (x h i) -> p k x h i", x=2, h=2)
        sw4 = s1[:, :, 320:384].rearrange("p k (x h i) -> p k x h i", x=2, h=2)
        for x_ in range(2):
            P.ts("dve", sw4[:, :, x_, 0, :], kv4[:, :, x_, 1, :], -1.0, None, ALU.mult)
            P.copy("dve", sw4[:, :, x_, 1, :], kv4[:, :, x_, 0, :])
        for blk in blks:
            cs = slice(blk * 512, (blk + 1) * 512)
            bk = [bank((0, 1, 2, 3)) for _ in range(2)]
            for j in range(2):
                for k in range(8):
                    P.mm(PS[:, bk[j], :], s1[:, k, j * 128:(j + 1) * 128], H[:, k, cs], start=(k == 0), stop=(k == 7))
            bs_ = bank((4, 5, 6, 7))
            for j in range(2):
                s = sqt()
                P.act(s, PS[:, bk[j], :], AF.Square)
                P.mm(PS[:, bs_, :], ONES[:], s, start=(j == 0), stop=(j == 1))
            r = rstd_from_ps(PS[:, bs_, :], 256.0)
            bp = bank((4, 5, 6, 7))
            for k in range(8):
                P.mm(PS[0:64, bp, :], s1[:, k, 256:320], H[:, k, cs], start=(k == 0), stop=(k == 7))
            if blk == 2:
                bp2 = bank((4, 5, 6, 7))
                for k in range(8):
                    P.mm(PS[0:64, bp2, :], s1[:, k, 320:384], H[:, k, cs], start=(k == 0), stop=(k == 7))
                t1 = tmp32()
                t2 = tmp32()
                P.tt("dve", t1[0:64], PS[0:64, bp, :], COS[:], ALU.mult)
                P.tt("dve", t2[0:64], PS[0:64, bp2, :], SIN[:], ALU.mult)
                P.tt("dve", KPEB[0:64, cs], t1[0:64], t2[0:64], ALU.add)
                for j in range(2):
                    P.stt(CKVN[:, j, cs], PS[:, bk[j], :], KVG[:, a, j:j + 1], r, ALU.mult, ALU.mult)
                for j in range(2):
                    P.dma("sp", agin[a][j * 128:(j + 1) * 128, :], CKVN[:, j, cs])
                P.dma("sp", agin[a][256:320, :], KPEB[0:64, cs])
                P.generic("pool", lambda e, a=a: e.collective_compute(
                    "AllGather", ALU.bypass, replica_groups=[[0, 1, 2, 3], [4, 5, 6, 7]],
                    ins=[agin[a].opt()], outs=[agout[a].opt()]), [agin[a]], [agout[a]], kind="cc")
            else:
                evac_copy(KPEB[0:64, cs], PS[0:64, bp, :], "act")
                tk = tmp32()
                evac_copy(tk[0:64], PS[0:64, bp, :], "dve")
                tcs = []
                for j in range(2):
                    tc = tmp32()
                    P.stt(tc, PS[:, bk[j], :], KVG[:, a, j:j + 1], r, ALU.mult, ALU.mult)
                    P.copy("dve", CKVN[:, j, cs], tc)
                    tcs.append(tc)
                emit_z(3)
                for j in range(2):
                    tc = tcs[j]
                    b = bank((0, 1, 2, 3))
                    for t4 in range(4):
                        P.tr(PS[:, b, t4 * 128:(t4 + 1) * 128], tc[:, t4 * 128:(t4 + 1) * 128], IDF[:])
                    evac_copy(OUTKV[:, :, j * 128:(j + 1) * 128], PS[:, b, :].rearrange("p (t f) -> p t f", t=4))
                b = bank((0, 1, 2, 3))
                for t4 in range(4):
                    P.tr(PS[:, b, t4 * 64:(t4 + 1) * 64], tk[0:64, t4 * 128:(t4 + 1) * 128], IDF[0:64, 0:64])
                evac_copy(OUTKV[:, :, 256:320], PS[:, b, 0:256].rearrange("p (t f) -> p t f", t=4))
                for sl in range(2):
                    seq = blk * 2 + sl
                    P.dma("sp", nckv[seq, a].rearrange("(t p) f -> p t f", p=128), OUTKV[:, 2 * sl:2 * sl + 2, 0:256])
                    P.dma("sp", nkpe[seq, a].rearrange("(t p) f -> p t f", p=128), OUTKV[:, 2 * sl:2 * sl + 2, 256:320])
            if blk == 2:
                emit_z(3)
        release_tag("kv")
        emit_z(len(zq))
        release_tag("z0")
        release_tag("z1")
        agv = agout[a].rearrange("(r f) t -> f r t", f=320)
        for j in range(2):
            P.dma("sp", KVALL[:, j, 512:NKEY_S].rearrange("p (r t) -> p r t", r=4),
                  agv[j * 128:(j + 1) * 128, :, :])
        P.dma("sp", KPEALL[0:64, 512:NKEY_S].rearrange("p (r t) -> p r t", r=4), agv[256:320, :, :])
        P.memset("pool", QR[64:128, :], 0.0)
        wkv = slab(2, 2048, "wkv")
        wload(wkv[:, :, 0:1024], a_w_ukv[a, :, 0:1024], 2)
        wload(wkv[:, :, 1024:2048], a_w_ukv[a, :, 1024:2048], 2)
        wq = None
        for h in range(8):
            if l + 1 < 4 and h < 6:
                emit_mod_slab(l + 1, h, (0, 1, 2, 3))
            if l == 0 and h >= 6:
                emit_mod_slab(0, h - 2, (0, 1, 2, 3))
            if h == 0 or h == 3:
                g4 = 0 if h == 0 else 1
                wq_new = slab(4, 768, "wq%d" % g4)
                wload(wq_new, a_w_uq[a, :, g4 * 768:(g4 + 1) * 768], 4)

                def build_sw(wq_new=wq_new, g4=g4):
                    for hh in range(4):
                        src = wq_new[:, :, hh * 192 + 128:hh * 192 + 192].rearrange("p k (x h i) -> p k x h i", x=2, h=2)
                        dst = WUQSW[:, :, (g4 * 4 + hh) * 64:(g4 * 4 + hh + 1) * 64].rearrange(
                            "p k (x h i) -> p k x h i", x=2, h=2)
                        for x_ in range(2):
                            P.ts("dve", dst[:, :, x_, 0, :], src[:, :, x_, 1, :], -1.0, None, ALU.mult)
                            P.copy("dve", dst[:, :, x_, 1, :], src[:, :, x_, 0, :])
                if h == 0:
                    build_sw()
                    wq = wq_new
            if h == 4:
                release_tag("wq0")
                wq = wq_new
            hq = h % 4
            for blk in (0, 1, 2):
                cs = slice(blk * 512, (blk + 1) * 512)
                b = bank((0, 1, 2, 3))
                for k in range(4):
                    P.mm(PS[:, b, :], wq[:, k, hq * 192:hq * 192 + 128], CQN[:, k, cs], start=(k == 0), stop=(k == 3))
                evac_copy(QN[:, cs], PS[:, b, :], "act")
                b = bank((0, 1, 2, 3))
                for k in range(4):
                    P.mm(PS[0:64, b, :], wq[:, k, hq * 192 + 128:hq * 192 + 192], CQN[:, k, cs],
                         start=(k == 0), stop=(k == 3))
                if blk == 2:
                    b2 = bank((0, 1, 2, 3))
                    for k in range(4):
                        P.mm(PS[0:64, b2, :], WUQSW[:, k, h * 64:(h + 1) * 64], CQN[:, k, cs],
                             start=(k == 0), stop=(k == 3))
                    t1 = tmp32()
                    t2 = tmp32()
                    P.tt("dve", t1[0:64], PS[0:64, b, :], COS[:], ALU.mult)
                    P.tt("dve", t2[0:64], PS[0:64, b2, :], SIN[:], ALU.mult)
                    P.tt("pool", QR[0:64, cs], t1[0:64], t2[0:64], ALU.add)
                else:
                    evac_copy(QR[0:64, cs], PS[0:64, b, :], "act")
            for cb in range(2):
                b = bank((0, 1, 2, 3))
                for k in range(2):
                    P.mm(PS[:, b, :], wkv[:, k, h * 256:h * 256 + 128], CKVN[:, k, cb * 512:(cb + 1) * 512],
                         start=(k == 0), stop=(k == 1))
                evac_copy(KTP[:, cb * 512:(cb + 1) * 512], PS[:, b, :], "act")
            for cb in range(5):
                b = bank((0, 1, 2, 3))
                for k in range(2):
                    P.mm(PS[:, b, :], wkv[:, k, h * 256:h * 256 + 128], KVALL[:, k, cb * 512:(cb + 1) * 512],
                         start=(k == 0), stop=(k == 1))
                evac_copy(KTS[:, cb * 512:(cb + 1) * 512], PS[:, b, :], "act")
            for g in range(7):
                b = bank((0, 1, 2, 3))
                for t4 in range(4):
                    ti = g * 4 + t4
                    for k in range(2):
                        if ti < 8:
                            lt = CKVN[:, k, ti * 128:(ti + 1) * 128]
                        else:
                            lt = KVALL[:, k, (ti - 8) * 128:(ti - 7) * 128]
                        P.mm(PS[:, b, t4 * 128:(t4 + 1) * 128], lt, wkv[:, k, h * 256 + 128:h * 256 + 256],
                             start=(k == 0), stop=(k == 1))
                evac_copy(VV[:, g * 4:(g + 1) * 4, :], PS[:, b, :].rearrange("p (t d) -> p t d", t=4), "dve")
            cs2 = slice(1024, 1536)
            units = []
            bo_p, bd_p = odpair()
            for sl in range(2):
                units.append(("p", sl, bo_p, bd_p))
            bo_s, bd_s = odpair()
            for kt in range(20):
                units.append(("s", kt, bo_s, bd_s))
            bo_p, bd_p = odpair()
            for sl in range(2):
                units.append(("p", 2 + sl, bo_p, bd_p))
            LA = 3
            nun = len(units)
            pts = {}
            for i in range(nun + LA):
                if i < nun:
                    kind, idx, bo, bd = units[i]
                    b = bank((0, 1, 2, 3))
                    if kind == "s":
                        kt = idx
                        P.mm(PS[:, b, :], KTS[:, kt * 128:(kt + 1) * 128], QN[:, cs2], start=True, stop=False)
                        P.mm(PS[:, b, :], KPEALL[:, kt * 128:(kt + 1) * 128], QR[:, cs2], start=False, stop=True)
                    else:
                        seq = idx
                        qs = slice(seq * 256, (seq + 1) * 256)
                        for kt in range(2):
                            ks = slice(seq * 256 + kt * 128, seq * 256 + (kt + 1) * 128)
                            P.mm(PS[:, b, kt * 256:(kt + 1) * 256], KTP[:, ks], QN[:, qs], start=True, stop=False)
                            P.mm(PS[:, b, kt * 256:(kt + 1) * 256], KPEB[:, ks], QR[:, qs], start=False, stop=True)
                    pt = PT[:, i % 4, :]
                    P.act(pt, PS[:, b, :], AF.Exp, scale=SM_SCALE)
                    pts[i] = pt
                j = i - LA
                if j >= 0:
                    kind, idx, bo, bd = units[j]
                    pt = pts.pop(j)
                    if kind == "s":
                        kt = idx
                        P.mm(PS[:, bo, :], VV[:, 8 + kt, :], pt, start=(kt == 0), stop=(kt == 19))
                        P.mm(PS[:, bd, :], ONES[:], pt, start=(kt == 0), stop=(kt == 19))
                        if kt == 19:
                            emit_og(h, 2, bo, bd)
                    else:
                        seq = idx
                        sl = seq % 2
                        for kt in range(2):
                            P.mm(PS[:, bo, sl * 256:(sl + 1) * 256], VV[:, seq * 2 + kt, :], pt[:, kt * 256:(kt + 1) * 256],
                                 start=(kt == 0), stop=(kt == 1))
                        for kt in range(2):
                            P.mm(PS[:, bd, sl * 256:(sl + 1) * 256], ONES[:], pt[:, kt * 256:(kt + 1) * 256],
                                 start=(kt == 0), stop=(kt == 1))
                        if sl == 1:
                            emit_og(h, seq // 2, bo, bd)
            if h == 3:
                build_sw()
        release_tag("wkv")
        release_tag("wq1")
        sos = []
        for os_ in range(2):
            so = slab(8, 512, "wo%d" % os_)
            wload(so, a_w_o[a, :, os_ * 512:(os_ + 1) * 512], 8)
            sos.append(so)
        prev = None
        for blk in blks:
            cs = slice(blk * 512, (blk + 1) * 512)
            for dc in range(8):
                so = sos[dc // 4]
                j = dc % 4
                b = bank((0, 1, 2, 3, 4))
                for k in range(8):
                    P.mm(PS[:, b, :], so[:, k, j * 128:(j + 1) * 128], GZ[:, k, cs], start=(k == 0), stop=(k == 7))
                x_update(l, dc, blk, b)
                if dc == 1 and prev is not None:
                    after_update_blk(l, prev)
                dq_step()
            prev = blk
        emit_mlp_setup(l // 2)
        after_update_blk(l, prev)
        dq_flush()
        release_tag("wo0")
        release_tag("wo1")

    def emit_og(h, blk, bo, bd):
        cs = slice(blk * 512, (blk + 1) * 512)
        rd = tmp32()
        P.act(rd, PS[:, bd, :], AF.Ln)
        P.act(rd, rd, AF.Exp, scale=-1.0)
        t = tmp32()
        P.tt("dve", t, PS[:, bo, :], rd, ALU.mult)
        P.tt("pool", GZ[:, h, cs], t, GZ[:, h, cs], ALU.mult)

    VH = av(12288, 12 * 2048).rearrange("p (t f) -> p t f", t=12)
    WST = av(36864, 1024).rearrange("p (g q) -> p g q", g=8)
    VBB = av(37888, 2048)
    UT = av(39936, 1536).rearrange("p (i n) -> p i n", i=3)
    GT = av(41472, 1536).rearrange("p (i n) -> p i n", i=3)
    CC = A32[:, 0:2048].rearrange("p (f q) -> p f q", f=16)
    BSB = A32[:, 2048:3072].rearrange("p (g q) -> p g q", g=8)

    def ln_group(t0):
        for ti in range(t0, t0 + 4):
            P.generic("dve", lambda e, ti=ti: e.bn_aggr(BNA[:, ti, :], BNS[:, ti, :]), [BNS[:, ti, :]], [BNA[:, ti, :]])
        P.act(RS1[:, t0:t0 + 4, :], BNA[:, t0:t0 + 4, 1:2], AF.Sqrt, bias=EPSB[:], scale=1.0)
        P.recip(RS1[:, t0:t0 + 4, :], RS1[:, t0:t0 + 4, :])
        for ti in range(t0, t0 + 4):
            P.ts("dve", VH[:, ti, :], VH[:, ti, :], BNA[:, ti, 0:1], RS1[:, ti, :], ALU.subtract, ALU.mult)

    def emit_mlp_setup(m):
        bp5 = (0, 1, 2, 3, 4)
        P.dma("pool", VBB, vb_d[:, m, :], hoist=True)
        P.dma("sp", BSB, bs_d[:, m, :, :], hoist=True)
        WSRAW8 = A32[:, 0:1024].rearrange("p (g q) -> p g q", g=8)
        P.dma("sp", WSRAW8, m_w_s[m].rearrange("g p q -> p g q"), hoist=True)
        for g2 in range(2):
            b = bank(bp5)
            for gg in range(4):
                P.tr(PS[:, b, gg * 128:(gg + 1) * 128], WSRAW8[:, g2 * 4 + gg, :], IDF[:])
            evac_copy(WST[:, g2 * 4:(g2 + 1) * 4, :], PS[:, b, :].rearrange("p (g q) -> p g q", g=4))
        for f4 in range(4):
            b = bank(bp5)
            for ff in range(4):
                fc = f4 * 4 + ff
                P.mm(PS[:, b, ff * 128:(ff + 1) * 128], VBB[:, fc * 128:(fc + 1) * 128], WST[:, fc // 2, :],
                     start=True, stop=True)
            for g1 in (1, 0):
                g = f4 * 2 + g1
                P.tt("dve", CC[:, 2 * g:2 * g + 2, :],
                     PS[:, b, g1 * 256:(g1 + 1) * 256].rearrange("p (f q) -> p f q", f=2),
                     BSB[:, g, :].unsqueeze(1).to_broadcast([128, 2, 128]), ALU.add)

    def emit_mlp(m, l):
        blks = (2, 0, 1)
        for vs in range(4):
            s = slab(8, 512)
            wload(s, m_w_in[m, :, 2048 + vs * 512:2048 + (vs + 1) * 512], 8)
            for ti in range(12):
                b = bank()
                for k in range(8):
                    P.mm(PS[:, b, :], H[:, k, ti * 128:(ti + 1) * 128], s[:, k, :], start=(k == 0), stop=(k == 7))
                P.act(VH[:, ti, vs * 512:(vs + 1) * 512], PS[:, b, :], AF.Gelu_apprx_tanh)
                P.generic("dve", lambda e, ti=ti, q4=vs: e.bn_stats(
                    BNS[:, ti, q4 * 6:(q4 + 1) * 6], VH[:, ti, q4 * 512:(q4 + 1) * 512]),
                    [VH[:, ti, vs * 512:(vs + 1) * 512]], [BNS[:, ti, vs * 6:(vs + 1) * 6]])
                if vs == 3 and ti % 4 == 3:
                    ln_group(ti - 3)
        release_tag("s")
        su = sz = None
        for fc in range(16):
            if l + 1 < 4 and fc % 2 == 1 and fc < 12:
                emit_mod_slab(l + 1, fc // 2)
            if fc % 4 == 0:
                su = slab(8, 512, "su")
                wload(su, m_w_in[m, :, fc * 128:fc * 128 + 512], 8)
                sz = slab(8, 512, "sz")
                wload(sz, m_w_in[m, :, 4096 + fc * 128:4096 + fc * 128 + 512], 8)
            j = fc % 4
            g = fc // 2
            for blk in blks:
                cs = slice(blk * 512, (blk + 1) * 512)
                b = bank()
                for k in range(8):
                    P.mm(PS[:, b, :], su[:, k, j * 128:(j + 1) * 128], H[:, k, cs], start=(k == 0), stop=(k == 7))
                P.act(UT[:, blk, :], PS[:, b, :], AF.Gelu_apprx_tanh)
            for blk in blks:
                cs = slice(blk * 512, (blk + 1) * 512)
                b = bank()
                for k in range(8):
                    P.mm(PS[:, b, :], sz[:, k, j * 128:(j + 1) * 128], H[:, k, cs], start=(k == 0), stop=(k == 7))
                P.act(GT[:, blk, :], PS[:, b, :], AF.Silu)
                P.tt("pool", UT[:, blk, :], UT[:, blk, :], GT[:, blk, :], ALU.mult)
            for blk in blks:
                b = bank()
                for t4 in range(4):
                    ti = blk * 4 + t4
                    P.mm(PS[:, b, t4 * 128:(t4 + 1) * 128], VH[:, ti, fc * 128:(fc + 1) * 128], WST[:, g, :],
                         start=True, stop=True)
                t = tmp32()
                P.stt(t.rearrange("p (t q) -> p t q", t=4), PS[:, b, :].rearrange("p (t q) -> p t q", t=4),
                      VG[:, m, fc:fc + 1], CC[:, fc, :].unsqueeze(1).to_broadcast([128, 4, 128]), ALU.mult, ALU.add)
                P.tt("dve", VH[:, blk * 4:(blk + 1) * 4, fc * 128:(fc + 1) * 128],
                     t.rearrange("p (t q) -> p t q", t=4), UT[:, blk, :].rearrange("p (t q) -> p t q", t=4), ALU.mult)
        release_tag("su")
        release_tag("sz")
        sos = []
        for os_ in range(4):
            so = slab(16, 256, "wo%d" % os_)
            wload(so, m_w_o[m, :, os_ * 256:(os_ + 1) * 256], 16)
            sos.append(so)
        prev = None
        for blk in blks:
            cs = slice(blk * 512, (blk + 1) * 512)
            for dc in range(8):
                so = sos[dc // 2]
                j = dc % 2
                b = bank((0, 1, 2, 3, 4))
                for k in range(16):
                    P.mm(PS[:, b, :].rearrange("p (t q) -> p t q", t=4), so[:, k, j * 128:(j + 1) * 128],
                         VH[:, blk * 4:(blk + 1) * 4, k * 128:(k + 1) * 128], start=(k == 0), stop=(k == 15))
                x_update(l, dc, blk, b)
                if dc == 1 and prev is not None:
                    after_update_blk(l, prev)
                dq_step()
            prev = blk
        after_update_blk(l, prev)
        dq_flush()
        for os_ in range(4):
            release_tag("wo%d" % os_)

    emit_load_x()
    for l in range(4):
        if l % 2 == 0:
            emit_attn(l // 2, l)
        else:
            emit_mlp(l // 2, l)

    with nc.Block() as block:
        P.finalize(sems, qsems, ccsems, block)
    es.close()
    return nc


def _rope_tables(core):
    r = core % 4
    t = np.arange(r * 512, (r + 1) * 512)
    row = (t // 64).astype(np.float32)
    col = (t % 64).astype(np.float32)
    inv = (1.0 / (np.float32(10000.0) ** (np.arange(0, 32, 2, dtype=np.float32) / np.float32(32)))).astype(np.float32)
    ang = np.concatenate([row[:, None] * inv, col[:, None] * inv], axis=-1).astype(np.float32)
    cos = np.cos(ang).astype(np.float32)
    sin = np.sin(ang).astype(np.float32)
    idx = np.array([(d // 32) * 16 + (d % 16) for d in range(64)])
    return np.ascontiguousarray(cos[:, idx].T), np.ascontiguousarray(sin[:, idx].T)


def _fm(v, nchunk):
    v = np.asarray(v, np.float32)
    lead = v.shape[:-1]
    v = v.reshape(lead + (nchunk, 128))
    v = np.moveaxis(v, -1, 0)
    return np.ascontiguousarray(v)


_NC_CACHE = {}


def kernel(x_prompt, x_sample, cache_ckv, cache_kpe, c, c_ctx, norm_g, w_mod, b_mod,
           attn_w_in, attn_q_norm_g, attn_kv_norm_g, attn_w_uq, attn_w_ukv, attn_w_o,
           mlp_w_in, mlp_v_norm_g, mlp_v_norm_b, mlp_w_s, mlp_b_s, mlp_w_o, final_norm_g):
    f = lambda a: np.ascontiguousarray(np.asarray(a, np.float32))
    x_prompt, x_sample, cache_ckv, cache_kpe = f(x_prompt), f(x_sample), f(cache_ckv), f(cache_kpe)
    c, c_ctx = f(c), f(c_ctx)
    if "nc" not in _NC_CACHE:
        _NC_CACHE["nc"] = build_program()
    nc = _NC_CACHE["nc"]
    shared = {
        "normg": _fm(norm_g, 8), "bmod": _fm(b_mod, 24), "qg": _fm(attn_q_norm_g, 4),
        "kvg": _fm(attn_kv_norm_g, 2), "vg": _fm(mlp_v_norm_g, 16), "fg": _fm(final_norm_g, 8),
        "vb_bc": np.ascontiguousarray(np.broadcast_to(f(mlp_v_norm_b)[None], (128, 2, 2048))),
        "bs_bc": np.ascontiguousarray(np.broadcast_to(f(mlp_b_s)[None], (128, 2, 8, 128))),
        "ident": np.eye(128, dtype=np.float32),
        "w_mod": f(w_mod), "attn_w_in": f(attn_w_in), "attn_w_uq": f(attn_w_uq), "attn_w_ukv": f(attn_w_ukv),
        "attn_w_o": f(attn_w_o), "mlp_w_in": f(mlp_w_in), "mlp_w_s": f(mlp_w_s), "mlp_w_o": f(mlp_w_o),
    }
    in_maps = []
    for i in range(NCORES):
        b = i // 4
        r = i % 4
        cos, sin = _rope_tables(i)
        cT = np.stack([c_ctx, c[b]], axis=-1).reshape(8, 128, 2).transpose(1, 0, 2)
        m = dict(shared)
        m.update({
            "xp": np.ascontiguousarray(x_prompt[4 * i:4 * i + 4].reshape(TP, D)),
            "xs": np.ascontiguousarray(x_sample[b, r * 512:(r + 1) * 512]),
            "cckv": np.ascontiguousarray(cache_ckv[b]),
            "ckpe": np.ascontiguousarray(cache_kpe[b]),
            "cT": np.ascontiguousarray(cT),
            "cos": cos, "sin": sin,
        })
        in_maps.append(m)
    res = run_bass_kernel_spmd(nc, in_maps, core_ids=list(range(NCORES)))
    rs = res.results
    y_prompt = np.concatenate([np.asarray(rs[i]["yp"]).reshape(4, 256, D) for i in range(NCORES)], axis=0)
    y_sample = np.stack([np.concatenate([np.asarray(rs[b * 4 + r]["ys"]) for r in range(4)], axis=0) for b in range(2)], axis=0)
    new_ckv = np.concatenate([np.asarray(rs[i]["nckv"]) for i in range(NCORES)], axis=0)
    new_kpe = np.concatenate([np.asarray(rs[i]["nkpe"]) for i in range(NCORES)], axis=0)
    return (y_prompt.astype(np.float32), y_sample.astype(np.float32),
            new_ckv.astype(np.float32), new_kpe.astype(np.float32))
```

```python
import numpy as np
import concourse.bass as bass
import concourse.mybir as mybir
from concourse.bass_utils import run_bass_kernel_spmd

F32 = mybir.dt.float32
BF16 = mybir.dt.bfloat16
AF = mybir.ActivationFunctionType
ALU = mybir.AluOpType

NCORES = 8
D = 1024
T = 1536
TP = 1024
TS = 512
NBLK = 3
EPS = 1e-6
NKEY_S = 2560
SM_SCALE = 192.0 ** -0.5
NQ = 16


class Op:
    __slots__ = ("eng", "kind", "fn", "deps", "sig", "cnt", "semkey", "semval", "waits", "hoist", "idx")

    def __init__(self, eng, kind, fn):
        self.hoist = False
        self.idx = 0
        self.eng = eng
        self.kind = kind
        self.fn = fn
        self.deps = set()
        self.sig = False
        self.cnt = 0
        self.semkey = None
        self.semval = 0
        self.waits = []


class Prog:
    def __init__(self, nc):
        self.nc = nc
        self.ops = []
        self.tinfo = {}
        self.recs = {}

    def reg(self, name, kind, row):
        self.tinfo[name] = (kind, row)

    def box(self, ap):
        name = ap.tensor.name
        kind, row = self.tinfo[name]
        aps = ap.ap
        off = ap.offset
        if kind == "dram":
            ext = sum((c - 1) * abs(s) for s, c in aps) + 1
            return name, (0, 1, off, off + ext)
        p0 = off // row
        f0 = off % row
        pc = aps[0][1]
        ext = sum((c - 1) * abs(s) for s, c in aps[1:]) + 1
        return name, (p0, p0 + pc, f0, f0 + ext)

    @staticmethod
    def _ov(a, b):
        return a[0] < b[1] and b[0] < a[1] and a[2] < b[3] and b[2] < a[3]

    @staticmethod
    def _contains(a, b):
        return a[0] <= b[0] and b[1] <= a[1] and a[2] <= b[2] and b[3] <= a[3]

    def _dep(self, i, j):
        if i == j:
            return
        a, b = self.ops[i], self.ops[j]
        if a.eng == "pe" and b.eng == "pe" and a.kind == "c" and b.kind == "c":
            return
        a.deps.add(b)

    def add(self, eng, fn, reads, writes, kind="c"):
        i = len(self.ops)
        op = Op(eng, kind, fn)
        op.idx = i
        self.ops.append(op)
        rkey = eng if kind == "c" else ("x", i)
        for ap in reads:
            name, bx = self.box(ap)
            lst = self.recs.setdefault(name, [])
            found = None
            for r in lst:
                if self._ov(r[0], bx):
                    if r[1] is not None:
                        self._dep(i, r[1])
                    if r[0] == bx:
                        found = r
            if found is None:
                found = [bx, None, {}]
                lst.append(found)
            found[2][rkey] = i
        for ap in writes:
            name, bx = self.box(ap)
            lst = self.recs.setdefault(name, [])
            keep = []
            for r in lst:
                if self._ov(r[0], bx):
                    if r[1] is not None:
                        self._dep(i, r[1])
                    for j in r[2].values():
                        self._dep(i, j)
                    if self._contains(bx, r[0]):
                        continue
                keep.append(r)
            keep.append([bx, i, {}])
            self.recs[name] = keep
        return i

    def mm(self, out, lhsT, rhs, start=True, stop=True):
        rd = [lhsT, rhs] + ([] if start else [out])
        return self.add("pe", lambda e: e.matmul(out, lhsT, rhs, start=start, stop=stop), rd, [out])

    def tr(self, out, in_, ident):
        return self.add("pe", lambda e: e.transpose(out, in_, ident), [in_, ident], [out])

    def act(self, out, in_, func, bias=None, scale=None):
        rd = [in_]
        kw = {}
        if bias is not None:
            kw["bias"] = bias
            if not isinstance(bias, (int, float)):
                rd.append(bias)
        if scale is not None:
            kw["scale"] = scale
            if not isinstance(scale, (int, float)):
                rd.append(scale)
        return self.add("act", lambda e: e.activation(out, in_, func, **kw), rd, [out])

    def tt(self, eng, out, a, b, op):
        return self.add(eng, lambda e: e.tensor_tensor(out, a, b, op), [a, b], [out])

    def ts(self, eng, out, a, s1, s2, op0, op1=None):
        rd = [a] + [s for s in (s1, s2) if s is not None and not isinstance(s, (int, float))]
        if op1 is None:
            return self.add(eng, lambda e: e.tensor_scalar(out, a, s1, None, op0), rd, [out])
        return self.add(eng, lambda e: e.tensor_scalar(out, a, s1, s2, op0, op1), rd, [out])

    def stt(self, out, in0, scalar, in1, op0, op1):
        rd = [in0, in1] + ([] if isinstance(scalar, (int, float)) else [scalar])
        return self.add("dve", lambda e: e.scalar_tensor_tensor(out, in0, scalar, in1, op0, op1), rd, [out])

    def copy(self, eng, out, in_):
        if eng == "act":
            return self.add("act", lambda e: e.copy(out, in_), [in_], [out])
        return self.add(eng, lambda e: e.tensor_copy(out, in_), [in_], [out])

    def recip(self, out, in_):
        return self.add("dve", lambda e: e.reciprocal(out, in_), [in_], [out])

    def memset(self, eng, ap, val):
        return self.add(eng, lambda e: e.memset(ap, val), [], [ap])

    def dma(self, q, out, in_, hoist=False):
        i = self.add(q, lambda e: e.dma_start(out=out, in_=in_), [in_], [out], kind="d")
        self.ops[i].hoist = hoist
        return i

    def hoist_ops(self):
        after = {}
        for op in self.ops:
            if op.hoist:
                t = max((d.idx for d in op.deps), default=-1)
                after.setdefault(t, []).append(op)
        new = list(after.get(-1, []))
        for op in self.ops:
            if not op.hoist:
                new.append(op)
            new.extend(after.get(op.idx, []))
        assert len(new) == len(self.ops)
        self.ops = new

    def generic(self, eng, fn, reads, writes, kind="c"):
        return self.add(eng, fn, reads, writes, kind=kind)

    def finalize(self, sems, qsems, ccsems, block):
        self.hoist_ops()
        ops = self.ops
        qn = {}
        ncc = 0
        for i, op in enumerate(ops):
            if op.kind == "d":
                lst = qn.setdefault(op.eng, [])
                n = len(lst)
                op.semkey = ("q", op.eng, n % NQ)
                op.semval = 16 * (n // NQ + 1)
                if n >= NQ:
                    op.deps.add(lst[n - NQ])
                lst.append(op)
            elif op.kind == "cc":
                op.semkey = ("cc", ncc)
                op.semval = 1
                ncc += 1
        for op in ops:
            for d in op.deps:
                d.sig = True
        cnts = {}
        for op in ops:
            if op.kind == "c" and op.sig:
                cnts[op.eng] = cnts.get(op.eng, 0) + 1
                op.cnt = cnts[op.eng]
                op.semkey = op.eng
                op.semval = op.cnt
        know = {}
        clocks = {}
        for i, op in enumerate(ops):
            kn = know.setdefault(op.eng, {})
            for pj in sorted(op.deps, key=lambda d: d.idx):
                if kn.get(pj.semkey, 0) < pj.semval:
                    op.waits.append((pj.semkey, pj.semval))
                ck = clocks.get(id(pj))
                if ck is not None:
                    for k, v in ck.items():
                        if kn.get(k, 0) < v:
                            kn[k] = v
                if kn.get(pj.semkey, 0) < pj.semval:
                    kn[pj.semkey] = pj.semval
            if op.kind != "c" or op.sig:
                ck = dict(kn)
                ck[op.semkey] = max(ck.get(op.semkey, 0), op.semval)
                clocks[id(op)] = ck
        final = {}
        for op in ops:
            if op.kind in ("d", "cc"):
                final[op.semkey] = max(final.get(op.semkey, 0), op.semval)

        def semof(key):
            if isinstance(key, str):
                return sems[key]
            if key[0] == "q":
                return qsems[key[1]][key[2]]
            return ccsems[key[1]]

        def emit(engname):
            def body(e):
                for op in ops:
                    if op.eng != engname:
                        continue
                    w = {}
                    for k, v in op.waits:
                        w[k] = max(w.get(k, 0), v)
                    for k, v in w.items():
                        e.wait_ge(semof(k), v)
                    inst = op.fn(e)
                    if op.kind == "d":
                        inst.then_inc(semof(op.semkey), 16)
                    elif op.kind == "cc":
                        inst.then_inc(semof(op.semkey))
                    elif op.sig:
                        inst.then_inc(sems[op.eng], 1)
                if engname == "sp":
                    for k, v in final.items():
                        e.wait_ge(semof(k), v)
            return body

        block.tensor(emit("pe"))
        block.scalar(emit("act"))
        block.vector(emit("dve"))
        block.gpsimd(emit("pool"))
        block.sync(emit("sp"))


def build_program():
    nc = bass.Bass("TRN2", target_bir_lowering=False)
    P = Prog(nc)

    def dram(name, shape, dt, kind):
        t = nc.dram_tensor(name, shape, dt, kind=kind) if kind else nc.dram_tensor(name, shape, dt)
        P.reg(name, "dram", 0)
        return t.ap()

    xp = dram("xp", [TP, D], F32, "ExternalInput")
    xs = dram("xs", [TS, D], F32, "ExternalInput")
    cckv = dram("cckv", [2, 512, 256], F32, "ExternalInput")
    ckpe = dram("ckpe", [2, 512, 64], F32, "ExternalInput")
    cT_d = dram("cT", [128, 8, 2], F32, "ExternalInput")
    normg_d = dram("normg", [128, 4, 8], F32, "ExternalInput")
    bmod_d = dram("bmod", [128, 4, 24], F32, "ExternalInput")
    qg_d = dram("qg", [128, 2, 4], F32, "ExternalInput")
    kvg_d = dram("kvg", [128, 2, 2], F32, "ExternalInput")
    vg_d = dram("vg", [128, 2, 16], F32, "ExternalInput")
    fg_d = dram("fg", [128, 8], F32, "ExternalInput")
    vb_d = dram("vb_bc", [128, 2, 2048], F32, "ExternalInput")
    bs_d = dram("bs_bc", [128, 2, 8, 128], F32, "ExternalInput")
    cos_d = dram("cos", [64, 512], F32, "ExternalInput")
    sin_d = dram("sin", [64, 512], F32, "ExternalInput")
    ident_d = dram("ident", [128, 128], F32, "ExternalInput")
    w_mod = dram("w_mod", [4, 1024, 3072], F32, "ExternalInput")
    a_w_in = dram("attn_w_in", [2, 1024, 1856], F32, "ExternalInput")
    a_w_uq = dram("attn_w_uq", [2, 512, 1536], F32, "ExternalInput")
    a_w_ukv = dram("attn_w_ukv", [2, 256, 2048], F32, "ExternalInput")
    a_w_o = dram("attn_w_o", [2, 1024, 1024], F32, "ExternalInput")
    m_w_in = dram("mlp_w_in", [2, 1024, 6144], F32, "ExternalInput")
    m_w_s = dram("mlp_w_s", [2, 8, 128, 128], F32, "ExternalInput")
    m_w_o = dram("mlp_w_o", [2, 2048, 1024], F32, "ExternalInput")
    yp = dram("yp", [TP, D], F32, "ExternalOutput")
    ys = dram("ys", [TS, D], F32, "ExternalOutput")
    nckv = dram("nckv", [4, 2, 256, 256], F32, "ExternalOutput")
    nkpe = dram("nkpe", [4, 2, 256, 64], F32, "ExternalOutput")
    agin = [dram("agin%d" % a, [320, 512], BF16, None) for a in range(2)]
    agout = [dram("agout%d" % a, [4 * 320, 512], BF16, None) for a in range(2)]

    import contextlib
    es = contextlib.ExitStack()

    def sb(name, shape, dt):
        t = es.enter_context(nc.sbuf_tensor(name, shape, dt))
        row = 1
        for s in shape[1:]:
            row *= s
        P.reg(name, "sb", row)
        return t

    ARN = 45056
    X = sb("X", [128, 8, T], F32)
    WR = sb("WR", [128, 4, 4096], BF16)
    AR = sb("AR", [128, ARN], BF16)
    A32 = sb("A32", [128, 3072], F32)
    TMP = sb("TMP", [128, 3, 512], F32)
    RSTD = sb("RSTD", [128, 2, 512], F32)
    SQ = sb("SQ", [128, 2, 512], BF16)
    SQ3 = sb("SQ3", [128, 4, 512], BF16)
    COS = sb("COS", [64, 512], F32)
    SIN = sb("SIN", [64, 512], F32)
    ONES = sb("ONES", [128, 128], BF16)
    IDF = sb("IDF", [128, 128], F32)
    IDB = sb("IDB", [128, 128], BF16)
    CT32 = sb("CT32", [128, 8, 2], F32)
    SC = sb("SC", [128, 8, 2], BF16)
    NORMG = sb("NORMG", [128, 4, 8], F32)
    BMOD = sb("BMOD", [128, 4, 24], F32)
    QG = sb("QG", [128, 2, 4], F32)
    KVG = sb("KVG", [128, 2, 2], F32)
    VG = sb("VG", [128, 2, 16], F32)
    FG = sb("FG", [128, 8], F32)
    MODV = sb("MODV", [128, 4, 24, 2], F32)
    MA = sb("MA", [128, 4, 8, 2], F32)
    BNS = sb("BNS", [128, 12, 24], F32)
    BNA = sb("BNA", [128, 12, 2], F32)
    RS1 = sb("RS1", [128, 12, 1], F32)
    NMR = sb("NMR", [128, 2, 1], F32)
    WSRAW = sb("WSRAW", [128, 2, 128], F32)

    PS = es.enter_context(nc.psum_tensor("PS", [128, 8, 512], F32))
    P.reg("PS", "ps", 4096)

    sems = {k: es.enter_context(nc.semaphore("s_" + k)) for k in ("pe", "act", "dve", "pool")}
    qsems = {q: [es.enter_context(nc.semaphore("q_%s_%d" % (q, i))) for i in range(NQ)] for q in ("sp", "pool")}
    ccsems = [es.enter_context(nc.semaphore("cc%d" % i)) for i in range(2)]

    state = {"slab": 0, "bank": 0, "tmp": 0, "sq": 0, "rstd": 0, "ev": 0, "sq3": 0}

    sl_busy = [False] * 4
    sl_rel = [0, 1, 2, 3]
    sl_tags = {}

    def release_tag(tag):
        if tag in sl_tags:
            i = sl_tags.pop(tag)
            sl_busy[i] = False
            state["slab"] += 1
            sl_rel[i] = 4 + state["slab"]

    def slab(kc, ncols, tag="s"):
        release_tag(tag)
        free = [i for i in range(4) if not sl_busy[i]]
        assert free, "weight ring exhausted"
        i = min(free, key=lambda j: sl_rel[j])
        sl_busy[i] = True
        sl_tags[tag] = i
        return WR[:, i, 0:kc * ncols].rearrange("p (k n) -> p k n", k=kc)

    def wload(dst, src2d, kc):
        P.dma("pool", dst, src2d.rearrange("(k p) n -> p k n", p=128), hoist=True)

    def bank(pool=None):
        pool = pool or (0, 1, 2, 3, 4, 5, 6, 7)
        b = pool[state["bank"] % len(pool)]
        state["bank"] += 1
        return b

    def odpair():
        state["od"] = state.get("od", 0) + 1
        return ((4, 6), (5, 7))[state["od"] % 2]

    def tmp32():
        i = state["tmp"] % 3
        state["tmp"] += 1
        return TMP[:, i, :]

    def sqt():
        i = state["sq"] % 2
        state["sq"] += 1
        return SQ[:, i, :]

    def rstdt():
        i = state["rstd"] % 2
        state["rstd"] += 1
        return RSTD[:, i, :]

    def evac_eng():
        state["ev"] += 1
        return "act" if state["ev"] % 2 else "dve"

    def evac_copy(out, in_, eng=None):
        eng = eng or evac_eng()
        P.copy(eng, out, in_)

    def av(off, n):
        return AR[:, off:off + n]

    def rstd_from_ps(ps_ap, n_feat, npart=128):
        r = rstdt()
        P.act(r[0:npart], ps_ap, AF.Sqrt, bias=EPSB[0:npart, :], scale=1.0 / n_feat)
        P.recip(r[0:npart], r[0:npart])
        return r

    EPSB = sb("EPSB", [128, 1], F32)

    P.memset("dve", EPSB[:], EPS)
    P.memset("dve", ONES[:], 1.0)
    for dst, src in ((IDF, ident_d), (CT32, cT_d), (NORMG, normg_d), (BMOD, bmod_d), (QG, qg_d),
                     (KVG, kvg_d), (VG, vg_d), (FG, fg_d), (COS, cos_d), (SIN, sin_d)):
        P.dma("sp", dst[:], src)
    P.copy("dve", IDB[:], IDF[:])
    P.act(SC[:], CT32[:], AF.Silu)

    def emit_mod_slab(l, sj, pool=None):
        s = slab(8, 512, "mod")
        wload(s, w_mod[l, :, sj * 512:(sj + 1) * 512], 8)
        b = bank(pool)
        for jj in range(4):
            for k in range(8):
                P.mm(PS[:, b, 2 * jj:2 * jj + 2], s[:, k, jj * 128:(jj + 1) * 128], SC[:, k, :],
                     start=(k == 0), stop=(k == 7))
        if pool is not None:
            for jj in (3, 2, 1, 0):
                P.act(MODV[:, l, 4 * sj + jj, :], PS[:, b, 2 * jj:2 * jj + 2], AF.Identity,
                      bias=BMOD[:, l, 4 * sj + jj:4 * sj + jj + 1])
        else:
            P.tt("dve", MODV[:, l, 4 * sj:4 * sj + 4, :], PS[:, b, 0:8].rearrange("p (j v) -> p j v", v=2),
                 BMOD[:, l, 4 * sj:4 * sj + 4].unsqueeze(2).to_broadcast([128, 4, 2]), ALU.add)
        release_tag("mod")
        if sj == 3:
            P.stt(MA[:, l, :, :], MODV[:, l, 8:16, :], 1.0,
                  NORMG[:, l, :].unsqueeze(2).to_broadcast([128, 8, 2]), ALU.add, ALU.mult)

    def emit_mod(l):
        for sj in range(6):
            emit_mod_slab(l, sj)

    def vsel(blk):
        return 0 if blk < 2 else 1

    def ssq_ps(src_fn, nchunks, blk_cols, npart=128):
        b = bank()
        n = blk_cols
        for c in range(nchunks):
            s = sqt()
            src = src_fn(c)
            if c % 2 == 0:
                P.act(s[:, 0:n], src, AF.Square)
            else:
                P.tt("dve", s[:, 0:n], src, src, ALU.mult)
            P.mm(PS[:, b, 0:n], ONES[:], s[:, 0:n], start=(c == 0), stop=(c == nchunks - 1))
        return PS[:, b, 0:n]

    H_OFF = 0
    H = av(H_OFF, 8 * T).rearrange("p (c t) -> p c t", c=8)

    SSQB = {0: 5, 1: 6, 2: 7}

    def x_update(l, dc, blk, b):
        cs = slice(blk * 512, (blk + 1) * 512)
        v = vsel(blk)
        P.stt(X[:, dc, cs], PS[:, b, :], MODV[:, l, 16 + dc, v:v + 1], X[:, dc, cs], ALU.mult, ALU.add)
        s = SQ3[:, state["sq3"] % 4, :]
        state["sq3"] += 1
        P.tt("pool", s, X[:, dc, cs], X[:, dc, cs], ALU.mult)
        pend.append((blk, dc, s))
        while len(pend) > 2:
            x_flush()

    pend = []

    def x_flush_item(item):
        blk, dc, s = item
        P.mm(PS[:, SSQB[blk], :], ONES[:], s, start=(dc == 0), stop=(dc == 7))

    def x_flush():
        x_flush_item(pend.pop(0))

    def flush_blk(blk):
        keep = []
        while pend:
            item = pend.pop(0)
            if item[0] == blk:
                x_flush_item(item)
            else:
                keep.append(item)
        pend.extend(keep)

    def emit_h_blk(l, blk):
        cs = slice(blk * 512, (blk + 1) * 512)
        v = vsel(blk)
        if l == 0:
            ps = ssq_ps(lambda c: X[:, c, cs], 8, 512)
        else:
            flush_blk(blk)
            ps = PS[:, SSQB[blk], :]
        r = rstd_from_ps(ps, 1024.0)

        def one(c):
            t = tmp32()
            P.stt(t, X[:, c, cs], MA[:, l, c, v:v + 1], r, ALU.mult, ALU.mult)
            P.act(H[:, c, cs], t, AF.Identity, bias=MODV[:, l, c, v:v + 1])
        for c in range(8):
            dq.append(lambda c=c: one(c))

    dq = []

    def dq_step():
        if dq:
            dq.pop(0)()

    def dq_flush():
        while dq:
            dq.pop(0)()

    def emit_h(l):
        for blk in (2, 0, 1):
            emit_h_blk(l, blk)

    def after_update_blk(l, blk):
        if l + 1 < 4:
            emit_h_blk(l + 1, blk)
        else:
            emit_final_blk(blk)

    def emit_load_x():
        r0 = {}

        def load_tile(tt_):
            st = A32[:, (tt_ % 3) * 1024:(tt_ % 3 + 1) * 1024]
            if tt_ < 8:
                src = xp[tt_ * 128:(tt_ + 1) * 128, :]
            else:
                src = xs[(tt_ - 8) * 128:(tt_ - 7) * 128, :]
            P.dma("sp", st, src)
            for half in range(2):
                b = bank()
                for cc in range(4):
                    c = half * 4 + cc
                    P.tr(PS[:, b, cc * 128:(cc + 1) * 128], st[:, c * 128:(c + 1) * 128], IDF[:])
                evac_copy(X[:, half * 4:half * 4 + 4, tt_ * 128:(tt_ + 1) * 128],
                          PS[:, b, :].rearrange("p (c t) -> p c t", c=4))

        def stats(blk, rt):
            cs = slice(blk * 512, (blk + 1) * 512)
            ps = ssq_ps(lambda c: X[:, c, cs], 8, 512)
            P.act(rt, ps, AF.Sqrt, bias=EPSB[:], scale=1.0 / 1024.0)
            P.recip(rt, rt)
            r0[blk] = rt

        emit_mod_slab(0, 0)
        load_tile(8)
        load_tile(9)
        emit_mod_slab(0, 1)
        load_tile(10)
        load_tile(11)
        stats(2, rstdt())
        emit_mod_slab(0, 2)
        load_tile(0)
        load_tile(1)
        emit_mod_slab(0, 3)
        load_tile(2)
        load_tile(3)
        for tt_ in (4, 5, 6, 7):
            load_tile(tt_)
        stats(0, A32[:, 0:512])
        stats(1, A32[:, 512:1024])
        emit_cache_part(0)
        for blk in (2, 0, 1):
            cs = slice(blk * 512, (blk + 1) * 512)
            v = vsel(blk)
            for c in range(8):
                t = tmp32()
                P.stt(t, X[:, c, cs], MA[:, 0, c, v:v + 1], r0[blk], ALU.mult, ALU.mult)
                P.act(H[:, c, cs], t, AF.Identity, bias=MODV[:, 0, c, v:v + 1])

    def emit_final_blk(blk):
        flush_blk(blk)
        cs = slice(blk * 512, (blk + 1) * 512)
        r = rstd_from_ps(PS[:, SSQB[blk], :], 1024.0)
        for c in range(8):
            dq.append(lambda c=c: P.stt(X[:, c, cs], X[:, c, cs], FG[:, c:c + 1], r, ALU.mult, ALU.mult))

        def out_tile(tt_):
            st = A32[:, (tt_ % 3) * 1024:(tt_ % 3 + 1) * 1024]
            for half in range(2):
                b = bank((0, 1, 2, 3, 4))
                for cc in range(4):
                    c = half * 4 + cc
                    P.tr(PS[:, b, cc * 128:(cc + 1) * 128], X[:, c, tt_ * 128:(tt_ + 1) * 128], IDF[:])
                evac_copy(st[:, half * 512:(half + 1) * 512], PS[:, b, :])
            if tt_ < 8:
                dst = yp[tt_ * 128:(tt_ + 1) * 128, :]
            else:
                dst = ys[(tt_ - 8) * 128:(tt_ - 7) * 128, :]
            P.dma("sp", dst, st)
        for t4 in range(4):
            dq.append(lambda t4=t4: out_tile(blk * 4 + t4))

    GZ = av(12288, 8 * T).rearrange("p (c t) -> p c t", c=8)
    CQN = av(24576, 4 * T).rearrange("p (c t) -> p c t", c=4)
    CKVN = av(30720, 2 * T).rearrange("p (c t) -> p c t", c=2)
    KPEB = av(33792, T)
    KVALL = av(35328, 2 * NKEY_S).rearrange("p (c t) -> p c t", c=2)
    KPEALL = av(40448, NKEY_S)
    WUQSW = av(43008, 2048).rearrange("p (k n) -> p k n", k=4)
    QN = av(0, T)
    QR = av(1536, T)
    KTP = av(3072, 1024)
    KTS = av(4096, NKEY_S)
    VV = av(6656, 28 * 128).rearrange("p (t d) -> p t d", t=28)
    PT = av(10240, 2048).rearrange("p (i n) -> p i n", i=4)
    ACCD = A32[:, 0:512]
    ACCP = A32[:, 512:1024]
    OUTKV = A32[:, 0:1280].rearrange("p (t f) -> p t f", t=4)
    CST = A32[:, 1280:2560].rearrange("p (t f) -> p t f", t=4)

    def emit_cache_part(a):
        P.dma("sp", CST[:, :, 0:256], cckv[a].rearrange("(t p) f -> p t f", p=128), hoist=True)
        P.dma("sp", CST[:, :, 256:320], ckpe[a].rearrange("(t p) f -> p t f", p=128), hoist=True)
        for j in range(2):
            b = bank((0, 1, 2, 3))
            for t4 in range(4):
                P.tr(PS[:, b, t4 * 128:(t4 + 1) * 128], CST[:, t4, j * 128:(j + 1) * 128], IDF[:])
            evac_copy(KVALL[:, j, 0:512], PS[:, b, :])
        b = bank((0, 1, 2, 3))
        for t4 in range(4):
            P.tr(PS[0:64, b, t4 * 128:(t4 + 1) * 128], CST[:, t4, 256:320], IDF[:])
        evac_copy(KPEALL[0:64, 0:512], PS[0:64, b, :])

    def emit_attn(a, l):
        blks = (2, 0, 1)
        P.memset("pool", KPEB[64:128, :], 0.0)
        P.memset("pool", KPEALL[64:128, :], 0.0)
        if l != 0:
            emit_cache_part(a)
        s0 = slab(8, 512, "q")
        wload(s0, a_w_in[a, :, 0:512], 8)
        s1 = slab(8, 384, "kv")
        wload(s1[:, :, 0:320], a_w_in[a, :, 512:832], 8)
        zsl = []
        for zs in range(2):
            s2 = slab(8, 512, "z%d" % zs)
            wload(s2, a_w_in[a, :, 832 + zs * 512:832 + (zs + 1) * 512], 8)
            zsl.append(s2)
        zq = [(zs, j, blk) for zs in range(2) for j in range(4) for blk in blks]

        def emit_z(n):
            for _ in range(n):
                if not zq:
                    return
                zs, j, blk = zq.pop(0)
                if zs == 1 and "z0" in sl_tags:
                    release_tag("z0")
                cs_ = slice(blk * 512, (blk + 1) * 512)
                b_ = bank((4, 5, 6, 7))
                for k in range(8):
                    P.mm(PS[:, b_, :], zsl[zs][:, k, j * 128:(j + 1) * 128], H[:, k, cs_], start=(k == 0), stop=(k == 7))
                P.act(GZ[:, zs * 4 + j, cs_], PS[:, b_, :], AF.Silu)

        for blk in blks:
            cs = slice(blk * 512, (blk + 1) * 512)
            bq = [bank((0, 1, 2, 3)) for _ in range(4)]
            for j in range(4):
                for k in range(8):
                    P.mm(PS[:, bq[j], :], s0[:, k, j * 128:(j + 1) * 128], H[:, k, cs], start=(k == 0), stop=(k == 7))
            bs_ = bank((4, 5, 6, 7))
            for j in range(4):
                s = sqt()
                P.act(s, PS[:, bq[j], :], AF.Square)
                P.mm(PS[:, bs_, :], ONES[:], s, start=(j == 0), stop=(j == 3))
            r = rstd_from_ps(PS[:, bs_, :], 512.0)
            for j in range(4):
                P.stt(CQN[:, j, cs], PS[:, bq[j], :], QG[:, a, j:j + 1], r, ALU.mult, ALU.mult)
            emit_z(4)
        release_tag("q")
        kv4 = s1[:, :, 256:320].rearrange("p k (x h i) -> p k x h i", x=2, h=2)
        sw4 = s1[:, :, 320:384].rearrange("p k (x h i) -> p k x h i", x=2, h=2)
        for x_ in range(2):
            P.ts("dve", sw4[:, :, x_, 0, :], kv4[:, :, x_, 1, :], -1.0, None, ALU.mult)
            P.copy("dve", sw4[:, :, x_, 1, :], kv4[:, :, x_, 0, :])
        for blk in blks:
            cs = slice(blk * 512, (blk + 1) * 512)
            bk = [bank((0, 1, 2, 3)) for _ in range(2)]
            for j in range(2):
                for k in range(8):
                    P.mm(PS[:, bk[j], :], s1[:, k, j * 128:(j + 1) * 128], H[:, k, cs], start=(k == 0), stop=(k == 7))
            bs_ = bank((4, 5, 6, 7))
            for j in range(2):
                s = sqt()
                P.act(s, PS[:, bk[j], :], AF.Square)
                P.mm(PS[:, bs_, :], ONES[:], s, start=(j == 0), stop=(j == 1))
            r = rstd_from_ps(PS[:, bs_, :], 256.0)
            bp = bank((4, 5, 6, 7))
            for k in range(8):
                P.mm(PS[0:64, bp, :], s1[:, k, 256:320], H[:, k, cs], start=(k == 0), stop=(k == 7))
            if blk == 2:
                bp2 = bank((4, 5, 6, 7))
                for k in range(8):
                    P.mm(PS[0:64, bp2, :], s1[:, k, 320:384], H[:, k, cs], start=(k == 0), stop=(k == 7))
                t1 = tmp32()
                t2 = tmp32()
                P.tt("dve", t1[0:64], PS[0:64, bp, :], COS[:], ALU.mult)
                P.tt("dve", t2[0:64], PS[0:64, bp2, :], SIN[:], ALU.mult)
                P.tt("dve", KPEB[0:64, cs], t1[0:64], t2[0:64], ALU.add)
                for j in range(2):
                    P.stt(CKVN[:, j, cs], PS[:, bk[j], :], KVG[:, a, j:j + 1], r, ALU.mult, ALU.mult)
                for j in range(2):
                    P.dma("sp", agin[a][j * 128:(j + 1) * 128, :], CKVN[:, j, cs])
                P.dma("sp", agin[a][256:320, :], KPEB[0:64, cs])
                P.generic("pool", lambda e, a=a: e.collective_compute(
                    "AllGather", ALU.bypass, replica_groups=[[0, 1, 2, 3], [4, 5, 6, 7]],
                    ins=[agin[a].opt()], outs=[agout[a].opt()]), [agin[a]], [agout[a]], kind="cc")
            else:
                evac_copy(KPEB[0:64, cs], PS[0:64, bp, :], "act")
                tk = tmp32()
                evac_copy(tk[0:64], PS[0:64, bp, :], "dve")
                tcs = []
                for j in range(2):
                    tc = tmp32()
                    P.stt(tc, PS[:, bk[j], :], KVG[:, a, j:j + 1], r, ALU.mult, ALU.mult)
                    P.copy("act", CKVN[:, j, cs], tc)
                    tcs.append(tc)
                emit_z(3)
                for j in range(2):
                    tc = tcs[j]
                    b = bank((0, 1, 2, 3))
                    for t4 in range(4):
                        P.tr(PS[:, b, t4 * 128:(t4 + 1) * 128], tc[:, t4 * 128:(t4 + 1) * 128], IDF[:])
                    evac_copy(OUTKV[:, :, j * 128:(j + 1) * 128], PS[:, b, :].rearrange("p (t f) -> p t f", t=4))
                b = bank((0, 1, 2, 3))
                for t4 in range(4):
                    P.tr(PS[:, b, t4 * 64:(t4 + 1) * 64], tk[0:64, t4 * 128:(t4 + 1) * 128], IDF[0:64, 0:64])
                evac_copy(OUTKV[:, :, 256:320], PS[:, b, 0:256].rearrange("p (t f) -> p t f", t=4))
                for sl in range(2):
                    seq = blk * 2 + sl
                    P.dma("sp", nckv[seq, a].rearrange("(t p) f -> p t f", p=128), OUTKV[:, 2 * sl:2 * sl + 2, 0:256])
                    P.dma("sp", nkpe[seq, a].rearrange("(t p) f -> p t f", p=128), OUTKV[:, 2 * sl:2 * sl + 2, 256:320])
            if blk == 2:
                emit_z(3)
        release_tag("kv")
        emit_z(len(zq))
        release_tag("z0")
        release_tag("z1")
        agv = agout[a].rearrange("(r f) t -> f r t", f=320)
        for j in range(2):
            P.dma("sp", KVALL[:, j, 512:NKEY_S].rearrange("p (r t) -> p r t", r=4),
                  agv[j * 128:(j + 1) * 128, :, :])
        P.dma("sp", KPEALL[0:64, 512:NKEY_S].rearrange("p (r t) -> p r t", r=4), agv[256:320, :, :])
        P.memset("pool", QR[64:128, :], 0.0)
        wkv = slab(2, 2048, "wkv")
        wload(wkv, a_w_ukv[a], 2)
        wq = None
        for h in range(8):
            if l + 1 < 4 and h < 6:
                emit_mod_slab(l + 1, h, (0, 1, 2, 3))
            if l == 0 and h >= 6:
                emit_mod_slab(0, h - 2, (0, 1, 2, 3))
            if h == 0 or h == 3:
                g4 = 0 if h == 0 else 1
                wq_new = slab(4, 768, "wq%d" % g4)
                wload(wq_new, a_w_uq[a, :, g4 * 768:(g4 + 1) * 768], 4)

                def build_sw(wq_new=wq_new, g4=g4):
                    for hh in range(4):
                        src = wq_new[:, :, hh * 192 + 128:hh * 192 + 192].rearrange("p k (x h i) -> p k x h i", x=2, h=2)
                        dst = WUQSW[:, :, (g4 * 4 + hh) * 64:(g4 * 4 + hh + 1) * 64].rearrange(
                            "p k (x h i) -> p k x h i", x=2, h=2)
                        for x_ in range(2):
                            P.ts("dve", dst[:, :, x_, 0, :], src[:, :, x_, 1, :], -1.0, None, ALU.mult)
                            P.copy("dve", dst[:, :, x_, 1, :], src[:, :, x_, 0, :])
                if h == 0:
                    build_sw()
                    wq = wq_new
            if h == 4:
                release_tag("wq0")
                wq = wq_new
            hq = h % 4
            for blk in (0, 1, 2):
                cs = slice(blk * 512, (blk + 1) * 512)
                b = bank((0, 1, 2, 3))
                for k in range(4):
                    P.mm(PS[:, b, :], wq[:, k, hq * 192:hq * 192 + 128], CQN[:, k, cs], start=(k == 0), stop=(k == 3))
                evac_copy(QN[:, cs], PS[:, b, :], "act")
                b = bank((0, 1, 2, 3))
                for k in range(4):
                    P.mm(PS[0:64, b, :], wq[:, k, hq * 192 + 128:hq * 192 + 192], CQN[:, k, cs],
                         start=(k == 0), stop=(k == 3))
                if blk == 2:
                    b2 = bank((0, 1, 2, 3))
                    for k in range(4):
                        P.mm(PS[0:64, b2, :], WUQSW[:, k, h * 64:(h + 1) * 64], CQN[:, k, cs],
                             start=(k == 0), stop=(k == 3))
                    t1 = tmp32()
                    t2 = tmp32()
                    P.tt("dve", t1[0:64], PS[0:64, b, :], COS[:], ALU.mult)
                    P.tt("dve", t2[0:64], PS[0:64, b2, :], SIN[:], ALU.mult)
                    P.tt("pool", QR[0:64, cs], t1[0:64], t2[0:64], ALU.add)
                else:
                    evac_copy(QR[0:64, cs], PS[0:64, b, :], "act")
            for cb in range(2):
                b = bank((0, 1, 2, 3))
                for k in range(2):
                    P.mm(PS[:, b, :], wkv[:, k, h * 256:h * 256 + 128], CKVN[:, k, cb * 512:(cb + 1) * 512],
                         start=(k == 0), stop=(k == 1))
                evac_copy(KTP[:, cb * 512:(cb + 1) * 512], PS[:, b, :], "act")
            for cb in range(5):
                b = bank((0, 1, 2, 3))
                for k in range(2):
                    P.mm(PS[:, b, :], wkv[:, k, h * 256:h * 256 + 128], KVALL[:, k, cb * 512:(cb + 1) * 512],
                         start=(k == 0), stop=(k == 1))
                evac_copy(KTS[:, cb * 512:(cb + 1) * 512], PS[:, b, :], "act")
            for g in range(7):
                b = bank((0, 1, 2, 3))
                for t4 in range(4):
                    ti = g * 4 + t4
                    for k in range(2):
                        if ti < 8:
                            lt = CKVN[:, k, ti * 128:(ti + 1) * 128]
                        else:
                            lt = KVALL[:, k, (ti - 8) * 128:(ti - 7) * 128]
                        P.mm(PS[:, b, t4 * 128:(t4 + 1) * 128], lt, wkv[:, k, h * 256 + 128:h * 256 + 256],
                             start=(k == 0), stop=(k == 1))
                evac_copy(VV[:, g * 4:(g + 1) * 4, :], PS[:, b, :].rearrange("p (t d) -> p t d", t=4), "dve")
            cs2 = slice(1024, 1536)
            units = []
            bo_p, bd_p = odpair()
            for sl in range(2):
                units.append(("p", sl, bo_p, bd_p))
            bo_s, bd_s = odpair()
            for kt in range(20):
                units.append(("s", kt, bo_s, bd_s))
            bo_p, bd_p = odpair()
            for sl in range(2):
                units.append(("p", 2 + sl, bo_p, bd_p))
            LA = 3
            nun = len(units)
            pts = {}
            for i in range(nun + LA):
                if i < nun:
                    kind, idx, bo, bd = units[i]
                    b = bank((0, 1, 2, 3))
                    if kind == "s":
                        kt = idx
                        P.mm(PS[:, b, :], KTS[:, kt * 128:(kt + 1) * 128], QN[:, cs2], start=True, stop=False)
                        P.mm(PS[:, b, :], KPEALL[:, kt * 128:(kt + 1) * 128], QR[:, cs2], start=False, stop=True)
                    else:
                        seq = idx
                        qs = slice(seq * 256, (seq + 1) * 256)
                        for kt in range(2):
                            ks = slice(seq * 256 + kt * 128, seq * 256 + (kt + 1) * 128)
                            P.mm(PS[:, b, kt * 256:(kt + 1) * 256], KTP[:, ks], QN[:, qs], start=True, stop=False)
                            P.mm(PS[:, b, kt * 256:(kt + 1) * 256], KPEB[:, ks], QR[:, qs], start=False, stop=True)
                    pt = PT[:, i % 4, :]
                    P.act(pt, PS[:, b, :], AF.Exp, scale=SM_SCALE)
                    pts[i] = pt
                j = i - LA
                if j >= 0:
                    kind, idx, bo, bd = units[j]
                    pt = pts.pop(j)
                    if kind == "s":
                        kt = idx
                        P.mm(PS[:, bo, :], VV[:, 8 + kt, :], pt, start=(kt == 0), stop=(kt == 19))
                        P.mm(PS[:, bd, :], ONES[:], pt, start=(kt == 0), stop=(kt == 19))
                        if kt == 19:
                            emit_og(h, 2, bo, bd)
                    else:
                        seq = idx
                        sl = seq % 2
                        for kt in range(2):
                            P.mm(PS[:, bo, sl * 256:(sl + 1) * 256], VV[:, seq * 2 + kt, :], pt[:, kt * 256:(kt + 1) * 256],
                                 start=(kt == 0), stop=(kt == 1))
                        for kt in range(2):
                            P.mm(PS[:, bd, sl * 256:(sl + 1) * 256], ONES[:], pt[:, kt * 256:(kt + 1) * 256],
                                 start=(kt == 0), stop=(kt == 1))
                        if sl == 1:
                            emit_og(h, seq // 2, bo, bd)
            if h == 3:
                build_sw()
        release_tag("wkv")
        release_tag("wq1")
        sos = []
        for os_ in range(2):
            so = slab(8, 512, "wo%d" % os_)
            wload(so, a_w_o[a, :, os_ * 512:(os_ + 1) * 512], 8)
            sos.append(so)
        prev = None
        for blk in blks:
            cs = slice(blk * 512, (blk + 1) * 512)
            for dc in range(8):
                so = sos[dc // 4]
                j = dc % 4
                b = bank((0, 1, 2, 3, 4))
                for k in range(8):
                    P.mm(PS[:, b, :], so[:, k, j * 128:(j + 1) * 128], GZ[:, k, cs], start=(k == 0), stop=(k == 7))
                x_update(l, dc, blk, b)
                if dc == 1 and prev is not None:
                    after_update_blk(l, prev)
                dq_step()
            prev = blk
        emit_mlp_setup(l // 2)
        after_update_blk(l, prev)
        dq_flush()
        release_tag("wo0")
        release_tag("wo1")

    def emit_og(h, blk, bo, bd):
        cs = slice(blk * 512, (blk + 1) * 512)
        rd = tmp32()
        P.act(rd, PS[:, bd, :], AF.Ln)
        P.act(rd, rd, AF.Exp, scale=-1.0)
        t = tmp32()
        P.tt("dve", t, PS[:, bo, :], rd, ALU.mult)
        P.tt("pool", GZ[:, h, cs], t, GZ[:, h, cs], ALU.mult)

    VH = av(12288, 12 * 2048).rearrange("p (t f) -> p t f", t=12)
    WST = av(36864, 1024).rearrange("p (g q) -> p g q", g=8)
    VBB = av(37888, 2048)
    UT = av(39936, 1536).rearrange("p (i n) -> p i n", i=3)
    GT = av(41472, 1536).rearrange("p (i n) -> p i n", i=3)
    CC = A32[:, 0:2048].rearrange("p (f q) -> p f q", f=16)
    BSB = A32[:, 2048:3072].rearrange("p (g q) -> p g q", g=8)

    def ln_group(t0):
        for ti in range(t0, t0 + 4):
            P.generic("dve", lambda e, ti=ti: e.bn_aggr(BNA[:, ti, :], BNS[:, ti, :]), [BNS[:, ti, :]], [BNA[:, ti, :]])
        P.act(RS1[:, t0:t0 + 4, :], BNA[:, t0:t0 + 4, 1:2], AF.Sqrt, bias=EPSB[:], scale=1.0)
        P.recip(RS1[:, t0:t0 + 4, :], RS1[:, t0:t0 + 4, :])
        for ti in range(t0, t0 + 4):
            P.ts("dve", VH[:, ti, :], VH[:, ti, :], BNA[:, ti, 0:1], RS1[:, ti, :], ALU.subtract, ALU.mult)

    def emit_mlp_setup(m):
        bp5 = (0, 1, 2, 3, 4)
        P.dma("pool", VBB, vb_d[:, m, :], hoist=True)
        P.dma("sp", BSB, bs_d[:, m, :, :], hoist=True)
        WSRAW8 = A32[:, 0:1024].rearrange("p (g q) -> p g q", g=8)
        P.dma("sp", WSRAW8, m_w_s[m].rearrange("g p q -> p g q"), hoist=True)
        for g2 in range(2):
            b = bank(bp5)
            for gg in range(4):
                P.tr(PS[:, b, gg * 128:(gg + 1) * 128], WSRAW8[:, g2 * 4 + gg, :], IDF[:])
            evac_copy(WST[:, g2 * 4:(g2 + 1) * 4, :], PS[:, b, :].rearrange("p (g q) -> p g q", g=4))
        for f4 in range(4):
            b = bank(bp5)
            for ff in range(4):
                fc = f4 * 4 + ff
                P.mm(PS[:, b, ff * 128:(ff + 1) * 128], VBB[:, fc * 128:(fc + 1) * 128], WST[:, fc // 2, :],
                     start=True, stop=True)
            for g1 in (1, 0):
                g = f4 * 2 + g1
                P.tt("dve", CC[:, 2 * g:2 * g + 2, :],
                     PS[:, b, g1 * 256:(g1 + 1) * 256].rearrange("p (f q) -> p f q", f=2),
                     BSB[:, g, :].unsqueeze(1).to_broadcast([128, 2, 128]), ALU.add)

    def emit_mlp(m, l):
        blks = (2, 0, 1)
        for vs in range(4):
            s = slab(8, 512)
            wload(s, m_w_in[m, :, 2048 + vs * 512:2048 + (vs + 1) * 512], 8)
            for ti in range(12):
                b = bank()
                for k in range(8):
                    P.mm(PS[:, b, :], H[:, k, ti * 128:(ti + 1) * 128], s[:, k, :], start=(k == 0), stop=(k == 7))
                P.act(VH[:, ti, vs * 512:(vs + 1) * 512], PS[:, b, :], AF.Gelu_apprx_tanh)
                P.generic("dve", lambda e, ti=ti, q4=vs: e.bn_stats(
                    BNS[:, ti, q4 * 6:(q4 + 1) * 6], VH[:, ti, q4 * 512:(q4 + 1) * 512]),
                    [VH[:, ti, vs * 512:(vs + 1) * 512]], [BNS[:, ti, vs * 6:(vs + 1) * 6]])
                if vs == 3 and ti % 4 == 3:
                    ln_group(ti - 3)
        release_tag("s")
        su = sz = None
        for fc in range(16):
            if l + 1 < 4 and fc % 2 == 1 and fc < 12:
                emit_mod_slab(l + 1, fc // 2)
            if fc % 4 == 0:
                su = slab(8, 512, "su")
                wload(su, m_w_in[m, :, fc * 128:fc * 128 + 512], 8)
                sz = slab(8, 512, "sz")
                wload(sz, m_w_in[m, :, 4096 + fc * 128:4096 + fc * 128 + 512], 8)
            j = fc % 4
            g = fc // 2
            for blk in blks:
                cs = slice(blk * 512, (blk + 1) * 512)
                b = bank()
                for k in range(8):
                    P.mm(PS[:, b, :], su[:, k, j * 128:(j + 1) * 128], H[:, k, cs], start=(k == 0), stop=(k == 7))
                P.act(UT[:, blk, :], PS[:, b, :], AF.Gelu_apprx_tanh)
            for blk in blks:
                cs = slice(blk * 512, (blk + 1) * 512)
                b = bank()
                for k in range(8):
                    P.mm(PS[:, b, :], sz[:, k, j * 128:(j + 1) * 128], H[:, k, cs], start=(k == 0), stop=(k == 7))
                P.act(GT[:, blk, :], PS[:, b, :], AF.Silu)
                P.tt("pool", UT[:, blk, :], UT[:, blk, :], GT[:, blk, :], ALU.mult)
            for blk in blks:
                b = bank()
                for t4 in range(4):
                    ti = blk * 4 + t4
                    P.mm(PS[:, b, t4 * 128:(t4 + 1) * 128], VH[:, ti, fc * 128:(fc + 1) * 128], WST[:, g, :],
                         start=True, stop=True)
                t = tmp32()
                P.stt(t.rearrange("p (t q) -> p t q", t=4), PS[:, b, :].rearrange("p (t q) -> p t q", t=4),
                      VG[:, m, fc:fc + 1], CC[:, fc, :].unsqueeze(1).to_broadcast([128, 4, 128]), ALU.mult, ALU.add)
                P.tt("dve", VH[:, blk * 4:(blk + 1) * 4, fc * 128:(fc + 1) * 128],
                     t.rearrange("p (t q) -> p t q", t=4), UT[:, blk, :].rearrange("p (t q) -> p t q", t=4), ALU.mult)
        release_tag("su")
        release_tag("sz")
        sos = []
        for os_ in range(4):
            so = slab(16, 256, "wo%d" % os_)
            wload(so, m_w_o[m, :, os_ * 256:(os_ + 1) * 256], 16)
            sos.append(so)
        prev = None
        for blk in blks:
            cs = slice(blk * 512, (blk + 1) * 512)
            for dc in range(8):
                so = sos[dc // 2]
                j = dc % 2
                b = bank((0, 1, 2, 3, 4))
                for k in range(16):
                    P.mm(PS[:, b, :].rearrange("p (t q) -> p t q", t=4), so[:, k, j * 128:(j + 1) * 128],
                         VH[:, blk * 4:(blk + 1) * 4, k * 128:(k + 1) * 128], start=(k == 0), stop=(k == 15))
                if blk == blks[-1] and dc % 2 == 1:
                    release_tag("wo%d" % (dc // 2))
                x_update(l, dc, blk, b)
                if dc == 1 and prev is not None:
                    after_update_blk(l, prev)
                dq_step()
            prev = blk
        after_update_blk(l, prev)
        dq_flush()
        for os_ in range(4):
            release_tag("wo%d" % os_)

    emit_load_x()
    for l in range(4):
        if l % 2 == 0:
            emit_attn(l // 2, l)
        else:
            emit_mlp(l // 2, l)

    with nc.Block() as block:
        P.finalize(sems, qsems, ccsems, block)
    es.close()
    return nc


def _rope_tables(core):
    r = core % 4
    t = np.arange(r * 512, (r + 1) * 512)
    row = (t // 64).astype(np.float32)
    col = (t % 64).astype(np.float32)
    inv = (1.0 / (np.float32(10000.0) ** (np.arange(0, 32, 2, dtype=np.float32) / np.float32(32)))).astype(np.float32)
    ang = np.concatenate([row[:, None] * inv, col[:, None] * inv], axis=-1).astype(np.float32)
    cos = np.cos(ang).astype(np.float32)
    sin = np.sin(ang).astype(np.float32)
    idx = np.array([(d // 32) * 16 + (d % 16) for d in range(64)])
    return np.ascontiguousarray(cos[:, idx].T), np.ascontiguousarray(sin[:, idx].T)


def _fm(v, nchunk):
    v = np.asarray(v, np.float32)
    lead = v.shape[:-1]
    v = v.reshape(lead + (nchunk, 128))
    v = np.moveaxis(v, -1, 0)
    return np.ascontiguousarray(v)


_NC_CACHE = {}


def kernel(x_prompt, x_sample, cache_ckv, cache_kpe, c, c_ctx, norm_g, w_mod, b_mod,
           attn_w_in, attn_q_norm_g, attn_kv_norm_g, attn_w_uq, attn_w_ukv, attn_w_o,
           mlp_w_in, mlp_v_norm_g, mlp_v_norm_b, mlp_w_s, mlp_b_s, mlp_w_o, final_norm_g):
    f = lambda a: np.ascontiguousarray(np.asarray(a, np.float32))
    x_prompt, x_sample, cache_ckv, cache_kpe = f(x_prompt), f(x_sample), f(cache_ckv), f(cache_kpe)
    c, c_ctx = f(c), f(c_ctx)
    if "nc" not in _NC_CACHE:
        _NC_CACHE["nc"] = build_program()
    nc = _NC_CACHE["nc"]
    shared = {
        "normg": _fm(norm_g, 8), "bmod": _fm(b_mod, 24), "qg": _fm(attn_q_norm_g, 4),
        "kvg": _fm(attn_kv_norm_g, 2), "vg": _fm(mlp_v_norm_g, 16), "fg": _fm(final_norm_g, 8),
        "vb_bc": np.ascontiguousarray(np.broadcast_to(f(mlp_v_norm_b)[None], (128, 2, 2048))),
        "bs_bc": np.ascontiguousarray(np.broadcast_to(f(mlp_b_s)[None], (128, 2, 8, 128))),
        "ident": np.eye(128, dtype=np.float32),
        "w_mod": f(w_mod), "attn_w_in": f(attn_w_in), "attn_w_uq": f(attn_w_uq), "attn_w_ukv": f(attn_w_ukv),
        "attn_w_o": f(attn_w_o), "mlp_w_in": f(mlp_w_in), "mlp_w_s": f(mlp_w_s), "mlp_w_o": f(mlp_w_o),
    }
    in_maps = []
    for i in range(NCORES):
        b = i // 4
        r = i % 4
        cos, sin = _rope_tables(i)
        cT = np.stack([c_ctx, c[b]], axis=-1).reshape(8, 128, 2).transpose(1, 0, 2)
        m = dict(shared)
        m.update({
            "xp": np.ascontiguousarray(x_prompt[4 * i:4 * i + 4].reshape(TP, D)),
            "xs": np.ascontiguousarray(x_sample[b, r * 512:(r + 1) * 512]),
            "cckv": np.ascontiguousarray(cache_ckv[b]),
            "ckpe": np.ascontiguousarray(cache_kpe[b]),
            "cT": np.ascontiguousarray(cT),
            "cos": cos, "sin": sin,
        })
        in_maps.append(m)
    res = run_bass_kernel_spmd(nc, in_maps, core_ids=list(range(NCORES)))
    rs = res.results
    y_prompt = np.concatenate([np.asarray(rs[i]["yp"]).reshape(4, 256, D) for i in range(NCORES)], axis=0)
    y_sample = np.stack([np.concatenate([np.asarray(rs[b * 4 + r]["ys"]) for r in range(4)], axis=0) for b in range(2)], axis=0)
    new_ckv = np.concatenate([np.asarray(rs[i]["nckv"]) for i in range(NCORES)], axis=0)
    new_kpe = np.concatenate([np.asarray(rs[i]["nkpe"]) for i in range(NCORES)], axis=0)
    return (y_prompt.astype(np.float32), y_sample.astype(np.float32),
            new_ckv.astype(np.float32), new_kpe.astype(np.float32))
```
